# Optimizing a Trainium2 kernel written in Bass

```python
import jax
import jax.numpy as jnp
from jax import lax
import numpy as np

D_MODEL = 1024
BATCH = 8
SEQ = 4096
DEPTH = 2

N_META = 16
EPS = 1e-6
GATE_CLAMP = 1.0 - 1e-6
CONV_DIM = D_MODEL // 2
CONV_K = 31
MLA_HEADS = D_MODEL // 128
Q_RANK = D_MODEL // 4
KV_RANK = D_MODEL // 8
NOPE_DIM = 64
ROPE_DIM = 32
V_DIM = 64
QK_DIM = NOPE_DIM + ROPE_DIM
ROPE_BASE = 10000.0
Q_BLOCK = 128
HGRN_HEADS = D_MODEL // 256
HGRN_DK = 128
HGRN_DV = (D_MODEL // 2) // HGRN_HEADS
HGRN_CHUNK = 64
D_FF = 4 * D_MODEL
N_BRANCH = 3

SPLIT_SIZES = (
    2 * CONV_DIM,
    Q_RANK,
    KV_RANK,
    ROPE_DIM,
    HGRN_HEADS * HGRN_DK,
    HGRN_HEADS * HGRN_DK,
    HGRN_HEADS * HGRN_DV,
    HGRN_HEADS * HGRN_DV,
    N_BRANCH * D_MODEL,
)
SPLIT_POINTS = tuple(int(s) for s in np.cumsum(SPLIT_SIZES)[:-1])
N_IN = int(sum(SPLIT_SIZES))

kernel_name = 'hybrid_conv_mla_hgrn2_block'


def rms_norm(x, g):
    xf = x.astype(jnp.float32)
    y = xf * lax.rsqrt(jnp.mean(xf * xf, axis=-1, keepdims=True) + EPS)
    return (y * g.astype(jnp.float32)).astype(x.dtype)


def layer_norm(x, g, b):
    xf = x.astype(jnp.float32)
    mu = jnp.mean(xf, axis=-1, keepdims=True)
    xc = xf - mu
    y = xc * lax.rsqrt(jnp.mean(xc * xc, axis=-1, keepdims=True) + EPS)
    return (y * g.astype(jnp.float32) + b.astype(jnp.float32)).astype(x.dtype)


def apply_rope(x, cos, sin):
    half = ROPE_DIM // 2
    xf = x.astype(jnp.float32)
    x1, x2 = xf[..., :half], xf[..., half:]
    out = jnp.concatenate([x1 * cos - x2 * sin, x1 * sin + x2 * cos], axis=-1)
    return out.astype(x.dtype)


def conv_module(u, conv_w, conv_b, ln_g, ln_b, w_proj):
    a, gt = jnp.split(u, 2, axis=-1)
    h = a * jax.nn.sigmoid(gt)
    h = lax.conv_general_dilated(
        h, conv_w[:, None, :].astype(h.dtype), window_strides=(1,), padding=[(CONV_K - 1, 0)],
        dimension_numbers=('NWC', 'WIO', 'NWC'), feature_group_count=CONV_DIM) + conv_b
    h = jax.nn.silu(layer_norm(h, ln_g, ln_b))
    return h @ w_proj


def causal_block_attention(q, k, v):
    B, L, H, Dq = q.shape
    n_blk = -(-L // Q_BLOCK)
    Lp = n_blk * Q_BLOCK
    qp = jnp.pad(q, ((0, 0), (0, Lp - L), (0, 0), (0, 0)))
    qb = qp.reshape(B, n_blk, Q_BLOCK, H, Dq).transpose(1, 0, 2, 3, 4)
    starts = jnp.arange(n_blk, dtype=jnp.int32) * Q_BLOCK
    k_pos = jnp.arange(L, dtype=jnp.int32)
    scale = Dq ** -0.5

    def one_block(args):
        q_blk, start = args
        s = jnp.einsum('bqhd,bkhd->bhqk', q_blk, k).astype(jnp.float32) * scale
        q_pos = start + jnp.arange(Q_BLOCK, dtype=jnp.int32)
        mask = k_pos[None, :] <= q_pos[:, None]
        s = jnp.where(mask, s, -1e30)
        p = jax.nn.softmax(s, axis=-1).astype(v.dtype)
        return jnp.einsum('bhqk,bkhd->bqhd', p, v)

    ob = lax.map(one_block, (qb, starts))
    return ob.transpose(1, 0, 2, 3, 4).reshape(B, Lp, H, V_DIM)[:, :L]


def mla(c_q, c_kv, k_rope, cos, sin, q_a_g, w_uq, kv_a_g, w_ukv, q_norm_g, k_norm_g, w_proj):
    B, L = c_q.shape[:2]
    q = (rms_norm(c_q, q_a_g) @ w_uq).reshape(B, L, MLA_HEADS, QK_DIM)
    kv = (rms_norm(c_kv, kv_a_g) @ w_ukv).reshape(B, L, MLA_HEADS, NOPE_DIM + V_DIM)
    k_nope, v = kv[..., :NOPE_DIM], kv[..., NOPE_DIM:]
    k_r = jnp.broadcast_to(k_rope[:, :, None, :], (B, L, MLA_HEADS, ROPE_DIM))
    k = jnp.concatenate([k_nope, k_r], axis=-1)
    q = rms_norm(q, q_norm_g)
    k = rms_norm(k, k_norm_g)
    q = jnp.concatenate([q[..., :NOPE_DIM], apply_rope(q[..., NOPE_DIM:], cos, sin)], axis=-1)
    k = jnp.concatenate([k[..., :NOPE_DIM], apply_rope(k[..., NOPE_DIM:], cos, sin)], axis=-1)
    o = causal_block_attention(q, k, v)
    return o.reshape(B, L, MLA_HEADS * V_DIM) @ w_proj


def hgrn2(q, f_raw, i, g, lb, norm_g, w_proj):
    B, L = q.shape[:2]
    f32 = jnp.float32
    lbf = lb.astype(f32)
    fr = f_raw.astype(f32)
    k = (1.0 - lbf) * jax.nn.sigmoid(-fr)
    log_f = jnp.log1p(-jnp.minimum(k, GATE_CLAMP))
    v = jax.nn.silu(i.astype(f32))
    qf = q.astype(f32)
    pad_front = (-N_META) % HGRN_CHUNK
    pad_back = (-(pad_front + L)) % HGRN_CHUNK
    pads = ((0, 0), (pad_front, pad_back), (0, 0))
    qf, k, v, log_f = [jnp.pad(a, pads) for a in (qf, k, v, log_f)]
    Lp = L + pad_front + pad_back
    nc = Lp // HGRN_CHUNK

    def to_chunks(a, d):
        return a.reshape(B, nc, HGRN_CHUNK, HGRN_HEADS, d).transpose(1, 0, 3, 2, 4)

    qc, kc, lfc = to_chunks(qf, HGRN_DK), to_chunks(k, HGRN_DK), to_chunks(log_f, HGRN_DK)
    vc = to_chunks(v, HGRN_DV)
    causal = jnp.tril(jnp.ones((HGRN_CHUNK, HGRN_CHUNK), dtype=bool))[:, :, None]

    def chunk_step(S, inp):
        q_c, k_c, v_c, lf_c = inp
        b = jnp.cumsum(lf_c, axis=2)
        o_inter = jnp.einsum('bhtk,bhkv->bhtv', q_c * jnp.exp(b), S)
        diff = b[:, :, :, None, :] - b[:, :, None, :, :]
        decay = jnp.where(causal, jnp.exp(jnp.where(causal, diff, 0.0)), 0.0)
        A = jnp.einsum('bhtk,bhtsk,bhsk->bhts', q_c, decay, k_c)
        o_intra = jnp.einsum('bhts,bhsv->bhtv', A, v_c)
        b_last = b[:, :, -1:, :]
        S_new = jnp.exp(b_last[:, :, 0, :])[..., None] * S + jnp.einsum(
            'bhsk,bhsv->bhkv', k_c * jnp.exp(b_last - b), v_c)
        return S_new, o_inter + o_intra

    S0 = jnp.zeros((B, HGRN_HEADS, HGRN_DK, HGRN_DV), f32)
    _, oc = lax.scan(chunk_step, S0, (qc, kc, vc, lfc))
    o = oc.transpose(1, 0, 3, 2, 4).reshape(B, Lp, HGRN_HEADS, HGRN_DV)[:, pad_front:pad_front + L]
    o = o * lax.rsqrt(jnp.mean(o * o, axis=-1, keepdims=True) + EPS)
    o = o.reshape(B, L, HGRN_HEADS * HGRN_DV) * norm_g.astype(f32)
    o = (o * jax.nn.silu(g.astype(f32))).astype(q.dtype)
    return o @ w_proj


def setup_inputs(seed: int = 0) -> dict:
    key = jax.random.key(seed)
    ks = iter(jax.random.split(key, 32))

    def nrm(shape, scale):
        return jax.random.normal(next(ks), shape, jnp.float32) * scale

    def gain(shape):
        return 1.0 + 0.1 * nrm(shape, 1.0)

    D = D_MODEL
    return {
        'x': nrm((BATCH, SEQ, D), 1.0),
        'meta': nrm((N_META, D), 1.0),
        'norm1_g': gain((DEPTH, D)),
        'w_in': nrm((DEPTH, D, N_IN), D ** -0.5),
        'conv_w': nrm((DEPTH, CONV_K, CONV_DIM), CONV_K ** -0.5),
        'conv_b': nrm((DEPTH, CONV_DIM), 0.02),
        'conv_ln_g': gain((DEPTH, CONV_DIM)),
        'conv_ln_b': nrm((DEPTH, CONV_DIM), 0.02),
        'w_conv_out': nrm((DEPTH, CONV_DIM, D), CONV_DIM ** -0.5),
        'q_a_norm_g': gain((DEPTH, Q_RANK)),
        'w_uq': nrm((DEPTH, Q_RANK, MLA_HEADS * QK_DIM), Q_RANK ** -0.5),
        'kv_a_norm_g': gain((DEPTH, KV_RANK)),
        'w_ukv': nrm((DEPTH, KV_RANK, MLA_HEADS * (NOPE_DIM + V_DIM)), KV_RANK ** -0.5),
        'q_norm_g': gain((DEPTH, QK_DIM)),
        'k_norm_g': gain((DEPTH, QK_DIM)),
        'w_attn_out': nrm((DEPTH, MLA_HEADS * V_DIM, D), (MLA_HEADS * V_DIM) ** -0.5),
        'hgrn_lb_logits': nrm((DEPTH, HGRN_HEADS * HGRN_DK), 1.0),
        'hgrn_norm_g': gain((DEPTH, HGRN_HEADS * HGRN_DV)),
        'w_hgrn_out': nrm((DEPTH, HGRN_HEADS * HGRN_DV, D), (HGRN_HEADS * HGRN_DV) ** -0.5),
        'w_out': nrm((DEPTH, D, D), D ** -0.5),
        'norm2_g': gain((DEPTH, D)),
        'w_ff1': nrm((DEPTH, D, D_FF), D ** -0.5),
        'w_ff2': nrm((DEPTH, D_FF, D), D_FF ** -0.5),
    }


def reference(x, meta, norm1_g, w_in, conv_w, conv_b, conv_ln_g, conv_ln_b, w_conv_out,
              q_a_norm_g, w_uq, kv_a_norm_g, w_ukv, q_norm_g, k_norm_g, w_attn_out,
              hgrn_lb_logits, hgrn_norm_g, w_hgrn_out, w_out, norm2_g, w_ff1, w_ff2):
    B = x.shape[0]
    D = D_MODEL
    x = jnp.concatenate([jnp.broadcast_to(meta[None].astype(x.dtype), (B, N_META, D)), x], axis=1)
    L = x.shape[1]
    half = ROPE_DIM // 2
    pos = jnp.arange(L, dtype=jnp.float32)
    inv_freq = ROPE_BASE ** (-jnp.arange(half, dtype=jnp.float32) / half)
    ang = pos[:, None] * inv_freq[None, :]
    cos = jnp.cos(ang)[None, :, None, :]
    sin = jnp.sin(ang)[None, :, None, :]
    p_lb = jax.nn.softmax(hgrn_lb_logits.astype(jnp.float32), axis=0)
    lower_bounds = jnp.cumsum(p_lb, axis=0) - p_lb[0:1]

    for l in range(DEPTH):
        h = rms_norm(x, norm1_g[l])
        u = h @ w_in[l]
        (u_conv, c_q, c_kv, k_rope, hq, hf, hi, hg, u_gate) = jnp.split(u, SPLIT_POINTS, axis=-1)
        y_a = conv_module(u_conv, conv_w[l], conv_b[l], conv_ln_g[l], conv_ln_b[l], w_conv_out[l])
        y_b = mla(c_q, c_kv, k_rope, cos, sin, q_a_norm_g[l], w_uq[l], kv_a_norm_g[l], w_ukv[l],
                  q_norm_g[l], k_norm_g[l], w_attn_out[l])
        y_c = hgrn2(hq, hf, hi, hg, lower_bounds[l], hgrn_norm_g[l], w_hgrn_out[l])
        gates = jax.nn.sigmoid(u_gate).reshape(B, L, N_BRANCH, D)
        mix = gates[:, :, 0] * y_a + gates[:, :, 1] * y_b + gates[:, :, 2] * y_c
        x = x + mix @ w_out[l]
        h2 = rms_norm(x, norm2_g[l])
        x = x + jnp.square(jax.nn.relu(h2 @ w_ff1[l])) @ w_ff2[l]

    return x[:, N_META:]
```

```python
import numpy as np
import concourse.bass as bass
import concourse.mybir as mybir
from concourse.bass_utils import run_bass_kernel_spmd
from contextlib import ExitStack

F32 = mybir.dt.float32
BF16 = mybir.dt.bfloat16
ALU = mybir.AluOpType
AF = mybir.ActivationFunctionType
AX = mybir.AxisListType

D = 1024
DEPTH = 2
N_META = 16
PAD = 112
EPS = 1e-6
CLAMP = 1.0 - 1e-6
N_IN = 6560
O_CONV, O_CQ, O_CKV, O_KR, O_HQ, O_HF, O_HI, O_HG, O_GA, O_GB, O_GC = (
    0, 1024, 1280, 1408, 1440, 1952, 2464, 2976, 3488, 4512, 5536)
CONV_K = 31
NH = 8
QK = 96
HH = 4

CENGS = ['pe', 'act', 'dve', 'pool']
ENGS = CENGS + ['sp']


class Buf:
    __slots__ = ('name', 'last_w', 'readers', 'dsem', 'dcnt')

    def __init__(self, name):
        self.name = name
        self.last_w = None
        self.readers = {}
        self.dsem = None
        self.dcnt = 0


class Prog:
    def __init__(self, nc, es):
        self.nc = nc
        self.es = es
        self.ops = {e: [] for e in ENGS}
        self.cnt = {e: 0 for e in CENGS}
        self.sem = {e: es.enter_context(nc.semaphore('c_' + e)) for e in CENGS}
        self.known = {e: {e2: 0 for e2 in CENGS} for e in ENGS}
        self.dknown = {e: {} for e in ENGS}
        self.dbufs = []
        self.dsem_free = {'sp': [], 'pool': []}
        self.dsem_q = {}
        self.dsem_tot = {}
        self.nbuf = 0

    def buf(self, name=None):
        self.nbuf += 1
        return Buf('%s_%d' % (name or 'b', self.nbuf))

    def _wait_eng(self, e, e2, idx):
        if e == 'pe' and e2 == 'pe':
            return
        if self.known[e][e2] >= idx:
            return
        self.known[e][e2] = idx
        sem = self.sem[e2]
        self.ops[e].append(lambda eng: eng.wait_ge(sem, idx))

    def _wait_dma(self, e, b):
        if b.dcnt == 0:
            return
        if self.dknown[e].get(b, 0) >= b.dcnt:
            return
        self.dknown[e][b] = b.dcnt
        sem, val = b.dsem, 16 * self.dsem_tot[id(b.dsem)]
        self.ops[e].append(lambda eng: eng.wait_ge(sem, val))

    def _deps(self, e, reads, writes):
        for b in reads:
            self._wait_dma(e, b)
            if b.last_w is not None:
                self._wait_eng(e, *b.last_w)
        for b in writes:
            self._wait_dma(e, b)
            if b.last_w is not None:
                self._wait_eng(e, *b.last_w)
            for e2, idx in b.readers.items():
                self._wait_eng(e, e2, idx)

    def op(self, e, name, reads=(), writes=(), inc=True, **kw):
        self._deps(e, reads, writes)
        if inc:
            self.cnt[e] += 1
            idx = self.cnt[e]
            sem = self.sem[e]
            self.ops[e].append(lambda eng: getattr(eng, name)(**kw).then_inc(sem, 1))
        else:
            idx = self.cnt[e] + 1
            self.ops[e].append(lambda eng: getattr(eng, name)(**kw))
        for b in writes:
            b.last_w = (e, idx)
            b.readers = {}
        for b in reads:
            b.readers[e] = idx

    def dma(self, e, out, in_, b, load, **kw):
        if b.dsem is None:
            if self.dsem_free[e]:
                b.dsem = self.dsem_free[e].pop()
            else:
                b.dsem = self.es.enter_context(self.nc.semaphore('d%d' % len(self.dsem_tot)))
                self.dsem_tot[id(b.dsem)] = 0
            b.dcnt = 0
            self.dsem_q[id(b.dsem)] = e
            self.dbufs.append(b)
        assert self.dsem_q[id(b.dsem)] == e, 'a DMA semaphore must stay on one queue'
        if load:
            self._deps(e, (), (b,))
        else:
            self._deps(e, (b,), ())
        b.dcnt += 1
        self.dsem_tot[id(b.dsem)] += 1
        sem = b.dsem
        self.ops[e].append(lambda eng: eng.dma_start(out=out, in_=in_, **kw).then_inc(sem, 16))
        if load:
            b.last_w = None
            b.readers = {}

    def barrier(self):
        for e in ENGS:
            for e2 in CENGS:
                if e2 != e and self.cnt[e2] > 0:
                    self._wait_eng(e, e2, self.cnt[e2])
            for b in self.dbufs:
                self._wait_dma(e, b)
        for b in self.dbufs:
            if self.dsem_q[id(b.dsem)] == 'sp':
                self.dsem_free['sp'].append(b.dsem)
            b.dsem = None
            b.dcnt = 0
            for e in ENGS:
                self.dknown[e].pop(b, None)
        self.dbufs = []

    def mm(self, out, lhsT, rhs, start, stop, reads, writes, inc=None):
        self.op('pe', 'matmul', reads, writes, inc=(stop if inc is None else inc),
                out=out, lhsT=lhsT, rhs=rhs, start=start, stop=stop)

    def tr(self, out, in_, identity, reads, writes, inc=True):
        self.op('pe', 'transpose', reads, writes, inc=inc, out=out, in_=in_, identity=identity)

    def act(self, out, in_, func, reads, writes, **kw):
        self.op('act', 'activation', reads, writes, out=out, in_=in_, func=func, **kw)

    def tt(self, e, out, in0, in1, op, reads, writes):
        self.op(e, 'tensor_tensor', reads, writes, out=out, in0=in0, in1=in1, op=op)

    def ts(self, e, out, in0, s1, s2, op0, op1, reads, writes):
        if op1 is None:
            self.op(e, 'tensor_scalar', reads, writes, out=out, in0=in0, scalar1=s1, scalar2=None, op0=op0)
        else:
            self.op(e, 'tensor_scalar', reads, writes, out=out, in0=in0, scalar1=s1, scalar2=s2, op0=op0, op1=op1)

    def stt(self, out, in0, scalar, in1, op0, op1, reads, writes):
        self.op('dve', 'scalar_tensor_tensor', reads, writes, out=out, in0=in0, scalar=scalar, in1=in1, op0=op0, op1=op1)

    def cp(self, e, out, in_, reads, writes):
        if e == 'act':
            self.op('act', 'copy', reads, writes, out=out, in_=in_)
        else:
            self.op(e, 'tensor_copy', reads, writes, out=out, in_=in_)

    def rsqrt(self, out, in_, scale, eps, reads, writes):
        self.act(out, in_, AF.Sqrt, reads, writes, scale=scale, bias=self.eps_ap(eps, out))
        self.op('dve', 'reciprocal', list(writes), list(writes), out=out, in_=out)

    def eps_ap(self, eps, like):
        return eps

    def memset(self, e, ap, val, writes):
        self.op(e, 'memset', (), writes, ap=ap, constant=val)

    def emit(self):
        self.barrier()
        nc = self.nc
        with nc.Block() as block:
            @block.tensor
            def _(eng):
                for f in self.ops['pe']:
                    f(eng)

            @block.scalar
            def _(eng):
                for f in self.ops['act']:
                    f(eng)

            @block.vector
            def _(eng):
                for f in self.ops['dve']:
                    f(eng)

            @block.gpsimd
            def _(eng):
                for f in self.ops['pool']:
                    f(eng)

            @block.sync
            def _(eng):
                for f in self.ops['sp']:
                    f(eng)


_UID = [0]


def U(name):
    _UID[0] += 1
    return '%s_u%d' % (name, _UID[0])


class Rot:
    def __init__(self, P, es, name, shape, dtype, n):
        self.tiles = [es.enter_context(P.nc.sbuf_tensor(U('%s%d' % (name, i)), shape, dtype)) for i in range(n)]
        self.bufs = [P.buf('%s%d' % (name, i)) for i in range(n)]
        self.i = 0

    def next(self):
        k = self.i % len(self.tiles)
        self.i += 1
        return self.tiles[k], self.bufs[k]


def groups_of(T, gsz=4):
    gs = [(0, 1)]
    t = 1
    while t < T:
        n = min(gsz, T - t)
        gs.append((t, n))
        t += n
    return gs


class K:
    pass


def load_w_cast(P, dst, dst_buf, src, K_chunks, ncols, col0=0, pk=128):
    v = src.rearrange("(kc p) n -> p kc n", p=pk)
    for k0 in range(0, K_chunks, 8):
        k1 = min(K_chunks, k0 + 8)
        c = 0
        while c < ncols:
            w = min(2048, ncols - c)
            P.dma('pool', dst[:, k0:k1, c:c + w], v[:, k0:k1, col0 + c:col0 + c + w], dst_buf, True)
            c += w


class Prefetch:
    def __init__(self, items, load_fn, ahead=1):
        self.items, self.load_fn, self.ahead = list(items), load_fn, ahead
        self.loaded = {}
        self.next = 0

    def get(self, i):
        while self.next < len(self.items) and self.next <= i + self.ahead:
            self.loaded[self.next] = self.load_fn(*self.items[self.next])
            self.next += 1
        return self.loaded.pop(i)


class BankRot:
    def __init__(self, k, banks):
        self.k, self.banks, self.i = k, list(banks), 0

    def next(self):
        b = self.banks[self.i % len(self.banks)]
        self.i += 1
        return self.k.bank(b)


def build(T, stop_after=None, debug=False, nlayers=DEPTH):
    SEQ = (T - 1) * 128
    TT = T * 128
    nc = bass.Bass("TRN2", target_bir_lowering=False)

    def din(name, shape):
        return nc.dram_tensor(name, list(shape), F32, kind="ExternalInput").ap()

    def dscr(name, shape, dt):
        if debug:
            return nc.dram_tensor(name, list(shape), dt, kind="ExternalOutput").ap()
        return nc.dram_tensor(name, list(shape), dt).ap()

    k = K()
    k.T, k.TT, k.SEQ = T, TT, SEQ
    x_in = din("x", (SEQ, D))
    meta = din("meta", (N_META, D))
    norm1_g = din("norm1_g", (DEPTH, D))
    w_in = din("w_in", (DEPTH, D, N_IN))
    conv_w = din("conv_w", (DEPTH, CONV_K, 512))
    conv_b = din("conv_b", (DEPTH, 512))
    conv_ln_g = din("conv_ln_g", (DEPTH, 512))
    conv_ln_b = din("conv_ln_b", (DEPTH, 512))
    w_conv_out = din("w_conv_out", (DEPTH, 512, D))
    q_a_norm_g = din("q_a_norm_g", (DEPTH, 256))
    w_uq = din("w_uq", (DEPTH, 256, 768))
    kv_a_norm_g = din("kv_a_norm_g", (DEPTH, 128))
    w_ukv = din("w_ukv", (DEPTH, 128, 1024))
    q_norm_g = din("q_norm_g", (DEPTH, 96))
    k_norm_g = din("k_norm_g", (DEPTH, 96))
    w_attn_out = din("w_attn_out", (DEPTH, 512, D))
    hgrn_lb_logits = din("hgrn_lb_logits", (DEPTH, 512))
    hgrn_norm_g = din("hgrn_norm_g", (DEPTH, 512))
    w_hgrn_out = din("w_hgrn_out", (DEPTH, 512, D))
    w_out = din("w_out", (DEPTH, D, D))
    norm2_g = din("norm2_g", (DEPTH, D))
    w_ff1 = din("w_ff1", (DEPTH, D, 4096))
    w_ff2 = din("w_ff2", (DEPTH, 4096, D))
    consts = din("consts", (128, NCONST))
    cs_tab = din("cs_tab", (128, T * 32))
    out = nc.dram_tensor("out", [SEQ, D], F32, kind="ExternalOutput").ap()

    xres = dscr("xres", (TT, D), F32)
    hT = dscr("hT", (D, TT), BF16)
    mixa = dscr("mixa", (D, TT), BF16)
    mixb = dscr("mixb", (D, TT), BF16)
    mixc = dscr("mixc", (D, TT), BF16)
    QT = dscr("QT", (NH, QK, TT), BF16)
    KT = dscr("KT", (NH, QK, TT), BF16)
    VA = dscr("VA", (TT, NH * 65), BF16)
    OT = dscr("OT", (NH, 64, TT), BF16)

    groups = groups_of(T)

    with ExitStack() as es0:
        P = Prog(nc, es0)
        k.P, k.nc = P, nc
        k.PS = [es0.enter_context(nc.psum_tensor('ps%d' % i, [128, 1024], F32)) for i in range(4)]
        k.PSB = [P.buf('psb%d' % i) for i in range(8)]

        def bank(i):
            return k.PS[i // 2][:, (i % 2) * 512:(i % 2) * 512 + 512], k.PSB[i]
        k.bank = bank
        cst = es0.enter_context(nc.sbuf_tensor('cst', [128, NCONST], F32))
        bcst = P.buf('cst')
        P.dma('sp', cst[:], consts, bcst, True)
        identb = es0.enter_context(nc.sbuf_tensor('identb', [128, 128], BF16))
        bidb = P.buf('identb')
        P.cp('dve', identb[:], cst[:, C_IDENT:C_IDENT + 128], [bcst], [bidb])
        k.cst, k.bcst, k.identb, k.bidb = cst, bcst, identb, bidb
        k.identf = cst[:, C_IDENT:C_IDENT + 128]
        k.onesf = cst[:, C_ONES:C_ONES + 128]

        with ExitStack() as es:
            rot = Rot(P, es, 'xi', [128, D], F32, 3)
            tz, bz = rot.next()
            P.memset('pool', tz[:], 0.0, [bz])
            P.dma('sp', tz[PAD:128, :], meta, bz, True)
            P.dma('sp', xres[0:128, :], tz[:], bz, False)
            P.barrier()

        def xsrc(l, t):
            if l == 0 and t >= 1:
                return x_in[(t - 1) * 128:t * 128, :]
            return xres[t * 128:(t + 1) * 128, :]
        k.xsrc = xsrc

        for l in range(nlayers):
            if stop_after == 'init':
                break
            last = (l == nlayers - 1)
            es_wc = ExitStack()
            wconv = conv_load(k, es_wc, w_in[l], conv_w[l], conv_b[l], conv_ln_g[l], conv_ln_b[l], w_conv_out[l])
            phase_norm1(k, l, xres, hT, norm1_g, groups)
            if stop_after == 'p0':
                es_wc.close()
                break
            phase_conv(k, wconv, hT, mixa, groups)
            es_wc.close()
            if stop_after == 'p1':
                break
            phase_mla_pre(k, l, hT, QT, KT, VA, w_in[l], q_a_norm_g[l], w_uq[l], kv_a_norm_g[l], w_ukv[l],
                          q_norm_g, k_norm_g, cs_tab, groups)
            if stop_after == 'p2a':
                break
            es_wh = ExitStack()
            whg = hgrn_load(k, es_wh, l, w_in[l], hgrn_lb_logits, hgrn_norm_g[l], w_hgrn_out[l])
            es_wa = ExitStack()
            wao = attn_out_load(k, es_wa, w_in[l], w_attn_out[l])
            phase_attn(k, l, QT, KT, VA, OT, groups)
            if stop_after == 'p2b':
                es_wa.close(); es_wh.close()
                break
            phase_attn_out(k, wao, hT, OT, mixb, groups)
            es_wa.close()
            if stop_after == 'p2c':
                es_wh.close()
                break
            phase_hgrn(k, whg, hT, mixc, groups)
            es_wh.close()
            if stop_after == 'p3':
                break
            es_w1 = ExitStack()
            w1pre = ffn_load_w1(k, es_w1, w_ff1[l])
            phase_merge(k, l, xres, hT, mixa, mixb, mixc, w_out[l], norm2_g, groups, after_loads=w1pre[2])
            if stop_after == 'p4a':
                es_w1.close()
                break
            phase_ffn(k, l, xres, hT, out, w1pre, w_ff2[l], last)
            es_w1.close()
        P.emit()
    k.P = P
    build.last_k = k
    return nc


def interleave(gens, depth):
    active = []
    it = iter(gens)
    more = True
    while True:
        while more and len(active) < depth:
            try:
                active.append(next(it))
            except StopIteration:
                more = False
        if not active:
            break
        for g in list(active):
            try:
                next(g)
            except StopIteration:
                active.remove(g)


class NormTools:
    def __init__(self, k, es, banks=(0, 1), nrot=4):
        self.k = k
        P = k.P
        self.junk = Rot(P, es, 'njunk', [128, D], BF16, 2)
        self.ss = Rot(P, es, 'nss', [128, 4], F32, nrot)
        self.hb = Rot(P, es, 'nhb', [128, D], BF16, nrot)
        self.br = BankRot(k, banks)

    def norm_tile_gen(self, xt, bx, gbc, bg, hg, bhg, j):
        k = self.k
        P = k.P
        jk, bj = self.junk.next()
        ss, bss = self.ss.next()
        hb, bhb = self.hb.next()
        P.act(jk[:], xt[:], AF.Square, [bx], [bj, bss], accum_out=ss[:, 0:1])
        P.rsqrt(ss[:, 2:3], ss[:, 0:1], 1.0 / D, EPS, [bss], [bss])
        P.stt(hb[:], xt[:], ss[:, 2:3], gbc[:], ALU.mult, ALU.mult, [bx, bss, bg], [bhb])
        yield
        pb, bpb = self.br.next()
        pbb = pb.bitcast(BF16)
        for kc in range(8):
            P.tr(pbb[:, kc * 128:(kc + 1) * 128], hb[:, kc * 128:(kc + 1) * 128], k.identb[:],
                 [bhb, k.bidb], [bpb], inc=(kc == 7))
        P.cp('act', hg[:, :, j * 128:(j + 1) * 128], pbb.rearrange("p (kc t) -> p kc t", kc=8), [bpb], [bhg])


def phase_norm1(k, l, xres, hT, norm1_g, groups):
    P, nc = k.P, k.nc
    with ExitStack() as es:
        gbc = es.enter_context(nc.sbuf_tensor(U('gbc'), [128, D], F32))
        bg = P.buf('gbc')
        P.dma('sp', gbc[:], norm1_g[l:l + 1, :].to_broadcast([128, D]), bg, True)
        xr = Rot(P, es, 'x', [128, D], F32, 6)
        hgr = Rot(P, es, 'hTg', [128, 8, 512], BF16, 3)
        nt = NormTools(k, es)
        hTv = hT.rearrange("(kc p) t -> p kc t", p=128)

        def load_x(t):
            xt, bx = xr.next()
            P.dma('sp', xt[:], k.xsrc(l, t), bx, True)
            return xt, bx
        pfx = Prefetch([(t,) for t in range(k.T)], load_x, ahead=2)

        def tile_gen(t0, n, j, hg, bhg):
            t = t0 + j
            xt, bx = pfx.get(t)
            yield from nt.norm_tile_gen(xt, bx, gbc, bg, hg, bhg, j)
            if j == n - 1:
                P.dma('sp', hTv[:, :, t0 * 128:(t0 + n) * 128], hg[:, :, 0:n * 128], bhg, False)

        def all_tiles():
            for (t0, n) in groups:
                hg, bhg = hgr.next()
                for j in range(n):
                    yield tile_gen(t0, n, j, hg, bhg)
        interleave(all_tiles(), 3)
        P.barrier()


def conv_load(k, es, w_in_l, conv_w, conv_b, ln_g, ln_b, w_co):
    P, nc = k.P, k.nc
    if True:
        def sb(name, shape, dt):
            return es.enter_context(nc.sbuf_tensor(U(name), shape, dt))
        Wc = sb('Wc', [128, 8, 1024], BF16); bWc = P.buf('Wc')
        Wga = sb('Wga', [128, 8, 1024], BF16); bWga = P.buf('Wga')
        Wco = sb('Wco', [128, 4, 1024], BF16); bWco = P.buf('Wco')
        load_w_cast(P, Wc, bWc, w_in_l, 8, 1024, O_CONV)
        load_w_cast(P, Wga, bWga, w_in_l, 8, 1024, O_GA)
        load_w_cast(P, Wco, bWco, w_co, 4, 1024, 0)
        cw31 = sb('cw31', [32, 512], F32); bcw31 = P.buf('cw31')
        P.dma('sp', cw31[0:CONV_K, :], conv_w, bcw31, True)
        cw = sb('cw', [128, 4, 32], F32); bcw = P.buf('cw')
        pb, bpb = k.bank(0)
        for cc in range(4):
            P.tr(pb[:, cc * 32:cc * 32 + CONV_K], cw31[0:CONV_K, cc * 128:(cc + 1) * 128],
                 k.identf[0:CONV_K, 0:CONV_K], [bcw31, k.bcst], [bpb], inc=(cc == 3))
        P.cp('dve', cw[:, :, 0:CONV_K], pb[:, 0:128].rearrange("p (c j) -> p c j", c=4)[:, :, 0:CONV_K], [bpb], [bcw])
        Dg = sb('Dg', [128, 4, CONV_K, 128], BF16); bDg = P.buf('Dg')
        for cc in range(4):
            P.tt('dve' if cc % 2 == 0 else 'pool', Dg[:, cc, :, :],
                 k.identf.unsqueeze(1).to_broadcast([128, CONV_K, 128]),
                 cw[:, cc, 0:CONV_K].unsqueeze(2).to_broadcast([128, CONV_K, 128]), ALU.mult, [k.bcst, bcw], [bDg])
        cols = sb('ccols', [128, 12], F32); bcols = P.buf('ccols')
        for i, src in enumerate((conv_b, ln_g, ln_b)):
            P.dma('sp', cols[:, i * 4:(i + 1) * 4], src.rearrange("(c p) -> p c", p=128), bcols, True,
                  allow_slow_non_contiguous=True)
    return dict(Wc=Wc, bWc=bWc, Wga=Wga, bWga=bWga, Wco=Wco, bWco=bWco, Dg=Dg, bDg=bDg, cols=cols, bcols=bcols)


def phase_conv(k, w, hT, mixa, groups):
    P, nc = k.P, k.nc
    Wc, bWc, Wga, bWga, Wco, bWco = w['Wc'], w['bWc'], w['Wga'], w['bWga'], w['Wco'], w['bWco']
    Dg, bDg, cols, bcols = w['Dg'], w['bDg'], w['cols'], w['bcols']
    with ExitStack() as es:
        hgr = Rot(P, es, 'hTg', [128, 8, 512], BF16, 3)
        hcr = Rot(P, es, 'hc', [128, 4, 30 + 512], BF16, 2)
        sgr = Rot(P, es, 'sg', [128, 512], F32, 2)
        cvr = Rot(P, es, 'cv', [128, 4, 512], F32, 2)
        sqr = Rot(P, es, 'sq', [128, 4, 512], F32, 2)
        str_ = Rot(P, es, 'st', [128, 4, 512], F32, 2)
        xcr = Rot(P, es, 'xc', [128, 512], F32, 2)
        actr = Rot(P, es, 'cact', [128, 4, 512], BF16, 2)
        gar = Rot(P, es, 'ga', [128, 512], F32, 2)
        mgr = Rot(P, es, 'mixg', [128, 8, 512], BF16, 2)
        hTv = hT.rearrange("(kc p) t -> p kc t", p=128)
        mixv = mixa.rearrange("(kc p) t -> p kc t", p=128)
        br = BankRot(k, range(8))
        prev = {}

        def load_hg(t0, n):
            hg, bhg = hgr.next()
            P.dma('sp', hg[:, :, 0:n * 128], hTv[:, :, t0 * 128:(t0 + n) * 128], bhg, True)
            return hg, bhg
        pf = Prefetch(groups, load_hg)

        def group_gen(gi, t0, n):
            N = n * 128
            tok0 = t0 * 128
            hg, bhg = pf.get(gi)
            hc, bhc = hcr.next()
            if prev:
                P.cp('pool', hc[:, :, 0:30], prev['hc'][:, :, prev['N']:prev['N'] + 30], [prev['bhc']], [bhc])
            else:
                P.memset('pool', hc[:, :, 0:30], 0.0, [bhc])
            for cc in range(4):
                pa, bpa = br.next()
                pg, bpg = br.next()
                for kc in range(8):
                    P.mm(pa[:, 0:N], Wc[:, kc, cc * 128:(cc + 1) * 128], hg[:, kc, 0:N], kc == 0, kc == 7, [bWc, bhg], [bpa])
                for kc in range(8):
                    P.mm(pg[:, 0:N], Wc[:, kc, 512 + cc * 128:512 + (cc + 1) * 128], hg[:, kc, 0:N], kc == 0, kc == 7, [bWc, bhg], [bpg])
                sg, bsg = sgr.next()
                P.act(sg[:, 0:N], pg[:, 0:N], AF.Sigmoid, [bpg], [bsg])
                P.tt('dve', hc[:, cc, 30:30 + N], pa[:, 0:N], sg[:, 0:N], ALU.mult, [bpa, bsg], [bhc])
            prev.update(hc=hc, bhc=bhc, N=N)
            yield
            cv, bcv = cvr.next()
            sq, bsq = sqr.next()
            for cc in range(4):
                pc, bpc = br.next()
                for j in range(CONV_K):
                    P.mm(pc[:, 0:N], Dg[:, cc, j, :], hc[:, cc, j:j + N], j == 0, j == CONV_K - 1, [bDg, bhc], [bpc])
                P.act(cv[:, cc, 0:N], pc[:, 0:N], AF.Identity, [bpc, bcols], [bcv], bias=cols[:, cc:cc + 1], scale=1.0)
                P.act(sq[:, cc, 0:N], pc[:, 0:N], AF.Square, [bpc, bcols], [bsq], bias=cols[:, cc:cc + 1], scale=1.0)
            yield
            st, bst = str_.next()
            pm, bpm = br.next()
            pq, bpq = br.next()
            for cc in range(4):
                P.mm(pm[:, 0:N], k.onesf, cv[:, cc, 0:N], cc == 0, cc == 3, [k.bcst, bcv], [bpm])
            for cc in range(4):
                P.mm(pq[:, 0:N], k.onesf, sq[:, cc, 0:N], cc == 0, cc == 3, [k.bcst, bsq], [bpq])
            P.act(st[:, 0, 0:N], pm[:, 0:N], AF.Copy, [bpm], [bst], scale=1.0 / 512)
            P.tt('pool', st[:, 1, 0:N], st[:, 0, 0:N], st[:, 0, 0:N], ALU.mult, [bst], [bst])
            P.stt(st[:, 2, 0:N], pq[:, 0:N], 1.0 / 512, st[:, 1, 0:N], ALU.mult, ALU.subtract, [bpq, bst], [bst])
            P.rsqrt(st[:, 3, 0:N], st[:, 2, 0:N], 1.0, EPS, [bst], [bst])
            yield
            act, bact = actr.next()
            for cc in range(4):
                xc, bxc = xcr.next()
                P.tt('pool', xc[:, 0:N], cv[:, cc, 0:N], st[:, 0, 0:N], ALU.subtract, [bcv, bst], [bxc])
                P.tt('dve', xc[:, 0:N], xc[:, 0:N], st[:, 3, 0:N], ALU.mult, [bxc, bst], [bxc])
                P.act(act[:, cc, 0:N], xc[:, 0:N], AF.Silu, [bxc, bcols], [bact],
                      bias=cols[:, 8 + cc:9 + cc], scale=cols[:, 4 + cc:5 + cc])
            yield
            mg, bmg = mgr.next()
            for nn in range(8):
                py, bpy = br.next()
                pgt, bpgt = br.next()
                for cc in range(4):
                    P.mm(py[:, 0:N], Wco[:, cc, nn * 128:(nn + 1) * 128], act[:, cc, 0:N], cc == 0, cc == 3, [bWco, bact], [bpy])
                for kc in range(8):
                    P.mm(pgt[:, 0:N], Wga[:, kc, nn * 128:(nn + 1) * 128], hg[:, kc, 0:N], kc == 0, kc == 7, [bWga, bhg], [bpgt])
                ga, bga = gar.next()
                P.act(ga[:, 0:N], pgt[:, 0:N], AF.Sigmoid, [bpgt], [bga])
                P.tt('dve', mg[:, nn, 0:N], py[:, 0:N], ga[:, 0:N], ALU.mult, [bpy, bga], [bmg])
            P.dma('sp', mixv[:, :, tok0:tok0 + N], mg[:, :, 0:N], bmg, False)

        interleave((group_gen(gi, t0, n) for gi, (t0, n) in enumerate(groups)), 2)
        P.barrier()

def phase_mla_pre(k, l, hT, QT, KT, VA, w_in_l, q_a_g, w_uq, kv_a_g, w_ukv, q_norm_g, k_norm_g, cs_tab, groups):
    P, nc = k.P, k.nc
    T = k.T
    with ExitStack() as es:
        def sb(name, shape, dt):
            return es.enter_context(nc.sbuf_tensor(U(name), shape, dt))
        Wm = sb('Wm', [128, 8, 416], BF16); bWm = P.buf('Wm')
        load_w_cast(P, Wm, bWm, w_in_l, 8, 416, O_CQ)
        wtmp = sb('wtmp', [128, 2, 1024], F32); bwt = P.buf('wtmp')
        gcol = sb('gcol', [128, 4], F32); bgc = P.buf('gcol')
        P.dma('sp', gcol[:, 0:2], q_a_g.rearrange("(c p) -> p c", p=128), bgc, True, allow_slow_non_contiguous=True)
        P.dma('sp', gcol[:, 2:3], kv_a_g.rearrange("(c p) -> p c", p=128), bgc, True, allow_slow_non_contiguous=True)
        Wuq = sb('Wuq', [128, 2, 768], BF16); bWuq = P.buf('Wuq')
        Wukv = sb('Wukv', [128, 1024], BF16); bWukv = P.buf('Wukv')
        P.dma('sp', wtmp[:, :, 0:768], w_uq.rearrange("(c p) n -> p c n", p=128), bwt, True)
        for c in range(2):
            P.ts('dve', Wuq[:, c, :], wtmp[:, c, 0:768], gcol[:, c:c + 1], None, ALU.mult, None, [bwt, bgc], [bWuq])
        P.dma('sp', wtmp[:, 0, :], w_ukv, bwt, True)
        P.ts('dve', Wukv[:], wtmp[:, 0, :], gcol[:, 2:3], None, ALU.mult, None, [bwt, bgc], [bWukv])
        gqk = sb('gqk', [128, 2, 96], F32); bgqk = P.buf('gqk')
        P.dma('sp', gqk[:, 0, :], q_norm_g[l:l + 1, :].to_broadcast([128, 96]), bgqk, True)
        P.dma('sp', gqk[:, 1, :], k_norm_g[l:l + 1, :].to_broadcast([128, 96]), bgqk, True)
        P.ts('dve', gqk[:, 0, :], gqk[:, 0, :], float(QK) ** -0.5, None, ALU.mult, None, [bgqk], [bgqk])
        cs = sb('cs', [128, T, 32], F32); bcs = P.buf('cs')
        P.dma('sp', cs[:], cs_tab.rearrange("p (t e) -> p t e", e=32), bcs, True)

        hgr = Rot(P, es, 'hTg', [128, 8, 512], BF16, 3)
        cTr = Rot(P, es, 'cT', [128, 3, 512], BF16, 2)
        junk = sb('mjunk', [128, 256], BF16); bjunk = P.buf('mjunk')
        smr = Rot(P, es, 'sm', [128, 48], F32, 4)
        krr_ = Rot(P, es, 'kr', [128, 4, 32], F32, 4)
        sqr = Rot(P, es, 'sqq', [128, 768], F32, 4)
        qnr = Rot(P, es, 'qn', [128, 8, 96], F32, 4)
        rtr = Rot(P, es, 'rt', [128, 4, 8, 16], F32, 4)
        qfr = Rot(P, es, 'qf', [128, 8, 96], BF16, 4)
        kfr = Rot(P, es, 'kf', [128, 8, 96], BF16, 4)
        knr = Rot(P, es, 'kn', [128, 8, 64], F32, 4)
        var_ = Rot(P, es, 'va', [128, 8, 65], BF16, 5)
        for t_, b_ in zip(var_.tiles, var_.bufs):
            P.memset('pool', t_[:], 1.0, [b_])
        QTg = Rot(P, es, 'QTg', [96, 8, 512], BF16, 2)
        KTg = Rot(P, es, 'KTg', [96, 8, 512], BF16, 2)
        hTv = hT.rearrange("(kc p) t -> p kc t", p=128)
        QTv = QT.rearrange("h d t -> d h t")
        KTv = KT.rearrange("h d t -> d h t")

        def load_hg(t0, n):
            hg, bhg = hgr.next()
            P.dma('sp', hg[:, :, 0:n * 128], hTv[:, :, t0 * 128:(t0 + n) * 128], bhg, True)
            return hg, bhg
        pf = Prefetch(groups, load_hg)

        def group_prep(t0, n):
            N = n * 128
            tok0 = t0 * 128
            hg, bhg = pf.get(groups.index((t0, n)))
            cT, bcT = cTr.next()
            pb, bpb = k.bank(0)
            for ch in range(3):
                for kc in range(8):
                    P.mm(pb[:, 0:N], Wm[:, kc, ch * 128:(ch + 1) * 128], hg[:, kc, 0:N], kc == 0, kc == 7, [bWm, bhg], [bpb])
                P.cp('act', cT[:, ch, 0:N], pb[:, 0:N], [bpb], [bcT])
            qtg, bqtg = QTg.next()
            ktg, bktg = KTg.next()
            return dict(N=N, tok0=tok0, hg=hg, bhg=bhg, cT=cT, bcT=bcT, qtg=qtg, bqtg=bqtg, ktg=ktg, bktg=bktg)

        def tile_gen(t0, n, j, gs):
            if j == 0:
                gs.update(group_prep(t0, n))
            N, tok0, hg, bhg, cT, bcT = gs['N'], gs['tok0'], gs['hg'], gs['bhg'], gs['cT'], gs['bcT']
            qtg, bqtg, ktg, bktg = gs['qtg'], gs['bqtg'], gs['ktg'], gs['bktg']
            t = t0 + j
            c0, c1 = j * 128, (j + 1) * 128
            sm, bsm = smr.next()
            kr, bkr = krr_.next()
            ptm, bptm = k.bank(1)
            for kc in range(8):
                P.mm(ptm[:, 0:416], hg[:, kc, c0:c1], Wm[:, kc, 0:416], kc == 0, kc == 7, [bhg, bWm], [bptm])
            P.act(junk[:, 0:256], ptm[:, 0:256], AF.Square, [bptm], [bjunk, bsm], accum_out=sm[:, 0:1])
            P.act(junk[:, 0:128], ptm[:, 256:384], AF.Square, [bptm], [bjunk, bsm], accum_out=sm[:, 1:2])
            P.act(junk[:, 0:32], ptm[:, 384:416], AF.Square, [bptm], [bjunk, bsm], accum_out=sm[:, 2:3])
            P.cp('act', kr[:, 0, :], ptm[:, 384:416], [bptm], [bkr])
            P.rsqrt(sm[:, 3:4], sm[:, 0:1], 1.0 / 256, EPS, [bsm], [bsm])
            P.rsqrt(sm[:, 4:5], sm[:, 1:2], 1.0 / 128, EPS, [bsm], [bsm])
            P.stt(sm[:, 5:6], sm[:, 3:4], 1.0 / QK, sm[:, 3:4], ALU.mult, ALU.mult, [bsm], [bsm])
            P.tt('dve', sm[:, 6:7], sm[:, 4:5], sm[:, 4:5], ALU.mult, [bsm], [bsm])
            yield
            pq0, bpq0 = k.bank(2)
            pq1, bpq1 = k.bank(3)
            pq = k.PS[1]
            for c in range(2):
                P.mm(pq0[:, 0:512], cT[:, c, c0:c1], Wuq[:, c, 0:512], c == 0, c == 1, [bcT, bWuq], [bpq0])
            for c in range(2):
                P.mm(pq1[:, 0:256], cT[:, c, c0:c1], Wuq[:, c, 512:768], c == 0, c == 1, [bcT, bWuq], [bpq1])
            sq, bsq = sqr.next()
            P.act(sq[:], pq[:, 0:768], AF.Square, [bpq0, bpq1], [bsq])
            P.op('dve', 'tensor_reduce', [bsq], [bsm], out=sm[:, 8:16], in_=sq[:].rearrange("p (h d) -> p h d", h=8),
                 axis=AX.X, op=ALU.add)
            P.ts('dve', sm[:, 8:16], sm[:, 8:16], sm[:, 5:6], None, ALU.mult, None, [bsm], [bsm])
            P.rsqrt(sm[:, 16:24], sm[:, 8:16], 1.0, EPS, [bsm], [bsm])
            P.ts('dve', sm[:, 16:24], sm[:, 16:24], sm[:, 3:4], None, ALU.mult, None, [bsm], [bsm])
            qn, bqn = qnr.next()
            P.tt('dve', qn[:], pq[:, 0:768].rearrange("p (h d) -> p h d", h=8),
                 sm[:, 16:24].unsqueeze(2).to_broadcast([128, 8, 96]), ALU.mult, [bpq0, bpq1, bsm], [bqn])
            yield
            P.tt('dve', qn[:], qn[:], gqk[:, 0, :].unsqueeze(1).to_broadcast([128, 8, 96]), ALU.mult, [bqn, bgqk], [bqn])
            qf, bqf = qfr.next()
            P.cp('act', qf[:, :, 0:64], qn[:, :, 0:64], [bqn], [bqf])
            yield
            rt, brt = rtr.next()
            cosb = cs[:, t, 0:16].unsqueeze(1).to_broadcast([128, 8, 16])
            sinb = cs[:, t, 16:32].unsqueeze(1).to_broadcast([128, 8, 16])
            x1, x2 = qn[:, :, 64:80], qn[:, :, 80:96]
            P.tt('pool', rt[:, 0], x1, cosb, ALU.mult, [bqn, bcs], [brt])
            P.tt('pool', rt[:, 1], x2, sinb, ALU.mult, [bqn, bcs], [brt])
            P.tt('pool', rt[:, 2], x1, sinb, ALU.mult, [bqn, bcs], [brt])
            P.tt('pool', rt[:, 3], x2, cosb, ALU.mult, [bqn, bcs], [brt])
            P.tt('pool', qf[:, :, 64:80], rt[:, 0], rt[:, 1], ALU.subtract, [brt], [bqf])
            P.tt('pool', qf[:, :, 80:96], rt[:, 2], rt[:, 3], ALU.add, [brt], [bqf])
            yield
            pk0, bpk0 = k.bank(4)
            pk1, bpk1 = k.bank(5)
            pkv = k.PS[2]
            P.mm(pk0[:, 0:512], cT[:, 2, c0:c1], Wukv[:, 0:512], True, True, [bcT, bWukv], [bpk0])
            P.mm(pk1[:, 0:512], cT[:, 2, c0:c1], Wukv[:, 512:1024], True, True, [bcT, bWukv], [bpk1])
            pkv3 = pkv[:, :].rearrange("p (h e) -> p h e", h=8)
            kn, bkn = knr.next()
            P.act(kn[:], pkv3[:, :, 0:64], AF.Square, [bpk0, bpk1], [bkn])
            P.op('dve', 'tensor_reduce', [bkn], [bsm], out=sm[:, 24:32], in_=kn[:], axis=AX.X, op=ALU.add)
            P.ts('dve', sm[:, 24:32], sm[:, 24:32], sm[:, 6:7], sm[:, 2:3], ALU.mult, ALU.add, [bsm], [bsm])
            P.rsqrt(sm[:, 32:40], sm[:, 24:32], 1.0 / QK, EPS, [bsm], [bsm])
            P.ts('dve', sm[:, 40:48], sm[:, 32:40], sm[:, 4:5], None, ALU.mult, None, [bsm], [bsm])
            P.tt('dve', kn[:], pkv3[:, :, 0:64], sm[:, 40:48].unsqueeze(2).to_broadcast([128, 8, 64]), ALU.mult,
                 [bpk0, bpk1, bsm], [bkn])
            va, bva = var_.next()
            P.ts('dve', va[:, :, 0:64], pkv3[:, :, 64:128], sm[:, 4:5], None, ALU.mult, None, [bpk0, bpk1, bsm], [bva])
            P.dma('sp', VA[t * 128:(t + 1) * 128, :], va[:].rearrange("p h e -> p (h e)"), bva, False)
            yield
            kf, bkf = kfr.next()
            P.tt('dve', kf[:, :, 0:64], kn[:], gqk[:, 1, 0:64].unsqueeze(1).to_broadcast([128, 8, 64]), ALU.mult, [bkn, bgqk], [bkf])
            yield
            P.tt('pool', kr[:, 1, :], kr[:, 0, :], gqk[:, 1, 64:96], ALU.mult, [bkr, bgqk], [bkr])
            P.tt('pool', kr[:, 2, 0:16], kr[:, 1, 0:16], cs[:, t, 0:16], ALU.mult, [bkr, bcs], [bkr])
            P.tt('pool', kr[:, 2, 16:32], kr[:, 1, 16:32], cs[:, t, 16:32], ALU.mult, [bkr, bcs], [bkr])
            P.tt('pool', kr[:, 3, 0:16], kr[:, 2, 0:16], kr[:, 2, 16:32], ALU.subtract, [bkr], [bkr])
            P.tt('pool', kr[:, 2, 0:16], kr[:, 1, 0:16], cs[:, t, 16:32], ALU.mult, [bkr, bcs], [bkr])
            P.tt('pool', kr[:, 2, 16:32], kr[:, 1, 16:32], cs[:, t, 0:16], ALU.mult, [bkr, bcs], [bkr])
            P.tt('pool', kr[:, 3, 16:32], kr[:, 2, 0:16], kr[:, 2, 16:32], ALU.add, [bkr], [bkr])
            P.tt('pool', kf[:, :, 64:96], kr[:, 3, :].unsqueeze(1).to_broadcast([128, 8, 32]),
                 sm[:, 32:40].unsqueeze(2).to_broadcast([128, 8, 32]), ALU.mult, [bkr, bsm], [bkf])
            yield
            ptq, bptq = k.bank(6)
            ptk, bptk = k.bank(7)
            ptqb = ptq.bitcast(BF16)
            ptkb = ptk.bitcast(BF16)
            for h in range(NH):
                P.tr(ptqb[0:QK, h * 128:(h + 1) * 128], qf[:, h, :], k.identb[:], [bqf, k.bidb], [bptq], inc=(h == NH - 1))
            P.cp('act', qtg[:, :, c0:c1], ptqb[0:QK, :].rearrange("p (h t) -> p h t", h=8), [bptq], [bqtg])
            for h in range(NH):
                P.tr(ptkb[0:QK, h * 128:(h + 1) * 128], kf[:, h, :], k.identb[:], [bkf, k.bidb], [bptk], inc=(h == NH - 1))
            P.cp('dve', ktg[:, :, c0:c1], ptkb[0:QK, :].rearrange("p (h t) -> p h t", h=8), [bptk], [bktg])
            if j == n - 1:
                P.dma('sp', QTv[:, :, tok0:tok0 + N], qtg[:, :, 0:N], bqtg, False)
                P.dma('sp', KTv[:, :, tok0:tok0 + N], ktg[:, :, 0:N], bktg, False)

        def all_tiles():
            for (t0, n) in groups:
                gs = {}
                for j in range(n):
                    yield tile_gen(t0, n, j, gs)
        interleave(all_tiles(), 3)
        P.barrier()


def phase_attn(k, l, QT, KT, VA, OT, groups):
    P, nc = k.P, k.nc
    T, TT = k.T, k.TT
    LA = 2
    with ExitStack() as es:
        def sb(name, shape, dt):
            return es.enter_context(nc.sbuf_tensor(U(name), shape, dt))
        Vall = sb('Vall', [128, T, NH * 65], BF16); bV = P.buf('Vall')
        P.dma('sp', Vall[:], VA.rearrange("(t p) e -> p t e", p=128), bV, True)
        qhr = Rot(P, es, 'QTh', [QK, TT], BF16, 2)
        khr = Rot(P, es, 'KTh', [QK, TT], BF16, 2)
        pTr = Rot(P, es, 'pT', [128, 2, 512], BF16, LA + 2)
        osr = Rot(P, es, 'osb', [65, 512], F32, 3)
        rcr = Rot(P, es, 'rc', [65, 512], F32, 3)
        otr = Rot(P, es, 'ot', [64, 512], BF16, 3)
        obr = BankRot(k, (0, 1))
        pair_i = [0]
        tri = k.cst[:, C_TRI:C_TRI + 128]
        padb = k.cst[:, C_PADB:C_PADB + 1]
        heads = {}

        def load_head(h):
            qh, bqh = qhr.next()
            kh, bkh = khr.next()
            P.dma('sp', qh[:], QT[h], bqh, True)
            P.dma('sp', kh[:], KT[h], bkh, True)
            heads[h] = (qh, bqh, kh, bkh)

        items = []
        for h in range(NH):
            for (t0, n) in groups:
                kts = list(range(t0 + n))
                i = 0
                while i < len(kts):
                    kt = kts[i]
                    if 1 <= kt and kt + 1 < t0:
                        items.append(dict(h=h, t0=t0, n=n, kts=[kt, kt + 1]))
                        i += 2
                    else:
                        items.append(dict(h=h, t0=t0, n=n, kts=[kt]))
                        i += 1
        state = {}

        def emit_score(it):
            h, t0, n, kts = it['h'], it['t0'], it['n'], it['kts']
            if h not in heads:
                load_head(h)
            if kts[0] == 0 and t0 == 0 and h + 1 < NH and (h + 1) not in heads:
                load_head(h + 1)
            qh, bqh, kh, bkh = heads[h]
            N = n * 128
            tok0 = t0 * 128
            pi_ = 1 + (pair_i[0] % 3)
            pair_i[0] += 1
            pp = k.PS[pi_]
            (p0, bp0), (p1, bp1) = k.bank(2 * pi_), k.bank(2 * pi_ + 1)
            pT, bpT = pTr.next()
            if len(kts) == 2:
                for j, (pb_, bpb_) in enumerate(((p0, bp0), (p1, bp1))):
                    kt = kts[j]
                    P.mm(pb_[:, 0:N], kh[:, kt * 128:(kt + 1) * 128], qh[:, tok0:tok0 + N], True, True, [bkh, bqh], [bpb_])
                if N == 512:
                    P.act(pT[:].rearrange("p a b -> p (a b)"), pp[:, 0:1024], AF.Exp, [bp0, bp1], [bpT])
                else:
                    P.act(pT[:, :, 0:N], pp[:, :].rearrange("p (a b) -> p a b", a=2)[:, :, 0:N], AF.Exp, [bp0, bp1], [bpT])
                it['c0'] = 0
            else:
                kt = kts[0]
                c0 = max(kt - t0, 0) * 128
                P.mm(p0[:, 0:N - c0], kh[:, kt * 128:(kt + 1) * 128], qh[:, tok0 + c0:tok0 + N], True, True, [bkh, bqh], [bp0])
                if kt == 0:
                    P.act(pT[:, 0, 0:N - c0], p0[:, 0:N - c0], AF.Exp, [bp0, k.bcst], [bpT], bias=padb, scale=1.0)
                else:
                    P.act(pT[:, 0, 0:N - c0], p0[:, 0:N - c0], AF.Exp, [bp0], [bpT])
                if kt >= t0:
                    P.tt('dve', pT[:, 0, 0:128], pT[:, 0, 0:128], tri, ALU.mult, [bpT, k.bcst], [bpT])
                it['c0'] = c0
            it['pT'], it['bpT'], it['N'] = pT, bpT, N

        pending = []

        def emit_pv(it):
            h, t0, n, kts = it['h'], it['t0'], it['n'], it['kts']
            N, c0 = it['N'], it['c0']
            if kts[0] == 0:
                state['po'] = obr.next()
            po, bpo = state['po']
            for j, kt in enumerate(kts):
                P.mm(po[0:65, c0:N], Vall[:, kt, h * 65:(h + 1) * 65], it['pT'][:, j, 0:N - c0], kt == 0, kt == t0 + n - 1,
                     [bV, it['bpT']], [bpo])
            if kts[-1] == t0 + n - 1:
                osb, bos = osr.next()
                P.cp('act', osb[:, 0:N], po[0:65, 0:N], [bpo], [bos])
                rc, brc = rcr.next()
                P.ts('dve', rc[64:65, 0:N], osb[64:65, 0:N], 1e-30, None, ALU.add, None, [bos], [brc])
                P.op('dve', 'reciprocal', [brc], [brc], out=rc[64:65, 0:N], in_=rc[64:65, 0:N])
                pending.append(dict(h=h, t0=t0, N=N, osb=osb, bos=bos, rc=rc, brc=brc, age=0))

        def emit_final(f):
            N, tok0 = f['N'], f['t0'] * 128
            pi_ = 1 + (pair_i[0] % 3)
            pair_i[0] += 1
            pbc, bpbc = k.bank(2 * pi_)
            P.mm(pbc[0:64, 0:N], k.onesf[64:65, 0:64], f['rc'][64:65, 0:N], True, True, [k.bcst, f['brc']], [bpbc])
            ot, bot = otr.next()
            P.tt('dve', ot[:, 0:N], f['osb'][0:64, 0:N], pbc[0:64, 0:N], ALU.mult, [f['bos'], bpbc], [bot])
            P.dma('sp', OT[f['h'], :, tok0:tok0 + N], ot[:, 0:N], bot, False)

        for i in range(len(items) + LA):
            if i < len(items):
                emit_score(items[i])
            if i - LA >= 0:
                emit_pv(items[i - LA])
            for f in pending:
                f['age'] += 1
            while pending and pending[0]['age'] > 4:
                emit_final(pending.pop(0))
        while pending:
            emit_final(pending.pop(0))
        P.barrier()


def attn_out_load(k, es, w_in_l, w_ao):
    P, nc = k.P, k.nc

    def sb(name, shape, dt):
        return es.enter_context(nc.sbuf_tensor(U(name), shape, dt))
    Wao = sb('Wao', [64, 8, 1024], BF16); bWao = P.buf('Wao')
    load_w_cast(P, Wao, bWao, w_ao, 8, 1024, 0, pk=64)
    Wgb = sb('Wgb', [128, 8, 1024], BF16); bWgb = P.buf('Wgb')
    load_w_cast(P, Wgb, bWgb, w_in_l, 8, 1024, O_GB)
    return dict(Wao=Wao, bWao=bWao, Wgb=Wgb, bWgb=bWgb)


def phase_attn_out(k, w, hT, OT, mixb, groups):
    P, nc = k.P, k.nc
    Wao, bWao, Wgb, bWgb = w['Wao'], w['bWao'], w['Wgb'], w['bWgb']
    with ExitStack() as es:
        hgr = Rot(P, es, 'hTg', [128, 8, 512], BF16, 2)
        ogr = Rot(P, es, 'OTg', [64, 8, 512], BF16, 2)
        gar = Rot(P, es, 'gb', [128, 512], F32, 2)
        mgr = Rot(P, es, 'mixg', [128, 8, 512], BF16, 2)
        hTv = hT.rearrange("(kc p) t -> p kc t", p=128)
        mixv = mixb.rearrange("(kc p) t -> p kc t", p=128)
        OTv = OT.rearrange("h d t -> d h t")
        br = BankRot(k, range(8))
        def loads(t0, n):
            N = n * 128
            tok0 = t0 * 128
            hg, bhg = hgr.next()
            P.dma('sp', hg[:, :, 0:N], hTv[:, :, tok0:tok0 + N], bhg, True)
            og, bog = ogr.next()
            P.dma('sp', og[:, :, 0:N], OTv[:, :, tok0:tok0 + N], bog, True)
            return hg, bhg, og, bog

        nxt = loads(*groups[0])
        for gi, (t0, n) in enumerate(groups):
            N = n * 128
            tok0 = t0 * 128
            hg, bhg, og, bog = nxt
            if gi + 1 < len(groups):
                nxt = loads(*groups[gi + 1])
            mg, bmg = mgr.next()
            for nn in range(8):
                py, bpy = br.next()
                pgt, bpgt = br.next()
                for h in range(NH):
                    P.mm(py[:, 0:N], Wao[:, h, nn * 128:(nn + 1) * 128], og[:, h, 0:N], h == 0, h == NH - 1, [bWao, bog], [bpy])
                for kc in range(8):
                    P.mm(pgt[:, 0:N], Wgb[:, kc, nn * 128:(nn + 1) * 128], hg[:, kc, 0:N], kc == 0, kc == 7, [bWgb, bhg], [bpgt])
                ga, bga = gar.next()
                P.act(ga[:, 0:N], pgt[:, 0:N], AF.Sigmoid, [bpgt], [bga])
                P.tt('dve', mg[:, nn, 0:N], py[:, 0:N], ga[:, 0:N], ALU.mult, [bpy, bga], [bmg])
            P.dma('sp', mixv[:, :, tok0:tok0 + N], mg[:, :, 0:N], bmg, False)
        P.barrier()


def hgrn_load(k, es, l, w_in_l, lb_logits, norm_g, w_ho):
    P, nc = k.P, k.nc
    assert DEPTH == 2
    if True:
        def sb(name, shape, dt):
            return es.enter_context(nc.sbuf_tensor(U(name), shape, dt))
        Whq = sb('Whq', [128, 8, 512], BF16); bWhq = P.buf('Whq')
        Whf = sb('Whf', [128, 8, 512], BF16); bWhf = P.buf('Whf')
        Whi = sb('Whi', [128, 8, 512], BF16); bWhi = P.buf('Whi')
        Whg = sb('Whg', [128, 8, 512], BF16); bWhg = P.buf('Whg')
        Wgc = sb('Wgc', [128, 8, 1024], BF16); bWgc = P.buf('Wgc')
        Who = sb('Who', [128, 4, 1024], BF16); bWho = P.buf('Who')
        load_w_cast(P, Whq, bWhq, w_in_l, 8, 512, O_HQ)
        load_w_cast(P, Whf, bWhf, w_in_l, 8, 512, O_HF)
        load_w_cast(P, Whi, bWhi, w_in_l, 8, 512, O_HI)
        load_w_cast(P, Whg, bWhg, w_in_l, 8, 512, O_HG)
        load_w_cast(P, Wgc, bWgc, w_in_l, 8, 1024, O_GC)
        load_w_cast(P, Who, bWho, w_ho, 4, 1024, 0)
        omlb = sb('omlb', [64, 512], F32); bomlb = P.buf('omlb')
        ocol = sb('ocol', [128, 8], F32); bocol = P.buf('ocol')
        P.dma('sp', ocol[:, 4:8], norm_g.rearrange("(h p) -> p h", p=128), bocol, True, allow_slow_non_contiguous=True)
        P.ts('dve', ocol[:, 4:8], ocol[:, 4:8], 0.125, None, ALU.mult, None, [bocol], [bocol])
        if l == 0:
            P.memset('pool', omlb[:], 0.5, [bomlb])
            P.memset('pool', ocol[:, 0:4], 0.5, [bocol])
        else:
            lt = sb('lbt', [64, 2, 512], F32); blt = P.buf('lbt')
            lc = sb('lbc', [128, 2, 4], F32); blc = P.buf('lbc')
            for r in range(2):
                P.dma('sp', lt[:, r, :], lb_logits[r:r + 1, :].to_broadcast([64, 512]), blt, True)
                P.dma('sp', lc[:, r, :], lb_logits[r].rearrange("(h p) -> p h", p=128), blc, True, allow_slow_non_contiguous=True)
            P.tt('dve', lt[:, 0, :], lt[:, 0, :], lt[:, 1, :], ALU.subtract, [blt], [blt])
            P.act(omlb[:], lt[:, 0, :], AF.Sigmoid, [blt], [bomlb])
            P.ts('dve', omlb[:], omlb[:], 0.5, None, ALU.mult, None, [bomlb], [bomlb])
            P.tt('dve', lc[:, 0, :], lc[:, 0, :], lc[:, 1, :], ALU.subtract, [blc], [blc])
            P.act(ocol[:, 0:4], lc[:, 0, :], AF.Sigmoid, [blc], [bocol])
            P.ts('dve', ocol[:, 0:4], ocol[:, 0:4], 0.5, None, ALU.mult, None, [bocol], [bocol])
    return dict(Whq=Whq, bWhq=bWhq, Whf=Whf, bWhf=bWhf, Whi=Whi, bWhi=bWhi, Whg=Whg, bWhg=bWhg, Wgc=Wgc, bWgc=bWgc,
                Who=Who, bWho=bWho, omlb=omlb, bomlb=bomlb, ocol=ocol, bocol=bocol)


def phase_hgrn(k, w, hT, mixc, groups):
    P, nc = k.P, k.nc
    Whq, bWhq, Whf, bWhf, Whi, bWhi, Whg, bWhg = (w[n] for n in ('Whq', 'bWhq', 'Whf', 'bWhf', 'Whi', 'bWhi', 'Whg', 'bWhg'))
    Wgc, bWgc, Who, bWho, omlb, bomlb, ocol, bocol = (w[n] for n in ('Wgc', 'bWgc', 'Who', 'bWho', 'omlb', 'bomlb', 'ocol', 'bocol'))
    with ExitStack() as es:
        def sb(name, shape, dt):
            return es.enter_context(nc.sbuf_tensor(U(name), shape, dt))
        Lm = k.cst[0:64, C_L:C_L + 64]
        UMa = k.cst[0:64, C_UM:C_UM + 64]
        UMb = k.cst[0:64, C_UM + 64:C_UM + 66]
        tri8 = k.cst[0:64, C_TRI8:C_TRI8 + 256]

        S = sb('S', [128, HH, 128], F32); bS = P.buf('S')
        St = sb('St', [128, HH, 128], F32); bSt = P.buf('St')
        P.memset('pool', S[:], 0.0, [bS])
        GN = 256
        NC_ = GN // 64
        hgr = Rot(P, es, 'hTg', [128, 8, GN], BF16, 3)
        qTr = Rot(P, es, 'hqT', [128, HH, GN], BF16, 2)
        kTsr = Rot(P, es, 'hkT', [128, HH, GN], BF16, 2)
        sgTr = Rot(P, es, 'hsgT', [128, HH, GN], BF16, 2)
        lfr = Rot(P, es, 'lf', [64, NC_, 512], F32, 2)
        kdr = Rot(P, es, 'kd', [64, NC_, 512], BF16, 2)
        vr = Rot(P, es, 'hv', [64, NC_, 512], BF16, 2)
        qer = Rot(P, es, 'qe', [128, HH, GN], BF16, 2)
        ker = Rot(P, es, 'ke', [128, HH, GN], BF16, 2)
        ktmr = Rot(P, es, 'ktm', [64, 512], F32, 2)
        kclr = Rot(P, es, 'kcl', [64, 512], F32, 2)
        thir = Rot(P, es, 'thi', [64, 512], F32, 2)
        erbr = Rot(P, es, 'erb', [64, 512], BF16, 2)
        scr = Rot(P, es, 'hscr', [128, GN], F32, 4)
        thr = epr = emr = osqr = rsr = onr = scr
        exsr = Rot(P, es, 'exs', [128, NC_, 2, HH], F32, 2)
        ATr = Rot(P, es, 'ATa', [64, NC_, 256], BF16, 2)
        Mr = Rot(P, es, 'Ma', [128, NC_, 512], F32, 2)
        Sbr = Rot(P, es, 'Sba', [128, NC_, 512], BF16, 2)
        ogr = Rot(P, es, 'og', [128, HH, GN], BF16, 2)
        gar = Rot(P, es, 'gc', [128, GN], F32, 2)
        mgr = Rot(P, es, 'mixg', [128, 8, GN], BF16, 2)
        hTv = hT.rearrange("(kc p) t -> p kc t", p=128)
        mixv = mixc.rearrange("(kc p) t -> p kc t", p=128)

        hgroups = groups_of(k.T, 2)

        def load_hg(t0, n):
            hg, bhg = hgr.next()
            P.dma('sp', hg[:, :, 0:n * 128], hTv[:, :, t0 * 128:(t0 + n) * 128], bhg, True)
            return hg, bhg
        pf = Prefetch(hgroups, load_hg)

        def group_gen(gi, t0, n):
            N = n * 128
            nch = N // 64
            tok0 = t0 * 128
            hg, bhg = pf.get(gi)
            qT, bqT = qTr.next(); kTs, bkTs = kTsr.next(); sgT, bsgT = sgTr.next()
            lf_all, blf = lfr.next(); kd_all, bkd = kdr.next(); v_all, bv = vr.next()
            qe, bqe = qer.next(); ke, bke = ker.next(); exs, bexs = exsr.next()
            AT_all, bAT = ATr.next(); M_all, bM = Mr.next(); Sb_all, bSb = Sbr.next(); og, bog = ogr.next()
            br1 = BankRot(k, (0, 1, 2, 3))
            for h in range(HH):
                hsl = slice(h * 128, (h + 1) * 128)
                pb, bpb = br1.next()
                for kc in range(8):
                    P.mm(pb[:, 0:N], Whq[:, kc, hsl], hg[:, kc, 0:N], kc == 0, kc == 7, [bWhq, bhg], [bpb])
                P.cp('act', qT[:, h, 0:N], pb[:, 0:N], [bpb], [bqT])
                pb, bpb = br1.next()
                for kc in range(8):
                    P.mm(pb[:, 0:N], Whf[:, kc, hsl], hg[:, kc, 0:N], kc == 0, kc == 7, [bWhf, bhg], [bpb])
                th, bth = thr.next()
                P.act(th[:, 0:N], pb[:, 0:N], AF.Tanh, [bpb], [bth], scale=0.5)
                P.ts('dve', kTs[:, h, 0:N], th[:, 0:N], -1.0, 1.0, ALU.mult, ALU.add, [bth], [bkTs])
                pb, bpb = br1.next()
                for kc in range(8):
                    P.mm(pb[:, 0:N], Whg[:, kc, hsl], hg[:, kc, 0:N], kc == 0, kc == 7, [bWhg, bhg], [bpb])
                th, bth = thr.next()
                P.act(th[:, 0:N], pb[:, 0:N], AF.Tanh, [bpb], [bth], scale=0.5)
                P.stt(sgT[:, h, 0:N], th[:, 0:N], 1.0, pb[:, 0:N], ALU.add, ALU.mult, [bth, bpb], [bsgT])
            yield
            brf = BankRot(k, (4, 5))
            bri = BankRot(k, (6, 7))
            brr = BankRot(k, (2, 3))
            st2 = {}

            def s2_mm(c):
                pf, bpf = brf.next()
                pi, bpi = bri.next()
                for kc in range(8):
                    P.mm(pf[0:64, 0:512], hg[:, kc, c * 64:(c + 1) * 64], Whf[:, kc, :], kc == 0, kc == 7, [bhg, bWhf], [bpf])
                for kc in range(8):
                    P.mm(pi[0:64, 0:512], hg[:, kc, c * 64:(c + 1) * 64], Whi[:, kc, :], kc == 0, kc == 7, [bhg, bWhi], [bpi])
                ktm, bktm = ktmr.next()
                thi, bthi = thir.next()
                P.act(ktm[:], pf[0:64, 0:512], AF.Tanh, [bpf], [bktm], scale=0.5)
                P.act(thi[:], pi[0:64, 0:512], AF.Tanh, [bpi], [bthi], scale=0.5)
                P.ts('dve', ktm[:], ktm[:], -1.0, 1.0, ALU.mult, ALU.add, [bktm], [bktm])
                P.tt('dve', ktm[:], ktm[:], omlb[:], ALU.mult, [bktm, bomlb], [bktm])
                P.stt(v_all[:, c, :], thi[:], 1.0, pi[0:64, 0:512], ALU.add, ALU.mult, [bthi, bpi], [bv])
                kcl, bkcl = kclr.next()
                P.ts('dve', kcl[:], ktm[:], CLAMP, None, ALU.min, None, [bktm], [bkcl])
                st2[c] = (ktm, bktm, kcl, bkcl)

            def s2_ln(c):
                ktm, bktm, kcl, bkcl = st2[c]
                P.act(lf_all[:, c, :], kcl[:], AF.Ln, [bkcl], [blf], scale=-1.0, bias=1.0)

            def s2_back(c):
                ktm, bktm, kcl, bkcl = st2.pop(c)
                prb, bprb = brr.next()
                P.mm(prb[0:64, 0:512], Lm, lf_all[:, c, :], True, True, [k.bcst, blf], [bprb])
                erb, berb = erbr.next()
                P.act(erb[:], prb[0:64, 0:512], AF.Exp, [bprb], [berb])
                P.tt('dve', kd_all[:, c, :], ktm[:], erb[:], ALU.mult, [bktm, berb], [bkd])

            for c2 in range(0, nch, 2):
                s2_mm(c2)
                s2_mm(c2 + 1)
                s2_ln(c2)
                s2_ln(c2 + 1)
                s2_back(c2)
                s2_back(c2 + 1)
            yield
            br3 = BankRot(k, (0, 1))
            exv = exs[:].rearrange("p c j h -> p h c j")
            for h in range(HH):
                pbm, bpbm = br3.next()
                pex, bpex = brr.next()
                for c in range(nch):
                    P.mm(pbm[:, c * 64:(c + 1) * 64], lf_all[:, c, h * 128:(h + 1) * 128], UMa, True, True, [blf, k.bcst], [bpbm],
                         inc=(c == nch - 1))
                for c in range(nch):
                    P.mm(pex[:, c * 2:c * 2 + 2], lf_all[:, c, h * 128:(h + 1) * 128], UMb, True, True,
                         [blf, k.bcst], [bpex], inc=(c == nch - 1))
                ep, bep = epr.next()
                em, bem = emr.next()
                P.act(ep[:, 0:N], pbm[:, 0:N], AF.Exp, [bpbm], [bep])
                P.act(em[:, 0:N], pbm[:, 0:N], AF.Exp, [bpbm], [bem], scale=-1.0)
                P.tt('dve', qe[:, h, 0:N], qT[:, h, 0:N], ep[:, 0:N], ALU.mult, [bqT, bep], [bqe])
                P.stt(ke[:, h, 0:N], kTs[:, h, 0:N], ocol[:, h:h + 1], em[:, 0:N], ALU.mult, ALU.mult, [bkTs, bocol, bem], [bke])
                P.act(exv[:, h, 0:nch, :], pex[:, 0:nch * 2].rearrange("p (c j) -> p c j", j=2), AF.Exp, [bpex], [bexs])
            yield
            brA = BankRot(k, (4, 5))
            brM = BankRot(k, (6, 7))
            for c in range(nch):
                cs_ = slice(c * 64, (c + 1) * 64)
                pA, bpA = brA.next()
                for h in range(HH):
                    P.mm(pA[0:64, h * 64:(h + 1) * 64], ke[:, h, cs_], qe[:, h, cs_], True, True, [bke, bqe], [bpA], inc=(h == HH - 1))
                P.tt('dve', AT_all[:, c, :], pA[0:64, 0:256], tri8, ALU.mult, [bpA, k.bcst], [bAT])
                pM, bpM_ = brM.next()
                for h in range(HH):
                    hs = slice(h * 128, (h + 1) * 128)
                    P.mm(pM[:, hs], kd_all[:, c, hs], v_all[:, c, hs], True, True, [bkd, bv], [bpM_], inc=(h == HH - 1))
                P.cp('act', M_all[:, c, :], pM[:, 0:512], [bpM_], [bM])
            yield
            S3 = S[:]
            for c in range(nch):
                e_mid = exs[:, c, 0, :].unsqueeze(2).to_broadcast([128, HH, 128])
                e_last = exs[:, c, 1, :].unsqueeze(2).to_broadcast([128, HH, 128])
                P.tt('pool', Sb_all[:, c, :].rearrange("p (h d) -> p h d", h=HH), S3, e_mid, ALU.mult, [bS, bexs], [bSb])
                P.tt('dve', St[:], S3, e_last, ALU.mult, [bS, bexs], [bSt])
                P.tt('dve', S3, St[:], M_all[:, c, :].rearrange("p (h d) -> p h d", h=HH), ALU.add, [bSt, bM], [bS])
            yield
            bpo = [k.bank(h) for h in range(HH)]
            for c in range(nch):
                cs_ = slice(c * 64, (c + 1) * 64)
                for h in range(HH):
                    po, bpo_h = bpo[h]
                    hs = slice(h * 128, (h + 1) * 128)
                    P.mm(po[:, cs_], v_all[:, c, hs], AT_all[:, c, h * 64:(h + 1) * 64], True, False, [bv, bAT], [bpo_h], inc=False)
                    P.mm(po[:, cs_], Sb_all[:, c, hs], qe[:, h, cs_], False, True, [bSb, bqe], [bpo_h], inc=True)
            for h in range(HH):
                po, bpo_h = bpo[h]
                osq, bosq = osqr.next()
                P.act(osq[:, 0:N], po[:, 0:N], AF.Square, [bpo_h], [bosq])
                pms, bpms = brA.next()
                P.mm(pms[:, 0:N], k.onesf, osq[:, 0:N], True, True, [k.bcst, bosq], [bpms])
                rs, brs = rsr.next()
                P.act(rs[:, 0:N], pms[:, 0:N], AF.Ln, [bpms], [brs], scale=0.25 / 128, bias=EPS)
                P.act(rs[:, 0:N], rs[:, 0:N], AF.Exp, [brs], [brs], scale=-0.5)
                on, bon = onr.next()
                P.tt('dve', on[:, 0:N], po[:, 0:N], rs[:, 0:N], ALU.mult, [bpo_h, brs], [bon])
                P.stt(og[:, h, 0:N], on[:, 0:N], ocol[:, 4 + h:5 + h], sgT[:, h, 0:N], ALU.mult, ALU.mult, [bon, bocol, bsgT], [bog])
            yield
            mg, bmg = mgr.next()
            br6 = BankRot(k, (4, 5, 6, 7))
            for nn in range(8):
                py, bpy = br6.next()
                pgt, bpgt = br6.next()
                for h in range(HH):
                    P.mm(py[:, 0:N], Who[:, h, nn * 128:(nn + 1) * 128], og[:, h, 0:N], h == 0, h == HH - 1, [bWho, bog], [bpy])
                for kc in range(8):
                    P.mm(pgt[:, 0:N], Wgc[:, kc, nn * 128:(nn + 1) * 128], hg[:, kc, 0:N], kc == 0, kc == 7, [bWgc, bhg], [bpgt])
                ga, bga = gar.next()
                P.act(ga[:, 0:N], pgt[:, 0:N], AF.Tanh, [bpgt], [bga], scale=0.5)
                P.stt(mg[:, nn, 0:N], ga[:, 0:N], 1.0, py[:, 0:N], ALU.add, ALU.mult, [bga, bpy], [bmg])
            P.dma('sp', mixv[:, :, tok0:tok0 + N], mg[:, :, 0:N], bmg, False)

        interleave((group_gen(gi, t0, n) for gi, (t0, n) in enumerate(hgroups)), 2)
        P.barrier()


def phase_merge(k, l, xres, hT, mixa, mixb, mixc, w_out_l, norm2_g, groups, after_loads=None):
    P, nc = k.P, k.nc
    with ExitStack() as es:
        def sb(name, shape, dt):
            return es.enter_context(nc.sbuf_tensor(U(name), shape, dt))
        Wo = sb('Wo', [128, 8, 1024], BF16); bWo = P.buf('Wo')
        load_w_cast(P, Wo, bWo, w_out_l, 8, 1024, 0)
        gbc = sb('g2bc', [128, D], F32); bg = P.buf('g2bc')
        P.dma('sp', gbc[:], norm2_g[l:l + 1, :].to_broadcast([128, D]), bg, True)
        if after_loads is not None:
            after_loads()
        groups = groups_of(k.T, 2)
        mar = Rot(P, es, 'ma', [128, 8, 256], BF16, 3)
        mbr = Rot(P, es, 'mb', [128, 8, 256], BF16, 3)
        mcr = Rot(P, es, 'mc', [128, 8, 256], BF16, 3)
        tmp = sb('mtmp', [128, 8, 256], F32); btmp = P.buf('mtmp')
        mxr = Rot(P, es, 'mx', [128, 8, 256], BF16, 2)
        xr = Rot(P, es, 'x', [128, D], F32, 6)
        x1r = Rot(P, es, 'x1', [128, D], F32, 4)
        hgr = Rot(P, es, 'h2g', [128, 8, 256], BF16, 2)
        nt = NormTools(k, es, banks=(0, 1))
        hTv = hT.rearrange("(kc p) t -> p kc t", p=128)
        views = [m.rearrange("(kc p) t -> p kc t", p=128) for m in (mixa, mixb, mixc)]
        pair = [0]

        def load_mix(t0, n):
            N = n * 128
            tok0 = t0 * 128
            ma, bma = mar.next()
            mb, bmb = mbr.next()
            mc, bmc = mcr.next()
            for (mt, bm, v) in ((ma, bma, views[0]), (mb, bmb, views[1]), (mc, bmc, views[2])):
                P.dma('sp', mt[:, :, 0:N], v[:, :, tok0:tok0 + N], bm, True)
            return ma, bma, mb, bmb, mc, bmc
        pf = Prefetch(groups, load_mix)

        def group_prep(t0, n):
            N = n * 128
            tok0 = t0 * 128
            ma, bma, mb, bmb, mc, bmc = pf.get(groups.index((t0, n)))
            P.tt('dve', tmp[:, :, 0:N], ma[:, :, 0:N], mb[:, :, 0:N], ALU.add, [bma, bmb], [btmp])
            mx, bmx = mxr.next()
            P.tt('dve', mx[:, :, 0:N], tmp[:, :, 0:N], mc[:, :, 0:N], ALU.add, [btmp, bmc], [bmx])
            return mx, bmx

        def load_x(t):
            xt, bx = xr.next()
            P.dma('sp', xt[:], k.xsrc(l, t), bx, True)
            return xt, bx
        pfx = Prefetch([(t,) for t in range(k.T)], load_x, ahead=2)

        def tile_gen(t0, n, j, gs):
            t = t0 + j
            if j == 0:
                gs['mx'] = group_prep(t0, n)
                gs['hg'] = hgr.next()
            mx, bmx = gs['mx']
            hg, bhg = gs['hg']
            xt, bx = pfx.get(t)
            pi_ = 1 + (pair[0] % 3)
            pair[0] += 1
            pd = k.PS[pi_]
            (p0, bp0), (p1, bp1) = k.bank(2 * pi_), k.bank(2 * pi_ + 1)
            for kc in range(8):
                P.mm(p0[:, 0:512], mx[:, kc, j * 128:(j + 1) * 128], Wo[:, kc, 0:512], kc == 0, kc == 7, [bmx, bWo], [bp0])
            for kc in range(8):
                P.mm(p1[:, 0:512], mx[:, kc, j * 128:(j + 1) * 128], Wo[:, kc, 512:1024], kc == 0, kc == 7, [bmx, bWo], [bp1])
            yield
            x1, bx1 = x1r.next()
            P.tt('dve', x1[:], pd[:, 0:1024], xt[:], ALU.add, [bp0, bp1, bx], [bx1])
            if t == 0:
                P.dma('sp', xres[PAD:128, :], x1[PAD:128, :], bx1, False)
            else:
                P.dma('sp', xres[t * 128:(t + 1) * 128, :], x1[:], bx1, False)
            yield from nt.norm_tile_gen(x1, bx1, gbc, bg, hg, bhg, j)
            if j == n - 1:
                P.dma('sp', hTv[:, :, t0 * 128:(t0 + n) * 128], hg[:, :, 0:n * 128], bhg, False)

        def all_tiles():
            for (t0, n) in groups:
                gs = {}
                for j in range(n):
                    yield tile_gen(t0, n, j, gs)
        interleave(all_tiles(), 3)
        P.barrier()


def ffn_load_w1(k, es, w1):
    P, nc = k.P, k.nc
    W1 = es.enter_context(nc.sbuf_tensor(U('W1'), [128, 8, 4096], BF16))
    bW1q = [P.buf('W1q%d' % q) for q in range(4)]
    w1v = w1.rearrange("(kc p) n -> p kc n", p=128)

    def issue():
        for q in range(4):
            P.dma('pool', W1[:, :, q * 1024:(q + 1) * 1024], w1v[:, :, q * 1024:(q + 1) * 1024], bW1q[q], True)
    return W1, bW1q, issue


def phase_ffn(k, l, xres, hT, out, w1pre, w2, last):
    P, nc = k.P, k.nc
    T = k.T
    W1, bW1q, _ = w1pre
    with ExitStack() as es:
        def sb(name, shape, dt):
            return es.enter_context(nc.sbuf_tensor(U(name), shape, dt))
        W2 = sb('W2', [128, 32, 1024], BF16); bW2 = P.buf('W2')
        load_w_cast(P, W2, bW2, w2, 32, 1024, 0)
        hgr = Rot(P, es, 'h2g', [128, 8, 256], BF16, 3)
        fTr = Rot(P, es, 'fT', [128, 32, 256], BF16, 2)
        rr = Rot(P, es, 'frl', [128, 256], F32, 3)
        xr = Rot(P, es, 'x1', [128, D], F32, 2)
        xor_ = Rot(P, es, 'xo', [128, D], F32, 2)
        hTv = hT.rearrange("(kc p) t -> p kc t", p=128)
        br = BankRot(k, (0, 1, 2, 3))
        pair = [0]
        fgroups = [(t0, n) for (t0, n) in groups_of(T, 2) if not (last and t0 == 0)]

        def load_hg(t0, n):
            hg, bhg = hgr.next()
            P.dma('sp', hg[:, :, 0:n * 128], hTv[:, :, t0 * 128:(t0 + n) * 128], bhg, True)
            return hg, bhg
        pf = Prefetch(fgroups, load_hg)

        def ffn1(t0, n):
            N = n * 128
            tok0 = t0 * 128
            hg, bhg = pf.get(fgroups.index((t0, n)))
            fT, bfT = fTr.next()
            for fc in range(32):
                pb, bpb = br.next()
                for kc in range(8):
                    P.mm(pb[:, 0:N], W1[:, kc, fc * 128:(fc + 1) * 128], hg[:, kc, 0:N], kc == 0, kc == 7, [bW1q[fc // 8], bhg], [bpb])
                r, brl = rr.next()
                P.act(r[:, 0:N], pb[:, 0:N], AF.Relu, [bpb], [brl])
                P.tt('dve', fT[:, fc, 0:N], r[:, 0:N], r[:, 0:N], ALU.mult, [brl], [bfT])
            return fT, bfT

        def ffn2(t0, n, fT, bfT):
            for j in range(n):
                t = t0 + j
                xt, bx = xr.next()
                P.dma('sp', xt[:], xres[t * 128:(t + 1) * 128, :], bx, True)
                pi_ = 2 + (pair[0] % 2)
                pair[0] += 1
                pd = k.PS[pi_]
                (p0, bp0), (p1, bp1) = k.bank(2 * pi_), k.bank(2 * pi_ + 1)
                for fc in range(32):
                    P.mm(p0[:, 0:512], fT[:, fc, j * 128:(j + 1) * 128], W2[:, fc, 0:512], fc == 0, fc == 31, [bfT, bW2], [bp0])
                    P.mm(p1[:, 0:512], fT[:, fc, j * 128:(j + 1) * 128], W2[:, fc, 512:1024], fc == 0, fc == 31, [bfT, bW2], [bp1])
                xo, bxo = xor_.next()
                P.tt('dve', xo[:], pd[:, 0:1024], xt[:], ALU.add, [bp0, bp1, bx], [bxo])
                if last:
                    P.dma('sp', out[(t - 1) * 128:t * 128, :], xo[:], bxo, False)
                elif t == 0:
                    P.dma('sp', xres[PAD:128, :], xo[PAD:128, :], bxo, False)
                else:
                    P.dma('sp', xres[t * 128:(t + 1) * 128, :], xo[:], bxo, False)

        prev = None
        for (t0, n) in fgroups:
            cur = (t0, n) + ffn1(t0, n)
            if prev is not None:
                ffn2(*prev)
            prev = cur
        ffn2(*prev)
        P.barrier()

C_IDENT = 0
C_ONES = 128
C_TRI = 256
C_L = 384
C_UM = 448
C_PADB = 514
C_TRI8 = 515
NCONST = C_TRI8 + 512


def make_consts():
    c = np.zeros((128, NCONST), np.float32)
    c[:, C_IDENT:C_IDENT + 128] = np.eye(128, dtype=np.float32)
    c[:, C_ONES:C_ONES + 128] = 1.0
    s = np.arange(128)[:, None]
    t = np.arange(128)[None, :]
    c[:, C_TRI:C_TRI + 128] = (s <= t).astype(np.float32)
    s64 = np.arange(64)[:, None]
    t64 = np.arange(64)[None, :]
    c[:64, C_L:C_L + 64] = (s64 > t64).astype(np.float32)
    c[:64, C_UM:C_UM + 64] = (s64 <= t64).astype(np.float32) - (s64 <= 31).astype(np.float32)
    c[:64, C_UM + 64] = (np.arange(64) <= 31).astype(np.float32)
    c[:64, C_UM + 65] = 1.0
    c[:PAD, C_PADB] = -30000.0
    c[:64, C_TRI8:C_TRI8 + 512] = np.tile((s64 <= t64).astype(np.float32), (1, 8))
    return c


def make_cs_tab(T):
    half = 16
    pos = (np.arange(T * 128, dtype=np.float32) - np.float32(PAD)).astype(np.float32)
    inv_freq = (np.float32(10000.0) ** (-np.arange(half, dtype=np.float32) / np.float32(half))).astype(np.float32)
    ang = (pos[:, None] * inv_freq[None, :]).astype(np.float32)
    cs = np.concatenate([np.cos(ang), np.sin(ang)], axis=1).astype(np.float32)
    return np.ascontiguousarray(cs.reshape(T, 128, 32).transpose(1, 0, 2).reshape(128, T * 32))


_W_NAMES = ['meta', 'norm1_g', 'w_in', 'conv_w', 'conv_b', 'conv_ln_g', 'conv_ln_b', 'w_conv_out',
            'q_a_norm_g', 'w_uq', 'kv_a_norm_g', 'w_ukv', 'q_norm_g', 'k_norm_g', 'w_attn_out',
            'hgrn_lb_logits', 'hgrn_norm_g', 'w_hgrn_out', 'w_out', 'norm2_g', 'w_ff1', 'w_ff2']


def kernel(**inputs):
    x = np.ascontiguousarray(inputs['x'], dtype=np.float32)
    B, SEQ, _ = x.shape
    T = 1 + SEQ // 128
    nc = build(T)
    shared = {n: np.ascontiguousarray(inputs[n], dtype=np.float32) for n in _W_NAMES}
    shared['consts'] = make_consts()
    shared['cs_tab'] = make_cs_tab(T)
    in_maps = []
    for b in range(B):
        m = dict(shared)
        m['x'] = x[b]
        in_maps.append(m)
    res = run_bass_kernel_spmd(nc, in_maps, core_ids=list(range(B)))
    return np.stack([np.asarray(r['out']) for r in res.results], axis=0).astype(np.float32)
```

```python
import numpy as np
import concourse.bass as bass
import concourse.mybir as mybir
from concourse.bass_utils import run_bass_kernel_spmd
from contextlib import ExitStack

F32 = mybir.dt.float32
BF16 = mybir.dt.bfloat16
ALU = mybir.AluOpType
AF = mybir.ActivationFunctionType
AX = mybir.AxisListType

D = 1024
DEPTH = 2
N_META = 16
PAD = 112
EPS = 1e-6
CLAMP = 1.0 - 1e-6
N_IN = 6560
O_CONV, O_CQ, O_CKV, O_KR, O_HQ, O_HF, O_HI, O_HG, O_GA, O_GB, O_GC = (
    0, 1024, 1280, 1408, 1440, 1952, 2464, 2976, 3488, 4512, 5536)
CONV_K = 31
NH = 8
QK = 96
HH = 4

CENGS = ['pe', 'act', 'dve', 'pool']
ENGS = CENGS + ['sp']


class Buf:
    __slots__ = ('name', 'last_w', 'readers', 'dsem', 'dcnt')

    def __init__(self, name):
        self.name = name
        self.last_w = None
        self.readers = {}
        self.dsem = None
        self.dcnt = 0


class Prog:
    def __init__(self, nc, es):
        self.nc = nc
        self.es = es
        self.ops = {e: [] for e in ENGS}
        self.cnt = {e: 0 for e in CENGS}
        self.sem = {e: es.enter_context(nc.semaphore('c_' + e)) for e in CENGS}
        self.known = {e: {e2: 0 for e2 in CENGS} for e in ENGS}
        self.dknown = {e: {} for e in ENGS}
        self.dbufs = []
        self.dsem_free = {'sp': [], 'pool': []}
        self.dsem_q = {}
        self.dsem_tot = {}
        self.nbuf = 0

    def buf(self, name=None):
        self.nbuf += 1
        return Buf('%s_%d' % (name or 'b', self.nbuf))

    def _wait_eng(self, e, e2, idx):
        if e == 'pe' and e2 == 'pe':
            return
        if self.known[e][e2] >= idx:
            return
        self.known[e][e2] = idx
        sem = self.sem[e2]
        self.ops[e].append(lambda eng: eng.wait_ge(sem, idx))

    def _wait_dma(self, e, b):
        if b.dcnt == 0:
            return
        if self.dknown[e].get(b, 0) >= b.dcnt:
            return
        self.dknown[e][b] = b.dcnt
        sem, val = b.dsem, 16 * self.dsem_tot[id(b.dsem)]
        self.ops[e].append(lambda eng: eng.wait_ge(sem, val))

    def _deps(self, e, reads, writes):
        for b in reads:
            self._wait_dma(e, b)
            if b.last_w is not None:
                self._wait_eng(e, *b.last_w)
        for b in writes:
            self._wait_dma(e, b)
            if b.last_w is not None:
                self._wait_eng(e, *b.last_w)
            for e2, idx in b.readers.items():
                self._wait_eng(e, e2, idx)

    def op(self, e, name, reads=(), writes=(), inc=True, **kw):
        self._deps(e, reads, writes)
        if inc:
            self.cnt[e] += 1
            idx = self.cnt[e]
            sem = self.sem[e]
            self.ops[e].append(lambda eng: getattr(eng, name)(**kw).then_inc(sem, 1))
        else:
            idx = self.cnt[e] + 1
            self.ops[e].append(lambda eng: getattr(eng, name)(**kw))
        for b in writes:
            b.last_w = (e, idx)
            b.readers = {}
        for b in reads:
            b.readers[e] = idx

    def dma(self, e, out, in_, b, load, **kw):
        if b.dsem is None:
            if self.dsem_free[e]:
                b.dsem = self.dsem_free[e].pop()
            else:
                b.dsem = self.es.enter_context(self.nc.semaphore('d%d' % len(self.dsem_tot)))
                self.dsem_tot[id(b.dsem)] = 0
            b.dcnt = 0
            self.dsem_q[id(b.dsem)] = e
            self.dbufs.append(b)
        assert self.dsem_q[id(b.dsem)] == e, 'a DMA semaphore must stay on one queue'
        if load:
            self._deps(e, (), (b,))
        else:
            self._deps(e, (b,), ())
        b.dcnt += 1
        self.dsem_tot[id(b.dsem)] += 1
        sem = b.dsem
        self.ops[e].append(lambda eng: eng.dma_start(out=out, in_=in_, **kw).then_inc(sem, 16))
        if load:
            b.last_w = None
            b.readers = {}

    def barrier(self):
        for e in ENGS:
            for e2 in CENGS:
                if e2 != e and self.cnt[e2] > 0:
                    self._wait_eng(e, e2, self.cnt[e2])
            for b in self.dbufs:
                self._wait_dma(e, b)
        for b in self.dbufs:
            if self.dsem_q[id(b.dsem)] == 'sp':
                self.dsem_free['sp'].append(b.dsem)
            b.dsem = None
            b.dcnt = 0
            for e in ENGS:
                self.dknown[e].pop(b, None)
        self.dbufs = []

    def mm(self, out, lhsT, rhs, start, stop, reads, writes, inc=None):
        self.op('pe', 'matmul', reads, writes, inc=(stop if inc is None else inc),
                out=out, lhsT=lhsT, rhs=rhs, start=start, stop=stop)

    def tr(self, out, in_, identity, reads, writes, inc=True):
        self.op('pe', 'transpose', reads, writes, inc=inc, out=out, in_=in_, identity=identity)

    def act(self, out, in_, func, reads, writes, **kw):
        self.op('act', 'activation', reads, writes, out=out, in_=in_, func=func, **kw)

    def tt(self, e, out, in0, in1, op, reads, writes):
        self.op(e, 'tensor_tensor', reads, writes, out=out, in0=in0, in1=in1, op=op)

    def ts(self, e, out, in0, s1, s2, op0, op1, reads, writes):
        if op1 is None:
            self.op(e, 'tensor_scalar', reads, writes, out=out, in0=in0, scalar1=s1, scalar2=None, op0=op0)
        else:
            self.op(e, 'tensor_scalar', reads, writes, out=out, in0=in0, scalar1=s1, scalar2=s2, op0=op0, op1=op1)

    def stt(self, out, in0, scalar, in1, op0, op1, reads, writes):
        self.op('dve', 'scalar_tensor_tensor', reads, writes, out=out, in0=in0, scalar=scalar, in1=in1, op0=op0, op1=op1)

    def cp(self, e, out, in_, reads, writes):
        if e == 'act':
            self.op('act', 'copy', reads, writes, out=out, in_=in_)
        else:
            self.op(e, 'tensor_copy', reads, writes, out=out, in_=in_)

    def rsqrt(self, out, in_, scale, eps, reads, writes):
        self.act(out, in_, AF.Sqrt, reads, writes, scale=scale, bias=self.eps_ap(eps, out))
        self.op('dve', 'reciprocal', list(writes), list(writes), out=out, in_=out)

    def eps_ap(self, eps, like):
        return eps

    def memset(self, e, ap, val, writes):
        self.op(e, 'memset', (), writes, ap=ap, constant=val)

    def emit(self):
        self.barrier()
        nc = self.nc
        with nc.Block() as block:
            @block.tensor
            def _(eng):
                for f in self.ops['pe']:
                    f(eng)

            @block.scalar
            def _(eng):
                for f in self.ops['act']:
                    f(eng)

            @block.vector
            def _(eng):
                for f in self.ops['dve']:
                    f(eng)

            @block.gpsimd
            def _(eng):
                for f in self.ops['pool']:
                    f(eng)

            @block.sync
            def _(eng):
                for f in self.ops['sp']:
                    f(eng)


_UID = [0]


def U(name):
    _UID[0] += 1
    return '%s_u%d' % (name, _UID[0])


class Rot:
    def __init__(self, P, es, name, shape, dtype, n):
        self.tiles = [es.enter_context(P.nc.sbuf_tensor(U('%s%d' % (name, i)), shape, dtype)) for i in range(n)]
        self.bufs = [P.buf('%s%d' % (name, i)) for i in range(n)]
        self.i = 0

    def next(self):
        k = self.i % len(self.tiles)
        self.i += 1
        return self.tiles[k], self.bufs[k]


def groups_of(T, gsz=4):
    gs = [(0, 1)]
    t = 1
    while t < T:
        n = min(gsz, T - t)
        gs.append((t, n))
        t += n
    return gs


class K:
    pass


def load_w_cast(P, dst, dst_buf, src, K_chunks, ncols, col0=0, pk=128):
    v = src.rearrange("(kc p) n -> p kc n", p=pk)
    for k0 in range(0, K_chunks, 8):
        k1 = min(K_chunks, k0 + 8)
        c = 0
        while c < ncols:
            w = min(2048, ncols - c)
            P.dma('pool', dst[:, k0:k1, c:c + w], v[:, k0:k1, col0 + c:col0 + c + w], dst_buf, True)
            c += w


class Prefetch:
    def __init__(self, items, load_fn, ahead=1):
        self.items, self.load_fn, self.ahead = list(items), load_fn, ahead
        self.loaded = {}
        self.next = 0

    def get(self, i):
        while self.next < len(self.items) and self.next <= i + self.ahead:
            self.loaded[self.next] = self.load_fn(*self.items[self.next])
            self.next += 1
        return self.loaded.pop(i)


class BankRot:
    def __init__(self, k, banks):
        self.k, self.banks, self.i = k, list(banks), 0

    def next(self):
        b = self.banks[self.i % len(self.banks)]
        self.i += 1
        return self.k.bank(b)


def build(T, stop_after=None, debug=False, nlayers=DEPTH):
    SEQ = (T - 1) * 128
    TT = T * 128
    nc = bass.Bass("TRN2", target_bir_lowering=False)

    def din(name, shape):
        return nc.dram_tensor(name, list(shape), F32, kind="ExternalInput").ap()

    def dscr(name, shape, dt):
        if debug:
            return nc.dram_tensor(name, list(shape), dt, kind="ExternalOutput").ap()
        return nc.dram_tensor(name, list(shape), dt).ap()

    k = K()
    k.T, k.TT, k.SEQ = T, TT, SEQ
    x_in = din("x", (SEQ, D))
    meta = din("meta", (N_META, D))
    norm1_g = din("norm1_g", (DEPTH, D))
    w_in = din("w_in", (DEPTH, D, N_IN))
    conv_w = din("conv_w", (DEPTH, CONV_K, 512))
    conv_b = din("conv_b", (DEPTH, 512))
    conv_ln_g = din("conv_ln_g", (DEPTH, 512))
    conv_ln_b = din("conv_ln_b", (DEPTH, 512))
    w_conv_out = din("w_conv_out", (DEPTH, 512, D))
    q_a_norm_g = din("q_a_norm_g", (DEPTH, 256))
    w_uq = din("w_uq", (DEPTH, 256, 768))
    kv_a_norm_g = din("kv_a_norm_g", (DEPTH, 128))
    w_ukv = din("w_ukv", (DEPTH, 128, 1024))
    q_norm_g = din("q_norm_g", (DEPTH, 96))
    k_norm_g = din("k_norm_g", (DEPTH, 96))
    w_attn_out = din("w_attn_out", (DEPTH, 512, D))
    hgrn_lb_logits = din("hgrn_lb_logits", (DEPTH, 512))
    hgrn_norm_g = din("hgrn_norm_g", (DEPTH, 512))
    w_hgrn_out = din("w_hgrn_out", (DEPTH, 512, D))
    w_out = din("w_out", (DEPTH, D, D))
    norm2_g = din("norm2_g", (DEPTH, D))
    w_ff1 = din("w_ff1", (DEPTH, D, 4096))
    w_ff2 = din("w_ff2", (DEPTH, 4096, D))
    consts = din("consts", (128, NCONST))
    cs_tab = din("cs_tab", (128, T * 32))
    out = nc.dram_tensor("out", [SEQ, D], F32, kind="ExternalOutput").ap()

    xres = dscr("xres", (TT, D), F32)
    hT = dscr("hT", (D, TT), BF16)
    mixa = dscr("mixa", (D, TT), BF16)
    mixb = dscr("mixb", (D, TT), BF16)
    mixc = dscr("mixc", (D, TT), BF16)
    QT = dscr("QT", (NH, QK, TT), BF16)
    KT = dscr("KT", (NH, QK, TT), BF16)
    VA = dscr("VA", (TT, NH * 65), BF16)
    OT = dscr("OT", (NH, 64, TT), BF16)

    groups = groups_of(T)

    with ExitStack() as es0:
        P = Prog(nc, es0)
        k.P, k.nc = P, nc
        k.PS = [es0.enter_context(nc.psum_tensor('ps%d' % i, [128, 1024], F32)) for i in range(4)]
        k.PSB = [P.buf('psb%d' % i) for i in range(8)]

        def bank(i):
            return k.PS[i // 2][:, (i % 2) * 512:(i % 2) * 512 + 512], k.PSB[i]
        k.bank = bank
        cst = es0.enter_context(nc.sbuf_tensor('cst', [128, NCONST], F32))
        bcst = P.buf('cst')
        P.dma('sp', cst[:], consts, bcst, True)
        identb = es0.enter_context(nc.sbuf_tensor('identb', [128, 128], BF16))
        bidb = P.buf('identb')
        P.cp('dve', identb[:], cst[:, C_IDENT:C_IDENT + 128], [bcst], [bidb])
        k.cst, k.bcst, k.identb, k.bidb = cst, bcst, identb, bidb
        k.identf = cst[:, C_IDENT:C_IDENT + 128]
        k.onesf = cst[:, C_ONES:C_ONES + 128]

        with ExitStack() as es:
            rot = Rot(P, es, 'xi', [128, D], F32, 3)
            tz, bz = rot.next()
            P.memset('pool', tz[:], 0.0, [bz])
            P.dma('sp', tz[PAD:128, :], meta, bz, True)
            P.dma('sp', xres[0:128, :], tz[:], bz, False)
            P.barrier()

        def xsrc(l, t):
            if l == 0 and t >= 1:
                return x_in[(t - 1) * 128:t * 128, :]
            return xres[t * 128:(t + 1) * 128, :]
        k.xsrc = xsrc

        for l in range(nlayers):
            if stop_after == 'init':
                break
            last = (l == nlayers - 1)
            es_wc = ExitStack()
            wconv = conv_load(k, es_wc, w_in[l], conv_w[l], conv_b[l], conv_ln_g[l], conv_ln_b[l], w_conv_out[l])
            phase_norm1(k, l, xres, hT, norm1_g, groups)
            if stop_after == 'p0':
                es_wc.close()
                break
            phase_conv(k, wconv, hT, mixa, groups)
            es_wc.close()
            if stop_after == 'p1':
                break
            phase_mla_pre(k, l, hT, QT, KT, VA, w_in[l], q_a_norm_g[l], w_uq[l], kv_a_norm_g[l], w_ukv[l],
                          q_norm_g, k_norm_g, cs_tab, groups)
            if stop_after == 'p2a':
                break
            es_wh = ExitStack()
            whg = hgrn_load(k, es_wh, l, w_in[l], hgrn_lb_logits, hgrn_norm_g[l], w_hgrn_out[l])
            es_wa = ExitStack()
            wao = attn_out_load(k, es_wa, w_in[l], w_attn_out[l])
            phase_attn(k, l, QT, KT, VA, OT, groups)
            if stop_after == 'p2b':
                es_wa.close(); es_wh.close()
                break
            phase_attn_out(k, wao, hT, OT, mixb, groups)
            es_wa.close()
            if stop_after == 'p2c':
                es_wh.close()
                break
            phase_hgrn(k, whg, hT, mixc, groups)
            es_wh.close()
            if stop_after == 'p3':
                break
            es_w1 = ExitStack()
            w1pre = ffn_load_w1(k, es_w1, w_ff1[l])
            phase_merge(k, l, xres, hT, mixa, mixb, mixc, w_out[l], norm2_g, groups, after_loads=w1pre[2])
            if stop_after == 'p4a':
                es_w1.close()
                break
            phase_ffn(k, l, xres, hT, out, w1pre, w_ff2[l], last)
            es_w1.close()
        P.emit()
    k.P = P
    build.last_k = k
    return nc


def interleave(gens, depth):
    active = []
    it = iter(gens)
    more = True
    while True:
        while more and len(active) < depth:
            try:
                active.append(next(it))
            except StopIteration:
                more = False
        if not active:
            break
        for g in list(active):
            try:
                next(g)
            except StopIteration:
                active.remove(g)


class NormTools:
    def __init__(self, k, es, banks=(0, 1), nrot=4):
        self.k = k
        P = k.P
        self.junk = Rot(P, es, 'njunk', [128, D], BF16, 2)
        self.ss = Rot(P, es, 'nss', [128, 4], F32, nrot)
        self.hb = Rot(P, es, 'nhb', [128, D], BF16, nrot)
        self.br = BankRot(k, banks)

    def norm_tile_gen(self, xt, bx, gbc, bg, hg, bhg, j):
        k = self.k
        P = k.P
        jk, bj = self.junk.next()
        ss, bss = self.ss.next()
        hb, bhb = self.hb.next()
        P.act(jk[:], xt[:], AF.Square, [bx], [bj, bss], accum_out=ss[:, 0:1])
        P.rsqrt(ss[:, 2:3], ss[:, 0:1], 1.0 / D, EPS, [bss], [bss])
        P.stt(hb[:], xt[:], ss[:, 2:3], gbc[:], ALU.mult, ALU.mult, [bx, bss, bg], [bhb])
        yield
        pb, bpb = self.br.next()
        pbb = pb.bitcast(BF16)
        for kc in range(8):
            P.tr(pbb[:, kc * 128:(kc + 1) * 128], hb[:, kc * 128:(kc + 1) * 128], k.identb[:],
                 [bhb, k.bidb], [bpb], inc=(kc == 7))
        P.cp('act', hg[:, :, j * 128:(j + 1) * 128], pbb.rearrange("p (kc t) -> p kc t", kc=8), [bpb], [bhg])


def phase_norm1(k, l, xres, hT, norm1_g, groups):
    P, nc = k.P, k.nc
    with ExitStack() as es:
        gbc = es.enter_context(nc.sbuf_tensor(U('gbc'), [128, D], F32))
        bg = P.buf('gbc')
        P.dma('sp', gbc[:], norm1_g[l:l + 1, :].to_broadcast([128, D]), bg, True)
        xr = Rot(P, es, 'x', [128, D], F32, 6)
        hgr = Rot(P, es, 'hTg', [128, 8, 512], BF16, 3)
        nt = NormTools(k, es)
        hTv = hT.rearrange("(kc p) t -> p kc t", p=128)

        def load_x(t):
            xt, bx = xr.next()
            P.dma('sp', xt[:], k.xsrc(l, t), bx, True)
            return xt, bx
        pfx = Prefetch([(t,) for t in range(k.T)], load_x, ahead=2)

        def tile_gen(t0, n, j, hg, bhg):
            t = t0 + j
            xt, bx = pfx.get(t)
            yield from nt.norm_tile_gen(xt, bx, gbc, bg, hg, bhg, j)
            if j == n - 1:
                P.dma('sp', hTv[:, :, t0 * 128:(t0 + n) * 128], hg[:, :, 0:n * 128], bhg, False)

        def all_tiles():
            for (t0, n) in groups:
                hg, bhg = hgr.next()
                for j in range(n):
                    yield tile_gen(t0, n, j, hg, bhg)
        interleave(all_tiles(), 3)
        P.barrier()


def conv_load(k, es, w_in_l, conv_w, conv_b, ln_g, ln_b, w_co):
    P, nc = k.P, k.nc
    if True:
        def sb(name, shape, dt):
            return es.enter_context(nc.sbuf_tensor(U(name), shape, dt))
        Wc = sb('Wc', [128, 8, 1024], BF16); bWc = P.buf('Wc')
        Wga = sb('Wga', [128, 8, 1024], BF16); bWga = P.buf('Wga')
        Wco = sb('Wco', [128, 4, 1024], BF16); bWco = P.buf('Wco')
        load_w_cast(P, Wc, bWc, w_in_l, 8, 1024, O_CONV)
        load_w_cast(P, Wga, bWga, w_in_l, 8, 1024, O_GA)
        load_w_cast(P, Wco, bWco, w_co, 4, 1024, 0)
        cw31 = sb('cw31', [32, 512], F32); bcw31 = P.buf('cw31')
        P.dma('sp', cw31[0:CONV_K, :], conv_w, bcw31, True)
        cw = sb('cw', [128, 4, 32], F32); bcw = P.buf('cw')
        pb, bpb = k.bank(0)
        for cc in range(4):
            P.tr(pb[:, cc * 32:cc * 32 + CONV_K], cw31[0:CONV_K, cc * 128:(cc + 1) * 128],
                 k.identf[0:CONV_K, 0:CONV_K], [bcw31, k.bcst], [bpb], inc=(cc == 3))
        P.cp('dve', cw[:, :, 0:CONV_K], pb[:, 0:128].rearrange("p (c j) -> p c j", c=4)[:, :, 0:CONV_K], [bpb], [bcw])
        Dg = sb('Dg', [128, 4, CONV_K, 128], BF16); bDg = P.buf('Dg')
        for cc in range(4):
            P.tt('dve' if cc % 2 == 0 else 'pool', Dg[:, cc, :, :],
                 k.identf.unsqueeze(1).to_broadcast([128, CONV_K, 128]),
                 cw[:, cc, 0:CONV_K].unsqueeze(2).to_broadcast([128, CONV_K, 128]), ALU.mult, [k.bcst, bcw], [bDg])
        cols = sb('ccols', [128, 12], F32); bcols = P.buf('ccols')
        for i, src in enumerate((conv_b, ln_g, ln_b)):
            P.dma('sp', cols[:, i * 4:(i + 1) * 4], src.rearrange("(c p) -> p c", p=128), bcols, True,
                  allow_slow_non_contiguous=True)
    return dict(Wc=Wc, bWc=bWc, Wga=Wga, bWga=bWga, Wco=Wco, bWco=bWco, Dg=Dg, bDg=bDg, cols=cols, bcols=bcols)


def phase_conv(k, w, hT, mixa, groups):
    P, nc = k.P, k.nc
    Wc, bWc, Wga, bWga, Wco, bWco = w['Wc'], w['bWc'], w['Wga'], w['bWga'], w['Wco'], w['bWco']
    Dg, bDg, cols, bcols = w['Dg'], w['bDg'], w['cols'], w['bcols']
    with ExitStack() as es:
        hgr = Rot(P, es, 'hTg', [128, 8, 512], BF16, 3)
        hcr = Rot(P, es, 'hc', [128, 4, 30 + 512], BF16, 2)
        sgr = Rot(P, es, 'sg', [128, 512], F32, 2)
        cvr = Rot(P, es, 'cv', [128, 4, 512], F32, 2)
        sqr = Rot(P, es, 'sq', [128, 4, 512], F32, 2)
        str_ = Rot(P, es, 'st', [128, 4, 512], F32, 2)
        xcr = Rot(P, es, 'xc', [128, 512], F32, 2)
        actr = Rot(P, es, 'cact', [128, 4, 512], BF16, 2)
        gar = Rot(P, es, 'ga', [128, 512], F32, 2)
        mgr = Rot(P, es, 'mixg', [128, 8, 512], BF16, 2)
        hTv = hT.rearrange("(kc p) t -> p kc t", p=128)
        mixv = mixa.rearrange("(kc p) t -> p kc t", p=128)
        br = BankRot(k, range(8))
        prev = {}

        def load_hg(t0, n):
            hg, bhg = hgr.next()
            P.dma('sp', hg[:, :, 0:n * 128], hTv[:, :, t0 * 128:(t0 + n) * 128], bhg, True)
            return hg, bhg
        pf = Prefetch(groups, load_hg)

        def group_gen(gi, t0, n):
            N = n * 128
            tok0 = t0 * 128
            hg, bhg = pf.get(gi)
            hc, bhc = hcr.next()
            if prev:
                P.cp('pool', hc[:, :, 0:30], prev['hc'][:, :, prev['N']:prev['N'] + 30], [prev['bhc']], [bhc])
            else:
                P.memset('pool', hc[:, :, 0:30], 0.0, [bhc])
            for cc in range(4):
                pa, bpa = br.next()
                pg, bpg = br.next()
                for kc in range(8):
                    P.mm(pa[:, 0:N], Wc[:, kc, cc * 128:(cc + 1) * 128], hg[:, kc, 0:N], kc == 0, kc == 7, [bWc, bhg], [bpa])
                for kc in range(8):
                    P.mm(pg[:, 0:N], Wc[:, kc, 512 + cc * 128:512 + (cc + 1) * 128], hg[:, kc, 0:N], kc == 0, kc == 7, [bWc, bhg], [bpg])
                sg, bsg = sgr.next()
                P.act(sg[:, 0:N], pg[:, 0:N], AF.Sigmoid, [bpg], [bsg])
                P.tt('dve', hc[:, cc, 30:30 + N], pa[:, 0:N], sg[:, 0:N], ALU.mult, [bpa, bsg], [bhc])
            prev.update(hc=hc, bhc=bhc, N=N)
            yield
            cv, bcv = cvr.next()
            sq, bsq = sqr.next()
            for cc in range(4):
                pc, bpc = br.next()
                for j in range(CONV_K):
                    P.mm(pc[:, 0:N], Dg[:, cc, j, :], hc[:, cc, j:j + N], j == 0, j == CONV_K - 1, [bDg, bhc], [bpc])
                P.act(cv[:, cc, 0:N], pc[:, 0:N], AF.Identity, [bpc, bcols], [bcv], bias=cols[:, cc:cc + 1], scale=1.0)
                P.act(sq[:, cc, 0:N], pc[:, 0:N], AF.Square, [bpc, bcols], [bsq], bias=cols[:, cc:cc + 1], scale=1.0)
            yield
            st, bst = str_.next()
            pm, bpm = br.next()
            pq, bpq = br.next()
            for cc in range(4):
                P.mm(pm[:, 0:N], k.onesf, cv[:, cc, 0:N], cc == 0, cc == 3, [k.bcst, bcv], [bpm])
            for cc in range(4):
                P.mm(pq[:, 0:N], k.onesf, sq[:, cc, 0:N], cc == 0, cc == 3, [k.bcst, bsq], [bpq])
            P.act(st[:, 0, 0:N], pm[:, 0:N], AF.Copy, [bpm], [bst], scale=1.0 / 512)
            P.tt('pool', st[:, 1, 0:N], st[:, 0, 0:N], st[:, 0, 0:N], ALU.mult, [bst], [bst])
            P.stt(st[:, 2, 0:N], pq[:, 0:N], 1.0 / 512, st[:, 1, 0:N], ALU.mult, ALU.subtract, [bpq, bst], [bst])
            P.rsqrt(st[:, 3, 0:N], st[:, 2, 0:N], 1.0, EPS, [bst], [bst])
            yield
            act, bact = actr.next()
            for cc in range(4):
                xc, bxc = xcr.next()
                P.tt('pool', xc[:, 0:N], cv[:, cc, 0:N], st[:, 0, 0:N], ALU.subtract, [bcv, bst], [bxc])
                P.tt('dve', xc[:, 0:N], xc[:, 0:N], st[:, 3, 0:N], ALU.mult, [bxc, bst], [bxc])
                P.act(act[:, cc, 0:N], xc[:, 0:N], AF.Silu, [bxc, bcols], [bact],
                      bias=cols[:, 8 + cc:9 + cc], scale=cols[:, 4 + cc:5 + cc])
            yield
            mg, bmg = mgr.next()
            for nn in range(8):
                py, bpy = br.next()
                pgt, bpgt = br.next()
                for cc in range(4):
                    P.mm(py[:, 0:N], Wco[:, cc, nn * 128:(nn + 1) * 128], act[:, cc, 0:N], cc == 0, cc == 3, [bWco, bact], [bpy])
                for kc in range(8):
                    P.mm(pgt[:, 0:N], Wga[:, kc, nn * 128:(nn + 1) * 128], hg[:, kc, 0:N], kc == 0, kc == 7, [bWga, bhg], [bpgt])
                ga, bga = gar.next()
                P.act(ga[:, 0:N], pgt[:, 0:N], AF.Sigmoid, [bpgt], [bga])
                P.tt('dve', mg[:, nn, 0:N], py[:, 0:N], ga[:, 0:N], ALU.mult, [bpy, bga], [bmg])
            P.dma('sp', mixv[:, :, tok0:tok0 + N], mg[:, :, 0:N], bmg, False)

        interleave((group_gen(gi, t0, n) for gi, (t0, n) in enumerate(groups)), 2)
        P.barrier()

def phase_mla_pre(k, l, hT, QT, KT, VA, w_in_l, q_a_g, w_uq, kv_a_g, w_ukv, q_norm_g, k_norm_g, cs_tab, groups):
    P, nc = k.P, k.nc
    T = k.T
    with ExitStack() as es:
        def sb(name, shape, dt):
            return es.enter_context(nc.sbuf_tensor(U(name), shape, dt))
        Wm = sb('Wm', [128, 8, 416], BF16); bWm = P.buf('Wm')
        load_w_cast(P, Wm, bWm, w_in_l, 8, 416, O_CQ)
        wtmp = sb('wtmp', [128, 2, 1024], F32); bwt = P.buf('wtmp')
        gcol = sb('gcol', [128, 4], F32); bgc = P.buf('gcol')
        P.dma('sp', gcol[:, 0:2], q_a_g.rearrange("(c p) -> p c", p=128), bgc, True, allow_slow_non_contiguous=True)
        P.dma('sp', gcol[:, 2:3], kv_a_g.rearrange("(c p) -> p c", p=128), bgc, True, allow_slow_non_contiguous=True)
        Wuq = sb('Wuq', [128, 2, 768], BF16); bWuq = P.buf('Wuq')
        Wukv = sb('Wukv', [128, 1024], BF16); bWukv = P.buf('Wukv')
        P.dma('sp', wtmp[:, :, 0:768], w_uq.rearrange("(c p) n -> p c n", p=128), bwt, True)
        for c in range(2):
            P.ts('dve', Wuq[:, c, :], wtmp[:, c, 0:768], gcol[:, c:c + 1], None, ALU.mult, None, [bwt, bgc], [bWuq])
        P.dma('sp', wtmp[:, 0, :], w_ukv, bwt, True)
        P.ts('dve', Wukv[:], wtmp[:, 0, :], gcol[:, 2:3], None, ALU.mult, None, [bwt, bgc], [bWukv])
        gqk = sb('gqk', [128, 2, 96], F32); bgqk = P.buf('gqk')
        P.dma('sp', gqk[:, 0, :], q_norm_g[l:l + 1, :].to_broadcast([128, 96]), bgqk, True)
        P.dma('sp', gqk[:, 1, :], k_norm_g[l:l + 1, :].to_broadcast([128, 96]), bgqk, True)
        P.ts('dve', gqk[:, 0, :], gqk[:, 0, :], float(QK) ** -0.5, None, ALU.mult, None, [bgqk], [bgqk])
        cs = sb('cs', [128, T, 32], F32); bcs = P.buf('cs')
        P.dma('sp', cs[:], cs_tab.rearrange("p (t e) -> p t e", e=32), bcs, True)

        hgr = Rot(P, es, 'hTg', [128, 8, 512], BF16, 3)
        cTr = Rot(P, es, 'cT', [128, 3, 512], BF16, 2)
        junk = sb('mjunk', [128, 256], BF16); bjunk = P.buf('mjunk')
        smr = Rot(P, es, 'sm', [128, 48], F32, 4)
        krr_ = Rot(P, es, 'kr', [128, 4, 32], F32, 4)
        sqr = Rot(P, es, 'sqq', [128, 768], F32, 4)
        qnr = Rot(P, es, 'qn', [128, 8, 96], F32, 4)
        rtr = Rot(P, es, 'rt', [128, 4, 8, 16], F32, 4)
        qfr = Rot(P, es, 'qf', [128, 8, 96], BF16, 4)
        kfr = Rot(P, es, 'kf', [128, 8, 96], BF16, 4)
        knr = Rot(P, es, 'kn', [128, 8, 64], F32, 4)
        var_ = Rot(P, es, 'va', [128, 8, 65], BF16, 5)
        for t_, b_ in zip(var_.tiles, var_.bufs):
            P.memset('pool', t_[:], 1.0, [b_])
        QTg = Rot(P, es, 'QTg', [96, 8, 512], BF16, 2)
        KTg = Rot(P, es, 'KTg', [96, 8, 512], BF16, 2)
        hTv = hT.rearrange("(kc p) t -> p kc t", p=128)
        QTv = QT.rearrange("h d t -> d h t")
        KTv = KT.rearrange("h d t -> d h t")

        def load_hg(t0, n):
            hg, bhg = hgr.next()
            P.dma('sp', hg[:, :, 0:n * 128], hTv[:, :, t0 * 128:(t0 + n) * 128], bhg, True)
            return hg, bhg
        pf = Prefetch(groups, load_hg)

        def group_prep(t0, n):
            N = n * 128
            tok0 = t0 * 128
            hg, bhg = pf.get(groups.index((t0, n)))
            cT, bcT = cTr.next()
            pb, bpb = k.bank(0)
            for ch in range(3):
                for kc in range(8):
                    P.mm(pb[:, 0:N], Wm[:, kc, ch * 128:(ch + 1) * 128], hg[:, kc, 0:N], kc == 0, kc == 7, [bWm, bhg], [bpb])
                P.cp('act', cT[:, ch, 0:N], pb[:, 0:N], [bpb], [bcT])
            qtg, bqtg = QTg.next()
            ktg, bktg = KTg.next()
            return dict(N=N, tok0=tok0, hg=hg, bhg=bhg, cT=cT, bcT=bcT, qtg=qtg, bqtg=bqtg, ktg=ktg, bktg=bktg)

        def tile_gen(t0, n, j, gs):
            if j == 0:
                gs.update(group_prep(t0, n))
            N, tok0, hg, bhg, cT, bcT = gs['N'], gs['tok0'], gs['hg'], gs['bhg'], gs['cT'], gs['bcT']
            qtg, bqtg, ktg, bktg = gs['qtg'], gs['bqtg'], gs['ktg'], gs['bktg']
            t = t0 + j
            c0, c1 = j * 128, (j + 1) * 128
            sm, bsm = smr.next()
            kr, bkr = krr_.next()
            ptm, bptm = k.bank(1)
            for kc in range(8):
                P.mm(ptm[:, 0:416], hg[:, kc, c0:c1], Wm[:, kc, 0:416], kc == 0, kc == 7, [bhg, bWm], [bptm])
            P.act(junk[:, 0:256], ptm[:, 0:256], AF.Square, [bptm], [bjunk, bsm], accum_out=sm[:, 0:1])
            P.act(junk[:, 0:128], ptm[:, 256:384], AF.Square, [bptm], [bjunk, bsm], accum_out=sm[:, 1:2])
            P.act(junk[:, 0:32], ptm[:, 384:416], AF.Square, [bptm], [bjunk, bsm], accum_out=sm[:, 2:3])
            P.cp('act', kr[:, 0, :], ptm[:, 384:416], [bptm], [bkr])
            P.rsqrt(sm[:, 3:4], sm[:, 0:1], 1.0 / 256, EPS, [bsm], [bsm])
            P.rsqrt(sm[:, 4:5], sm[:, 1:2], 1.0 / 128, EPS, [bsm], [bsm])
            P.stt(sm[:, 5:6], sm[:, 3:4], 1.0 / QK, sm[:, 3:4], ALU.mult, ALU.mult, [bsm], [bsm])
            P.tt('dve', sm[:, 6:7], sm[:, 4:5], sm[:, 4:5], ALU.mult, [bsm], [bsm])
            yield
            pq0, bpq0 = k.bank(2)
            pq1, bpq1 = k.bank(3)
            pq = k.PS[1]
            for c in range(2):
                P.mm(pq0[:, 0:512], cT[:, c, c0:c1], Wuq[:, c, 0:512], c == 0, c == 1, [bcT, bWuq], [bpq0])
            for c in range(2):
                P.mm(pq1[:, 0:256], cT[:, c, c0:c1], Wuq[:, c, 512:768], c == 0, c == 1, [bcT, bWuq], [bpq1])
            sq, bsq = sqr.next()
            P.act(sq[:], pq[:, 0:768], AF.Square, [bpq0, bpq1], [bsq])
            P.op('dve', 'tensor_reduce', [bsq], [bsm], out=sm[:, 8:16], in_=sq[:].rearrange("p (h d) -> p h d", h=8),
                 axis=AX.X, op=ALU.add)
            P.ts('dve', sm[:, 8:16], sm[:, 8:16], sm[:, 5:6], None, ALU.mult, None, [bsm], [bsm])
            P.rsqrt(sm[:, 16:24], sm[:, 8:16], 1.0, EPS, [bsm], [bsm])
            P.ts('dve', sm[:, 16:24], sm[:, 16:24], sm[:, 3:4], None, ALU.mult, None, [bsm], [bsm])
            qn, bqn = qnr.next()
            P.tt('dve', qn[:], pq[:, 0:768].rearrange("p (h d) -> p h d", h=8),
                 sm[:, 16:24].unsqueeze(2).to_broadcast([128, 8, 96]), ALU.mult, [bpq0, bpq1, bsm], [bqn])
            yield
            P.tt('dve', qn[:], qn[:], gqk[:, 0, :].unsqueeze(1).to_broadcast([128, 8, 96]), ALU.mult, [bqn, bgqk], [bqn])
            qf, bqf = qfr.next()
            P.cp('act', qf[:, :, 0:64], qn[:, :, 0:64], [bqn], [bqf])
            yield
            rt, brt = rtr.next()
            cosb = cs[:, t, 0:16].unsqueeze(1).to_broadcast([128, 8, 16])
            sinb = cs[:, t, 16:32].unsqueeze(1).to_broadcast([128, 8, 16])
            x1, x2 = qn[:, :, 64:80], qn[:, :, 80:96]
            P.tt('pool', rt[:, 0], x1, cosb, ALU.mult, [bqn, bcs], [brt])
            P.tt('pool', rt[:, 1], x2, sinb, ALU.mult, [bqn, bcs], [brt])
            P.tt('pool', rt[:, 2], x1, sinb, ALU.mult, [bqn, bcs], [brt])
            P.tt('pool', rt[:, 3], x2, cosb, ALU.mult, [bqn, bcs], [brt])
            P.tt('pool', qf[:, :, 64:80], rt[:, 0], rt[:, 1], ALU.subtract, [brt], [bqf])
            P.tt('pool', qf[:, :, 80:96], rt[:, 2], rt[:, 3], ALU.add, [brt], [bqf])
            yield
            pk0, bpk0 = k.bank(4)
            pk1, bpk1 = k.bank(5)
            pkv = k.PS[2]
            P.mm(pk0[:, 0:512], cT[:, 2, c0:c1], Wukv[:, 0:512], True, True, [bcT, bWukv], [bpk0])
            P.mm(pk1[:, 0:512], cT[:, 2, c0:c1], Wukv[:, 512:1024], True, True, [bcT, bWukv], [bpk1])
            pkv3 = pkv[:, :].rearrange("p (h e) -> p h e", h=8)
            kn, bkn = knr.next()
            P.act(kn[:], pkv3[:, :, 0:64], AF.Square, [bpk0, bpk1], [bkn])
            P.op('dve', 'tensor_reduce', [bkn], [bsm], out=sm[:, 24:32], in_=kn[:], axis=AX.X, op=ALU.add)
            P.ts('dve', sm[:, 24:32], sm[:, 24:32], sm[:, 6:7], sm[:, 2:3], ALU.mult, ALU.add, [bsm], [bsm])
            P.rsqrt(sm[:, 32:40], sm[:, 24:32], 1.0 / QK, EPS, [bsm], [bsm])
            P.ts('dve', sm[:, 40:48], sm[:, 32:40], sm[:, 4:5], None, ALU.mult, None, [bsm], [bsm])
            P.tt('dve', kn[:], pkv3[:, :, 0:64], sm[:, 40:48].unsqueeze(2).to_broadcast([128, 8, 64]), ALU.mult,
                 [bpk0, bpk1, bsm], [bkn])
            va, bva = var_.next()
            P.ts('dve', va[:, :, 0:64], pkv3[:, :, 64:128], sm[:, 4:5], None, ALU.mult, None, [bpk0, bpk1, bsm], [bva])
            P.dma('sp', VA[t * 128:(t + 1) * 128, :], va[:].rearrange("p h e -> p (h e)"), bva, False)
            yield
            kf, bkf = kfr.next()
            P.tt('dve', kf[:, :, 0:64], kn[:], gqk[:, 1, 0:64].unsqueeze(1).to_broadcast([128, 8, 64]), ALU.mult, [bkn, bgqk], [bkf])
            yield
            P.tt('pool', kr[:, 1, :], kr[:, 0, :], gqk[:, 1, 64:96], ALU.mult, [bkr, bgqk], [bkr])
            P.tt('pool', kr[:, 2, 0:16], kr[:, 1, 0:16], cs[:, t, 0:16], ALU.mult, [bkr, bcs], [bkr])
            P.tt('pool', kr[:, 2, 16:32], kr[:, 1, 16:32], cs[:, t, 16:32], ALU.mult, [bkr, bcs], [bkr])
            P.tt('pool', kr[:, 3, 0:16], kr[:, 2, 0:16], kr[:, 2, 16:32], ALU.subtract, [bkr], [bkr])
            P.tt('pool', kr[:, 2, 0:16], kr[:, 1, 0:16], cs[:, t, 16:32], ALU.mult, [bkr, bcs], [bkr])
            P.tt('pool', kr[:, 2, 16:32], kr[:, 1, 16:32], cs[:, t, 0:16], ALU.mult, [bkr, bcs], [bkr])
            P.tt('pool', kr[:, 3, 16:32], kr[:, 2, 0:16], kr[:, 2, 16:32], ALU.add, [bkr], [bkr])
            P.tt('pool', kf[:, :, 64:96], kr[:, 3, :].unsqueeze(1).to_broadcast([128, 8, 32]),
                 sm[:, 32:40].unsqueeze(2).to_broadcast([128, 8, 32]), ALU.mult, [bkr, bsm], [bkf])
            yield
            ptq, bptq = k.bank(6)
            ptk, bptk = k.bank(7)
            ptqb = ptq.bitcast(BF16)
            ptkb = ptk.bitcast(BF16)
            for h in range(NH):
                P.tr(ptqb[0:QK, h * 128:(h + 1) * 128], qf[:, h, :], k.identb[:], [bqf, k.bidb], [bptq], inc=(h == NH - 1))
            P.cp('act', qtg[:, :, c0:c1], ptqb[0:QK, :].rearrange("p (h t) -> p h t", h=8), [bptq], [bqtg])
            for h in range(NH):
                P.tr(ptkb[0:QK, h * 128:(h + 1) * 128], kf[:, h, :], k.identb[:], [bkf, k.bidb], [bptk], inc=(h == NH - 1))
            P.cp('dve', ktg[:, :, c0:c1], ptkb[0:QK, :].rearrange("p (h t) -> p h t", h=8), [bptk], [bktg])
            if j == n - 1:
                P.dma('sp', QTv[:, :, tok0:tok0 + N], qtg[:, :, 0:N], bqtg, False)
                P.dma('sp', KTv[:, :, tok0:tok0 + N], ktg[:, :, 0:N], bktg, False)

        def all_tiles():
            for (t0, n) in groups:
                gs = {}
                for j in range(n):
                    yield tile_gen(t0, n, j, gs)
        interleave(all_tiles(), 3)
        P.barrier()


def phase_attn(k, l, QT, KT, VA, OT, groups):
    P, nc = k.P, k.nc
    T, TT = k.T, k.TT
    LA = 2
    with ExitStack() as es:
        def sb(name, shape, dt):
            return es.enter_context(nc.sbuf_tensor(U(name), shape, dt))
        Vall = sb('Vall', [128, T, NH * 65], BF16)
        VAv = VA.rearrange("(t p) e -> p t e", p=128)
        vcuts = [0, min(T, 2), min(T, 10), min(T, 20), T]
        bVs = []
        for ci in range(4):
            a, b_ = vcuts[ci], vcuts[ci + 1]
            bVs.append(P.buf('Vall%d' % ci))
            if b_ > a:
                P.dma('sp', Vall[:, a:b_, :], VAv[:, a:b_, :], bVs[ci], True)

        def bV_of(kt):
            for ci in range(4):
                if vcuts[ci] <= kt < vcuts[ci + 1]:
                    return bVs[ci]
        qhr = Rot(P, es, 'QTh', [QK, TT], BF16, 2)
        khr = Rot(P, es, 'KTh', [QK, TT], BF16, 2)
        pTr = Rot(P, es, 'pT', [128, 2, 512], BF16, LA + 2)
        osr = Rot(P, es, 'osb', [65, 512], F32, 3)
        rcr = Rot(P, es, 'rc', [65, 512], F32, 3)
        otr = Rot(P, es, 'ot', [64, 512], BF16, 3)
        obr = BankRot(k, (0, 1))
        pair_i = [0]
        tri = k.cst[:, C_TRI:C_TRI + 128]
        padb = k.cst[:, C_PADB:C_PADB + 1]
        heads = {}

        def load_head(h):
            qh, bqh = qhr.next()
            kh, bkh = khr.next()
            P.dma('sp', qh[:], QT[h], bqh, True)
            P.dma('sp', kh[:], KT[h], bkh, True)
            heads[h] = (qh, bqh, kh, bkh)

        items = []
        for h in range(NH):
            for (t0, n) in groups:
                kts = list(range(t0 + n))
                i = 0
                while i < len(kts):
                    kt = kts[i]
                    if 1 <= kt and kt + 1 < t0:
                        items.append(dict(h=h, t0=t0, n=n, kts=[kt, kt + 1]))
                        i += 2
                    else:
                        items.append(dict(h=h, t0=t0, n=n, kts=[kt]))
                        i += 1
        state = {}

        def emit_score(it):
            h, t0, n, kts = it['h'], it['t0'], it['n'], it['kts']
            if h not in heads:
                load_head(h)
            if kts[0] == 0 and t0 == 0 and h + 1 < NH and (h + 1) not in heads:
                load_head(h + 1)
            qh, bqh, kh, bkh = heads[h]
            N = n * 128
            tok0 = t0 * 128
            pi_ = 1 + (pair_i[0] % 3)
            pair_i[0] += 1
            pp = k.PS[pi_]
            (p0, bp0), (p1, bp1) = k.bank(2 * pi_), k.bank(2 * pi_ + 1)
            pT, bpT = pTr.next()
            if len(kts) == 2:
                for j, (pb_, bpb_) in enumerate(((p0, bp0), (p1, bp1))):
                    kt = kts[j]
                    P.mm(pb_[:, 0:N], kh[:, kt * 128:(kt + 1) * 128], qh[:, tok0:tok0 + N], True, True, [bkh, bqh], [bpb_])
                if N == 512:
                    P.act(pT[:].rearrange("p a b -> p (a b)"), pp[:, 0:1024], AF.Exp, [bp0, bp1], [bpT])
                else:
                    P.act(pT[:, :, 0:N], pp[:, :].rearrange("p (a b) -> p a b", a=2)[:, :, 0:N], AF.Exp, [bp0, bp1], [bpT])
                it['c0'] = 0
            else:
                kt = kts[0]
                c0 = max(kt - t0, 0) * 128
                P.mm(p0[:, 0:N - c0], kh[:, kt * 128:(kt + 1) * 128], qh[:, tok0 + c0:tok0 + N], True, True, [bkh, bqh], [bp0])
                if kt == 0:
                    P.act(pT[:, 0, 0:N - c0], p0[:, 0:N - c0], AF.Exp, [bp0, k.bcst], [bpT], bias=padb, scale=1.0)
                else:
                    P.act(pT[:, 0, 0:N - c0], p0[:, 0:N - c0], AF.Exp, [bp0], [bpT])
                if kt >= t0:
                    P.tt('dve', pT[:, 0, 0:128], pT[:, 0, 0:128], tri, ALU.mult, [bpT, k.bcst], [bpT])
                it['c0'] = c0
            it['pT'], it['bpT'], it['N'] = pT, bpT, N

        pending = []

        def emit_pv(it):
            h, t0, n, kts = it['h'], it['t0'], it['n'], it['kts']
            N, c0 = it['N'], it['c0']
            if kts[0] == 0:
                state['po'] = obr.next()
            po, bpo = state['po']
            for j, kt in enumerate(kts):
                P.mm(po[0:65, c0:N], Vall[:, kt, h * 65:(h + 1) * 65], it['pT'][:, j, 0:N - c0], kt == 0, kt == t0 + n - 1,
                     [bV_of(kt), it['bpT']], [bpo])
            if kts[-1] == t0 + n - 1:
                osb, bos = osr.next()
                P.cp('dve', osb[:, 0:N], po[0:65, 0:N], [bpo], [bos])
                rc, brc = rcr.next()
                P.ts('dve', rc[64:65, 0:N], osb[64:65, 0:N], 1e-30, None, ALU.add, None, [bos], [brc])
                P.op('dve', 'reciprocal', [brc], [brc], out=rc[64:65, 0:N], in_=rc[64:65, 0:N])
                pending.append(dict(h=h, t0=t0, N=N, osb=osb, bos=bos, rc=rc, brc=brc, age=0))

        def emit_final(f):
            N, tok0 = f['N'], f['t0'] * 128
            pi_ = 1 + (pair_i[0] % 3)
            pair_i[0] += 1
            pbc, bpbc = k.bank(2 * pi_)
            P.mm(pbc[0:64, 0:N], k.onesf[64:65, 0:64], f['rc'][64:65, 0:N], True, True, [k.bcst, f['brc']], [bpbc])
            ot, bot = otr.next()
            P.tt('dve', ot[:, 0:N], f['osb'][0:64, 0:N], pbc[0:64, 0:N], ALU.mult, [f['bos'], bpbc], [bot])
            P.dma('sp', OT[f['h'], :, tok0:tok0 + N], ot[:, 0:N], bot, False)

        for i in range(len(items) + LA):
            if i < len(items):
                emit_score(items[i])
            if i - LA >= 0:
                emit_pv(items[i - LA])
            for f in pending:
                f['age'] += 1
            while pending and pending[0]['age'] > 4:
                emit_final(pending.pop(0))
        while pending:
            emit_final(pending.pop(0))
        P.barrier()


def attn_out_load(k, es, w_in_l, w_ao):
    P, nc = k.P, k.nc

    def sb(name, shape, dt):
        return es.enter_context(nc.sbuf_tensor(U(name), shape, dt))
    Wao = sb('Wao', [64, 8, 1024], BF16); bWao = P.buf('Wao')
    load_w_cast(P, Wao, bWao, w_ao, 8, 1024, 0, pk=64)
    Wgb = sb('Wgb', [128, 8, 1024], BF16); bWgb = P.buf('Wgb')
    load_w_cast(P, Wgb, bWgb, w_in_l, 8, 1024, O_GB)
    return dict(Wao=Wao, bWao=bWao, Wgb=Wgb, bWgb=bWgb)


def phase_attn_out(k, w, hT, OT, mixb, groups):
    P, nc = k.P, k.nc
    Wao, bWao, Wgb, bWgb = w['Wao'], w['bWao'], w['Wgb'], w['bWgb']
    with ExitStack() as es:
        hgr = Rot(P, es, 'hTg', [128, 8, 512], BF16, 2)
        ogr = Rot(P, es, 'OTg', [64, 8, 512], BF16, 2)
        gar = Rot(P, es, 'gb', [128, 512], F32, 2)
        mgr = Rot(P, es, 'mixg', [128, 8, 512], BF16, 2)
        hTv = hT.rearrange("(kc p) t -> p kc t", p=128)
        mixv = mixb.rearrange("(kc p) t -> p kc t", p=128)
        OTv = OT.rearrange("h d t -> d h t")
        br = BankRot(k, range(8))
        def loads(t0, n):
            N = n * 128
            tok0 = t0 * 128
            hg, bhg = hgr.next()
            P.dma('sp', hg[:, :, 0:N], hTv[:, :, tok0:tok0 + N], bhg, True)
            og, bog = ogr.next()
            P.dma('sp', og[:, :, 0:N], OTv[:, :, tok0:tok0 + N], bog, True)
            return hg, bhg, og, bog

        nxt = loads(*groups[0])
        for gi, (t0, n) in enumerate(groups):
            N = n * 128
            tok0 = t0 * 128
            hg, bhg, og, bog = nxt
            if gi + 1 < len(groups):
                nxt = loads(*groups[gi + 1])
            mg, bmg = mgr.next()
            for nn in range(8):
                py, bpy = br.next()
                pgt, bpgt = br.next()
                for h in range(NH):
                    P.mm(py[:, 0:N], Wao[:, h, nn * 128:(nn + 1) * 128], og[:, h, 0:N], h == 0, h == NH - 1, [bWao, bog], [bpy])
                for kc in range(8):
                    P.mm(pgt[:, 0:N], Wgb[:, kc, nn * 128:(nn + 1) * 128], hg[:, kc, 0:N], kc == 0, kc == 7, [bWgb, bhg], [bpgt])
                ga, bga = gar.next()
                P.act(ga[:, 0:N], pgt[:, 0:N], AF.Sigmoid, [bpgt], [bga])
                P.tt('dve', mg[:, nn, 0:N], py[:, 0:N], ga[:, 0:N], ALU.mult, [bpy, bga], [bmg])
            P.dma('sp', mixv[:, :, tok0:tok0 + N], mg[:, :, 0:N], bmg, False)
        P.barrier()


def hgrn_load(k, es, l, w_in_l, lb_logits, norm_g, w_ho):
    P, nc = k.P, k.nc
    assert DEPTH == 2
    if True:
        def sb(name, shape, dt):
            return es.enter_context(nc.sbuf_tensor(U(name), shape, dt))
        Whq = sb('Whq', [128, 8, 512], BF16); bWhq = P.buf('Whq')
        Whf = sb('Whf', [128, 8, 512], BF16); bWhf = P.buf('Whf')
        Whi = sb('Whi', [128, 8, 512], BF16); bWhi = P.buf('Whi')
        Whg = sb('Whg', [128, 8, 512], BF16); bWhg = P.buf('Whg')
        Wgc = sb('Wgc', [128, 8, 1024], BF16); bWgc = P.buf('Wgc')
        Who = sb('Who', [128, 4, 1024], BF16); bWho = P.buf('Who')
        load_w_cast(P, Whq, bWhq, w_in_l, 8, 512, O_HQ)
        load_w_cast(P, Whf, bWhf, w_in_l, 8, 512, O_HF)
        load_w_cast(P, Whi, bWhi, w_in_l, 8, 512, O_HI)
        load_w_cast(P, Whg, bWhg, w_in_l, 8, 512, O_HG)
        load_w_cast(P, Wgc, bWgc, w_in_l, 8, 1024, O_GC)
        load_w_cast(P, Who, bWho, w_ho, 4, 1024, 0)
        omlb = sb('omlb', [64, 512], F32); bomlb = P.buf('omlb')
        ocol = sb('ocol', [128, 8], F32); bocol = P.buf('ocol')
        P.dma('sp', ocol[:, 4:8], norm_g.rearrange("(h p) -> p h", p=128), bocol, True, allow_slow_non_contiguous=True)
        P.ts('dve', ocol[:, 4:8], ocol[:, 4:8], 0.125, None, ALU.mult, None, [bocol], [bocol])
        if l == 0:
            P.memset('pool', omlb[:], 0.5, [bomlb])
            P.memset('pool', ocol[:, 0:4], 0.5, [bocol])
        else:
            lt = sb('lbt', [64, 2, 512], F32); blt = P.buf('lbt')
            lc = sb('lbc', [128, 2, 4], F32); blc = P.buf('lbc')
            for r in range(2):
                P.dma('sp', lt[:, r, :], lb_logits[r:r + 1, :].to_broadcast([64, 512]), blt, True)
                P.dma('sp', lc[:, r, :], lb_logits[r].rearrange("(h p) -> p h", p=128), blc, True, allow_slow_non_contiguous=True)
            P.tt('dve', lt[:, 0, :], lt[:, 0, :], lt[:, 1, :], ALU.subtract, [blt], [blt])
            P.act(omlb[:], lt[:, 0, :], AF.Sigmoid, [blt], [bomlb])
            P.ts('dve', omlb[:], omlb[:], 0.5, None, ALU.mult, None, [bomlb], [bomlb])
            P.tt('dve', lc[:, 0, :], lc[:, 0, :], lc[:, 1, :], ALU.subtract, [blc], [blc])
            P.act(ocol[:, 0:4], lc[:, 0, :], AF.Sigmoid, [blc], [bocol])
            P.ts('dve', ocol[:, 0:4], ocol[:, 0:4], 0.5, None, ALU.mult, None, [bocol], [bocol])
    return dict(Whq=Whq, bWhq=bWhq, Whf=Whf, bWhf=bWhf, Whi=Whi, bWhi=bWhi, Whg=Whg, bWhg=bWhg, Wgc=Wgc, bWgc=bWgc,
                Who=Who, bWho=bWho, omlb=omlb, bomlb=bomlb, ocol=ocol, bocol=bocol)


def phase_hgrn(k, w, hT, mixc, groups):
    P, nc = k.P, k.nc
    Whq, bWhq, Whf, bWhf, Whi, bWhi, Whg, bWhg = (w[n] for n in ('Whq', 'bWhq', 'Whf', 'bWhf', 'Whi', 'bWhi', 'Whg', 'bWhg'))
    Wgc, bWgc, Who, bWho, omlb, bomlb, ocol, bocol = (w[n] for n in ('Wgc', 'bWgc', 'Who', 'bWho', 'omlb', 'bomlb', 'ocol', 'bocol'))
    with ExitStack() as es:
        def sb(name, shape, dt):
            return es.enter_context(nc.sbuf_tensor(U(name), shape, dt))
        Lm = k.cst[0:64, C_L:C_L + 64]
        UMa = k.cst[0:64, C_UM:C_UM + 64]
        UMb = k.cst[0:64, C_UM + 64:C_UM + 66]
        tri8 = k.cst[0:64, C_TRI8:C_TRI8 + 256]

        S = sb('S', [128, HH, 128], F32); bS = P.buf('S')
        St = sb('St', [128, HH, 128], F32); bSt = P.buf('St')
        P.memset('pool', S[:], 0.0, [bS])
        GN = 256
        NC_ = GN // 64
        hgr = Rot(P, es, 'hTg', [128, 8, GN], BF16, 3)
        qTr = Rot(P, es, 'hqT', [128, HH, GN], BF16, 2)
        kTsr = Rot(P, es, 'hkT', [128, HH, GN], BF16, 2)
        sgTr = Rot(P, es, 'hsgT', [128, HH, GN], BF16, 2)
        lfr = Rot(P, es, 'lf', [64, NC_, 512], F32, 2)
        kdr = Rot(P, es, 'kd', [64, NC_, 512], BF16, 2)
        vr = Rot(P, es, 'hv', [64, NC_, 512], BF16, 2)
        qer = Rot(P, es, 'qe', [128, HH, GN], BF16, 2)
        ker = Rot(P, es, 'ke', [128, HH, GN], BF16, 2)
        ktmr = Rot(P, es, 'ktm', [64, 512], F32, 2)
        kclr = Rot(P, es, 'kcl', [64, 512], F32, 2)
        thir = Rot(P, es, 'thi', [64, 512], F32, 2)
        erbr = Rot(P, es, 'erb', [64, 512], BF16, 2)
        scr = Rot(P, es, 'hscr', [128, GN], F32, 4)
        thr = epr = emr = osqr = rsr = onr = scr
        exsr = Rot(P, es, 'exs', [128, NC_, 2, HH], F32, 2)
        ATr = Rot(P, es, 'ATa', [64, NC_, 256], BF16, 2)
        Mr = Rot(P, es, 'Ma', [128, NC_, 512], F32, 2)
        Sbr = Rot(P, es, 'Sba', [128, NC_, 512], BF16, 2)
        ogr = Rot(P, es, 'og', [128, HH, GN], BF16, 2)
        gar = Rot(P, es, 'gc', [128, GN], F32, 2)
        mgr = Rot(P, es, 'mixg', [128, 8, GN], BF16, 2)
        hTv = hT.rearrange("(kc p) t -> p kc t", p=128)
        mixv = mixc.rearrange("(kc p) t -> p kc t", p=128)

        hgroups = groups_of(k.T, 2)

        def load_hg(t0, n):
            hg, bhg = hgr.next()
            P.dma('sp', hg[:, :, 0:n * 128], hTv[:, :, t0 * 128:(t0 + n) * 128], bhg, True)
            return hg, bhg
        pf = Prefetch(hgroups, load_hg)

        def group_gen(gi, t0, n):
            N = n * 128
            nch = N // 64
            tok0 = t0 * 128
            hg, bhg = pf.get(gi)
            qT, bqT = qTr.next(); kTs, bkTs = kTsr.next(); sgT, bsgT = sgTr.next()
            lf_all, blf = lfr.next(); kd_all, bkd = kdr.next(); v_all, bv = vr.next()
            qe, bqe = qer.next(); ke, bke = ker.next(); exs, bexs = exsr.next()
            AT_all, bAT = ATr.next(); M_all, bM = Mr.next(); Sb_all, bSb = Sbr.next(); og, bog = ogr.next()
            br1 = BankRot(k, (0, 1, 2, 3))
            for h in range(HH):
                hsl = slice(h * 128, (h + 1) * 128)
                pb, bpb = br1.next()
                for kc in range(8):
                    P.mm(pb[:, 0:N], Whq[:, kc, hsl], hg[:, kc, 0:N], kc == 0, kc == 7, [bWhq, bhg], [bpb])
                P.cp('act', qT[:, h, 0:N], pb[:, 0:N], [bpb], [bqT])
                pb, bpb = br1.next()
                for kc in range(8):
                    P.mm(pb[:, 0:N], Whf[:, kc, hsl], hg[:, kc, 0:N], kc == 0, kc == 7, [bWhf, bhg], [bpb])
                th, bth = thr.next()
                P.act(th[:, 0:N], pb[:, 0:N], AF.Tanh, [bpb], [bth], scale=0.5)
                P.ts('dve', kTs[:, h, 0:N], th[:, 0:N], -1.0, 1.0, ALU.mult, ALU.add, [bth], [bkTs])
                pb, bpb = br1.next()
                for kc in range(8):
                    P.mm(pb[:, 0:N], Whg[:, kc, hsl], hg[:, kc, 0:N], kc == 0, kc == 7, [bWhg, bhg], [bpb])
                th, bth = thr.next()
                P.act(th[:, 0:N], pb[:, 0:N], AF.Tanh, [bpb], [bth], scale=0.5)
                P.stt(sgT[:, h, 0:N], th[:, 0:N], 1.0, pb[:, 0:N], ALU.add, ALU.mult, [bth, bpb], [bsgT])
            yield
            brf = BankRot(k, (4, 5))
            bri = BankRot(k, (6, 7))
            brr = BankRot(k, (2, 3))
            st2 = {}

            def s2_mm(c):
                pf, bpf = brf.next()
                pi, bpi = bri.next()
                for kc in range(8):
                    P.mm(pf[0:64, 0:512], hg[:, kc, c * 64:(c + 1) * 64], Whf[:, kc, :], kc == 0, kc == 7, [bhg, bWhf], [bpf])
                for kc in range(8):
                    P.mm(pi[0:64, 0:512], hg[:, kc, c * 64:(c + 1) * 64], Whi[:, kc, :], kc == 0, kc == 7, [bhg, bWhi], [bpi])
                ktm, bktm = ktmr.next()
                thi, bthi = thir.next()
                P.act(ktm[:], pf[0:64, 0:512], AF.Tanh, [bpf], [bktm], scale=0.5)
                P.act(thi[:], pi[0:64, 0:512], AF.Tanh, [bpi], [bthi], scale=0.5)
                P.ts('dve', ktm[:], ktm[:], -1.0, 1.0, ALU.mult, ALU.add, [bktm], [bktm])
                P.tt('dve', ktm[:], ktm[:], omlb[:], ALU.mult, [bktm, bomlb], [bktm])
                P.stt(v_all[:, c, :], thi[:], 1.0, pi[0:64, 0:512], ALU.add, ALU.mult, [bthi, bpi], [bv])
                kcl, bkcl = kclr.next()
                P.ts('dve', kcl[:], ktm[:], CLAMP, None, ALU.min, None, [bktm], [bkcl])
                st2[c] = (ktm, bktm, kcl, bkcl)

            def s2_ln(c):
                ktm, bktm, kcl, bkcl = st2[c]
                P.act(lf_all[:, c, :], kcl[:], AF.Ln, [bkcl], [blf], scale=-1.0, bias=1.0)

            def s2_back(c):
                ktm, bktm, kcl, bkcl = st2.pop(c)
                prb, bprb = brr.next()
                P.mm(prb[0:64, 0:512], Lm, lf_all[:, c, :], True, True, [k.bcst, blf], [bprb])
                erb, berb = erbr.next()
                P.act(erb[:], prb[0:64, 0:512], AF.Exp, [bprb], [berb])
                P.tt('dve', kd_all[:, c, :], ktm[:], erb[:], ALU.mult, [bktm, berb], [bkd])

            for c2 in range(0, nch, 2):
                s2_mm(c2)
                s2_mm(c2 + 1)
                s2_ln(c2)
                s2_ln(c2 + 1)
                s2_back(c2)
                s2_back(c2 + 1)
            yield
            br3 = BankRot(k, (0, 1))
            exv = exs[:].rearrange("p c j h -> p h c j")
            for h in range(HH):
                pbm, bpbm = br3.next()
                pex, bpex = brr.next()
                for c in range(nch):
                    P.mm(pbm[:, c * 64:(c + 1) * 64], lf_all[:, c, h * 128:(h + 1) * 128], UMa, True, True, [blf, k.bcst], [bpbm],
                         inc=(c == nch - 1))
                for c in range(nch):
                    P.mm(pex[:, c * 2:c * 2 + 2], lf_all[:, c, h * 128:(h + 1) * 128], UMb, True, True,
                         [blf, k.bcst], [bpex], inc=(c == nch - 1))
                ep, bep = epr.next()
                em, bem = emr.next()
                P.act(ep[:, 0:N], pbm[:, 0:N], AF.Exp, [bpbm], [bep])
                P.act(em[:, 0:N], pbm[:, 0:N], AF.Exp, [bpbm], [bem], scale=-1.0)
                P.tt('dve', qe[:, h, 0:N], qT[:, h, 0:N], ep[:, 0:N], ALU.mult, [bqT, bep], [bqe])
                P.stt(ke[:, h, 0:N], kTs[:, h, 0:N], ocol[:, h:h + 1], em[:, 0:N], ALU.mult, ALU.mult, [bkTs, bocol, bem], [bke])
                P.act(exv[:, h, 0:nch, :], pex[:, 0:nch * 2].rearrange("p (c j) -> p c j", j=2), AF.Exp, [bpex], [bexs])
            yield
            brA = BankRot(k, (4, 5))
            brM = BankRot(k, (6, 7))
            for c in range(nch):
                cs_ = slice(c * 64, (c + 1) * 64)
                pA, bpA = brA.next()
                for h in range(HH):
                    P.mm(pA[0:64, h * 64:(h + 1) * 64], ke[:, h, cs_], qe[:, h, cs_], True, True, [bke, bqe], [bpA], inc=(h == HH - 1))
                P.tt('dve', AT_all[:, c, :], pA[0:64, 0:256], tri8, ALU.mult, [bpA, k.bcst], [bAT])
                pM, bpM_ = brM.next()
                for h in range(HH):
                    hs = slice(h * 128, (h + 1) * 128)
                    P.mm(pM[:, hs], kd_all[:, c, hs], v_all[:, c, hs], True, True, [bkd, bv], [bpM_], inc=(h == HH - 1))
                P.cp('act', M_all[:, c, :], pM[:, 0:512], [bpM_], [bM])
            yield
            S3 = S[:]
            for c in range(nch):
                e_mid = exs[:, c, 0, :].unsqueeze(2).to_broadcast([128, HH, 128])
                e_last = exs[:, c, 1, :].unsqueeze(2).to_broadcast([128, HH, 128])
                P.tt('pool', Sb_all[:, c, :].rearrange("p (h d) -> p h d", h=HH), S3, e_mid, ALU.mult, [bS, bexs], [bSb])
                P.tt('dve', St[:], S3, e_last, ALU.mult, [bS, bexs], [bSt])
                P.tt('dve', S3, St[:], M_all[:, c, :].rearrange("p (h d) -> p h d", h=HH), ALU.add, [bSt, bM], [bS])
            yield
            bpo = [k.bank(h) for h in range(HH)]
            for c in range(nch):
                cs_ = slice(c * 64, (c + 1) * 64)
                for h in range(HH):
                    po, bpo_h = bpo[h]
                    hs = slice(h * 128, (h + 1) * 128)
                    P.mm(po[:, cs_], v_all[:, c, hs], AT_all[:, c, h * 64:(h + 1) * 64], True, False, [bv, bAT], [bpo_h], inc=False)
                    P.mm(po[:, cs_], Sb_all[:, c, hs], qe[:, h, cs_], False, True, [bSb, bqe], [bpo_h], inc=True)
            for h in range(HH):
                po, bpo_h = bpo[h]
                osq, bosq = osqr.next()
                P.act(osq[:, 0:N], po[:, 0:N], AF.Square, [bpo_h], [bosq])
                pms, bpms = brA.next()
                P.mm(pms[:, 0:N], k.onesf, osq[:, 0:N], True, True, [k.bcst, bosq], [bpms])
                rs, brs = rsr.next()
                P.act(rs[:, 0:N], pms[:, 0:N], AF.Ln, [bpms], [brs], scale=0.25 / 128, bias=EPS)
                P.act(rs[:, 0:N], rs[:, 0:N], AF.Exp, [brs], [brs], scale=-0.5)
                on, bon = onr.next()
                P.tt('dve', on[:, 0:N], po[:, 0:N], rs[:, 0:N], ALU.mult, [bpo_h, brs], [bon])
                P.stt(og[:, h, 0:N], on[:, 0:N], ocol[:, 4 + h:5 + h], sgT[:, h, 0:N], ALU.mult, ALU.mult, [bon, bocol, bsgT], [bog])
            yield
            mg, bmg = mgr.next()
            br6 = BankRot(k, (4, 5, 6, 7))
            for nn in range(8):
                py, bpy = br6.next()
                pgt, bpgt = br6.next()
                for h in range(HH):
                    P.mm(py[:, 0:N], Who[:, h, nn * 128:(nn + 1) * 128], og[:, h, 0:N], h == 0, h == HH - 1, [bWho, bog], [bpy])
                for kc in range(8):
                    P.mm(pgt[:, 0:N], Wgc[:, kc, nn * 128:(nn + 1) * 128], hg[:, kc, 0:N], kc == 0, kc == 7, [bWgc, bhg], [bpgt])
                ga, bga = gar.next()
                P.act(ga[:, 0:N], pgt[:, 0:N], AF.Tanh, [bpgt], [bga], scale=0.5)
                P.stt(mg[:, nn, 0:N], ga[:, 0:N], 1.0, py[:, 0:N], ALU.add, ALU.mult, [bga, bpy], [bmg])
            P.dma('sp', mixv[:, :, tok0:tok0 + N], mg[:, :, 0:N], bmg, False)

        interleave((group_gen(gi, t0, n) for gi, (t0, n) in enumerate(hgroups)), 2)
        P.barrier()


def phase_merge(k, l, xres, hT, mixa, mixb, mixc, w_out_l, norm2_g, groups, after_loads=None):
    P, nc = k.P, k.nc
    with ExitStack() as es:
        def sb(name, shape, dt):
            return es.enter_context(nc.sbuf_tensor(U(name), shape, dt))
        Wo = sb('Wo', [128, 8, 1024], BF16); bWo = P.buf('Wo')
        load_w_cast(P, Wo, bWo, w_out_l, 8, 1024, 0)
        gbc = sb('g2bc', [128, D], F32); bg = P.buf('g2bc')
        P.dma('sp', gbc[:], norm2_g[l:l + 1, :].to_broadcast([128, D]), bg, True)
        if after_loads is not None:
            after_loads()
        groups = groups_of(k.T, 2)
        mar = Rot(P, es, 'ma', [128, 8, 256], BF16, 3)
        mbr = Rot(P, es, 'mb', [128, 8, 256], BF16, 3)
        mcr = Rot(P, es, 'mc', [128, 8, 256], BF16, 3)
        tmp = sb('mtmp', [128, 8, 256], F32); btmp = P.buf('mtmp')
        mxr = Rot(P, es, 'mx', [128, 8, 256], BF16, 2)
        xr = Rot(P, es, 'x', [128, D], F32, 6)
        x1r = Rot(P, es, 'x1', [128, D], F32, 4)
        hgr = Rot(P, es, 'h2g', [128, 8, 256], BF16, 2)
        nt = NormTools(k, es, banks=(0, 1))
        hTv = hT.rearrange("(kc p) t -> p kc t", p=128)
        views = [m.rearrange("(kc p) t -> p kc t", p=128) for m in (mixa, mixb, mixc)]
        pair = [0]

        def load_mix(t0, n):
            N = n * 128
            tok0 = t0 * 128
            ma, bma = mar.next()
            mb, bmb = mbr.next()
            mc, bmc = mcr.next()
            for (mt, bm, v) in ((ma, bma, views[0]), (mb, bmb, views[1]), (mc, bmc, views[2])):
                P.dma('sp', mt[:, :, 0:N], v[:, :, tok0:tok0 + N], bm, True)
            return ma, bma, mb, bmb, mc, bmc
        pf = Prefetch(groups, load_mix)

        def group_prep(t0, n):
            N = n * 128
            tok0 = t0 * 128
            ma, bma, mb, bmb, mc, bmc = pf.get(groups.index((t0, n)))
            P.tt('dve', tmp[:, :, 0:N], ma[:, :, 0:N], mb[:, :, 0:N], ALU.add, [bma, bmb], [btmp])
            mx, bmx = mxr.next()
            P.tt('dve', mx[:, :, 0:N], tmp[:, :, 0:N], mc[:, :, 0:N], ALU.add, [btmp, bmc], [bmx])
            return mx, bmx

        def load_x(t):
            xt, bx = xr.next()
            P.dma('sp', xt[:], k.xsrc(l, t), bx, True)
            return xt, bx
        pfx = Prefetch([(t,) for t in range(k.T)], load_x, ahead=2)

        def tile_gen(t0, n, j, gs):
            t = t0 + j
            if j == 0:
                gs['mx'] = group_prep(t0, n)
                gs['hg'] = hgr.next()
            mx, bmx = gs['mx']
            hg, bhg = gs['hg']
            xt, bx = pfx.get(t)
            pi_ = 1 + (pair[0] % 3)
            pair[0] += 1
            pd = k.PS[pi_]
            (p0, bp0), (p1, bp1) = k.bank(2 * pi_), k.bank(2 * pi_ + 1)
            for kc in range(8):
                P.mm(p0[:, 0:512], mx[:, kc, j * 128:(j + 1) * 128], Wo[:, kc, 0:512], kc == 0, kc == 7, [bmx, bWo], [bp0])
            for kc in range(8):
                P.mm(p1[:, 0:512], mx[:, kc, j * 128:(j + 1) * 128], Wo[:, kc, 512:1024], kc == 0, kc == 7, [bmx, bWo], [bp1])
            yield
            x1, bx1 = x1r.next()
            P.tt('dve', x1[:], pd[:, 0:1024], xt[:], ALU.add, [bp0, bp1, bx], [bx1])
            if t == 0:
                P.dma('sp', xres[PAD:128, :], x1[PAD:128, :], bx1, False)
            else:
                P.dma('sp', xres[t * 128:(t + 1) * 128, :], x1[:], bx1, False)
            yield from nt.norm_tile_gen(x1, bx1, gbc, bg, hg, bhg, j)
            if j == n - 1:
                P.dma('sp', hTv[:, :, t0 * 128:(t0 + n) * 128], hg[:, :, 0:n * 128], bhg, False)

        def all_tiles():
            for (t0, n) in groups:
                gs = {}
                for j in range(n):
                    yield tile_gen(t0, n, j, gs)
        interleave(all_tiles(), 3)
        P.barrier()


def ffn_load_w1(k, es, w1):
    P, nc = k.P, k.nc
    W1 = es.enter_context(nc.sbuf_tensor(U('W1'), [128, 8, 4096], BF16))
    bW1q = [P.buf('W1q%d' % q) for q in range(4)]
    w1v = w1.rearrange("(kc p) n -> p kc n", p=128)

    def issue():
        for q in range(4):
            P.dma('pool', W1[:, :, q * 1024:(q + 1) * 1024], w1v[:, :, q * 1024:(q + 1) * 1024], bW1q[q], True)
    return W1, bW1q, issue


def phase_ffn(k, l, xres, hT, out, w1pre, w2, last):
    P, nc = k.P, k.nc
    T = k.T
    W1, bW1q, _ = w1pre
    with ExitStack() as es:
        def sb(name, shape, dt):
            return es.enter_context(nc.sbuf_tensor(U(name), shape, dt))
        W2 = sb('W2', [128, 32, 1024], BF16); bW2 = P.buf('W2')
        load_w_cast(P, W2, bW2, w2, 32, 1024, 0)
        hgr = Rot(P, es, 'h2g', [128, 8, 256], BF16, 3)
        fTr = Rot(P, es, 'fT', [128, 32, 256], BF16, 2)
        rr = Rot(P, es, 'frl', [128, 256], F32, 3)
        xr = Rot(P, es, 'x1', [128, D], F32, 2)
        xor_ = Rot(P, es, 'xo', [128, D], F32, 2)
        hTv = hT.rearrange("(kc p) t -> p kc t", p=128)
        br = BankRot(k, (0, 1, 2, 3))
        pair = [0]
        fgroups = [(t0, n) for (t0, n) in groups_of(T, 2) if not (last and t0 == 0)]

        def load_hg(t0, n):
            hg, bhg = hgr.next()
            P.dma('sp', hg[:, :, 0:n * 128], hTv[:, :, t0 * 128:(t0 + n) * 128], bhg, True)
            return hg, bhg
        pf = Prefetch(fgroups, load_hg)

        def ffn1(t0, n):
            N = n * 128
            tok0 = t0 * 128
            hg, bhg = pf.get(fgroups.index((t0, n)))
            fT, bfT = fTr.next()
            for fc in range(32):
                pb, bpb = br.next()
                for kc in range(8):
                    P.mm(pb[:, 0:N], W1[:, kc, fc * 128:(fc + 1) * 128], hg[:, kc, 0:N], kc == 0, kc == 7, [bW1q[fc // 8], bhg], [bpb])
                r, brl = rr.next()
                P.act(r[:, 0:N], pb[:, 0:N], AF.Relu, [bpb], [brl])
                P.tt('dve', fT[:, fc, 0:N], r[:, 0:N], r[:, 0:N], ALU.mult, [brl], [bfT])
            return fT, bfT

        def ffn2(t0, n, fT, bfT):
            for j in range(n):
                t = t0 + j
                xt, bx = xr.next()
                P.dma('sp', xt[:], xres[t * 128:(t + 1) * 128, :], bx, True)
                pi_ = 2 + (pair[0] % 2)
                pair[0] += 1
                pd = k.PS[pi_]
                (p0, bp0), (p1, bp1) = k.bank(2 * pi_), k.bank(2 * pi_ + 1)
                for fc in range(32):
                    P.mm(p0[:, 0:512], fT[:, fc, j * 128:(j + 1) * 128], W2[:, fc, 0:512], fc == 0, fc == 31, [bfT, bW2], [bp0])
                    P.mm(p1[:, 0:512], fT[:, fc, j * 128:(j + 1) * 128], W2[:, fc, 512:1024], fc == 0, fc == 31, [bfT, bW2], [bp1])
                xo, bxo = xor_.next()
                P.tt('dve', xo[:], pd[:, 0:1024], xt[:], ALU.add, [bp0, bp1, bx], [bxo])
                if last:
                    P.dma('sp', out[(t - 1) * 128:t * 128, :], xo[:], bxo, False)
                elif t == 0:
                    P.dma('sp', xres[PAD:128, :], xo[PAD:128, :], bxo, False)
                else:
                    P.dma('sp', xres[t * 128:(t + 1) * 128, :], xo[:], bxo, False)

        prev = None
        for (t0, n) in fgroups:
            cur = (t0, n) + ffn1(t0, n)
            if prev is not None:
                ffn2(*prev)
            prev = cur
        ffn2(*prev)
        P.barrier()

C_IDENT = 0
C_ONES = 128
C_TRI = 256
C_L = 384
C_UM = 448
C_PADB = 514
C_TRI8 = 515
NCONST = C_TRI8 + 512


def make_consts():
    c = np.zeros((128, NCONST), np.float32)
    c[:, C_IDENT:C_IDENT + 128] = np.eye(128, dtype=np.float32)
    c[:, C_ONES:C_ONES + 128] = 1.0
    s = np.arange(128)[:, None]
    t = np.arange(128)[None, :]
    c[:, C_TRI:C_TRI + 128] = (s <= t).astype(np.float32)
    s64 = np.arange(64)[:, None]
    t64 = np.arange(64)[None, :]
    c[:64, C_L:C_L + 64] = (s64 > t64).astype(np.float32)
    c[:64, C_UM:C_UM + 64] = (s64 <= t64).astype(np.float32) - (s64 <= 31).astype(np.float32)
    c[:64, C_UM + 64] = (np.arange(64) <= 31).astype(np.float32)
    c[:64, C_UM + 65] = 1.0
    c[:PAD, C_PADB] = -30000.0
    c[:64, C_TRI8:C_TRI8 + 512] = np.tile((s64 <= t64).astype(np.float32), (1, 8))
    return c


def make_cs_tab(T):
    half = 16
    pos = (np.arange(T * 128, dtype=np.float32) - np.float32(PAD)).astype(np.float32)
    inv_freq = (np.float32(10000.0) ** (-np.arange(half, dtype=np.float32) / np.float32(half))).astype(np.float32)
    ang = (pos[:, None] * inv_freq[None, :]).astype(np.float32)
    cs = np.concatenate([np.cos(ang), np.sin(ang)], axis=1).astype(np.float32)
    return np.ascontiguousarray(cs.reshape(T, 128, 32).transpose(1, 0, 2).reshape(128, T * 32))


_W_NAMES = ['meta', 'norm1_g', 'w_in', 'conv_w', 'conv_b', 'conv_ln_g', 'conv_ln_b', 'w_conv_out',
            'q_a_norm_g', 'w_uq', 'kv_a_norm_g', 'w_ukv', 'q_norm_g', 'k_norm_g', 'w_attn_out',
            'hgrn_lb_logits', 'hgrn_norm_g', 'w_hgrn_out', 'w_out', 'norm2_g', 'w_ff1', 'w_ff2']


def kernel(**inputs):
    x = np.ascontiguousarray(inputs['x'], dtype=np.float32)
    B, SEQ, _ = x.shape
    T = 1 + SEQ // 128
    nc = build(T)
    shared = {n: np.ascontiguousarray(inputs[n], dtype=np.float32) for n in _W_NAMES}
    shared['consts'] = make_consts()
    shared['cs_tab'] = make_cs_tab(T)
    in_maps = []
    for b in range(B):
        m = dict(shared)
        m['x'] = x[b]
        in_maps.append(m)
    res = run_bass_kernel_spmd(nc, in_maps, core_ids=list(range(B)))
    return np.stack([np.asarray(r['out']) for r in res.results], axis=0).astype(np.float32)
```

```python
import numpy as np
import concourse.bass as bass
import concourse.mybir as mybir
from concourse.bass_utils import run_bass_kernel_spmd
from contextlib import ExitStack

F32 = mybir.dt.float32
BF16 = mybir.dt.bfloat16
ALU = mybir.AluOpType
AF = mybir.ActivationFunctionType
AX = mybir.AxisListType

D = 1024
DEPTH = 2
N_META = 16
PAD = 112
EPS = 1e-6
CLAMP = 1.0 - 1e-6
N_IN = 6560
O_CONV, O_CQ, O_CKV, O_KR, O_HQ, O_HF, O_HI, O_HG, O_GA, O_GB, O_GC = (
    0, 1024, 1280, 1408, 1440, 1952, 2464, 2976, 3488, 4512, 5536)
CONV_K = 31
NH = 8
QK = 96
HH = 4

CENGS = ['pe', 'act', 'dve', 'pool']
ENGS = CENGS + ['sp']


class Buf:
    __slots__ = ('name', 'last_w', 'readers', 'dsem', 'dcnt')

    def __init__(self, name):
        self.name = name
        self.last_w = None
        self.readers = {}
        self.dsem = None
        self.dcnt = 0


class Prog:
    def __init__(self, nc, es):
        self.nc = nc
        self.es = es
        self.ops = {e: [] for e in ENGS}
        self.cnt = {e: 0 for e in CENGS}
        self.sem = {e: es.enter_context(nc.semaphore('c_' + e)) for e in CENGS}
        self.known = {e: {e2: 0 for e2 in CENGS} for e in ENGS}
        self.dknown = {e: {} for e in ENGS}
        self.dbufs = []
        self.dsem_free = {'sp': [], 'pool': []}
        self.dsem_q = {}
        self.dsem_tot = {}
        self.nbuf = 0

    def buf(self, name=None):
        self.nbuf += 1
        return Buf('%s_%d' % (name or 'b', self.nbuf))

    def _wait_eng(self, e, e2, idx):
        if e == 'pe' and e2 == 'pe':
            return
        if self.known[e][e2] >= idx:
            return
        self.known[e][e2] = idx
        sem = self.sem[e2]
        self.ops[e].append(lambda eng: eng.wait_ge(sem, idx))

    def _wait_dma(self, e, b):
        if b.dcnt == 0:
            return
        if self.dknown[e].get(b, 0) >= b.dcnt:
            return
        self.dknown[e][b] = b.dcnt
        sem, val = b.dsem, 16 * self.dsem_tot[id(b.dsem)]
        self.ops[e].append(lambda eng: eng.wait_ge(sem, val))

    def _deps(self, e, reads, writes):
        for b in reads:
            self._wait_dma(e, b)
            if b.last_w is not None:
                self._wait_eng(e, *b.last_w)
        for b in writes:
            self._wait_dma(e, b)
            if b.last_w is not None:
                self._wait_eng(e, *b.last_w)
            for e2, idx in b.readers.items():
                self._wait_eng(e, e2, idx)

    def op(self, e, name, reads=(), writes=(), inc=True, **kw):
        self._deps(e, reads, writes)
        if inc:
            self.cnt[e] += 1
            idx = self.cnt[e]
            sem = self.sem[e]
            self.ops[e].append(lambda eng: getattr(eng, name)(**kw).then_inc(sem, 1))
        else:
            idx = self.cnt[e] + 1
            self.ops[e].append(lambda eng: getattr(eng, name)(**kw))
        for b in writes:
            b.last_w = (e, idx)
            b.readers = {}
        for b in reads:
            b.readers[e] = idx

    def dma(self, e, out, in_, b, load, **kw):
        if b.dsem is None:
            if self.dsem_free[e]:
                b.dsem = self.dsem_free[e].pop()
            else:
                b.dsem = self.es.enter_context(self.nc.semaphore('d%d' % len(self.dsem_tot)))
                self.dsem_tot[id(b.dsem)] = 0
            b.dcnt = 0
            self.dsem_q[id(b.dsem)] = e
            self.dbufs.append(b)
        assert self.dsem_q[id(b.dsem)] == e, 'a DMA semaphore must stay on one queue'
        if load:
            self._deps(e, (), (b,))
        else:
            self._deps(e, (b,), ())
        b.dcnt += 1
        self.dsem_tot[id(b.dsem)] += 1
        sem = b.dsem
        self.ops[e].append(lambda eng: eng.dma_start(out=out, in_=in_, **kw).then_inc(sem, 16))
        if load:
            b.last_w = None
            b.readers = {}

    def barrier(self):
        for e in ENGS:
            for e2 in CENGS:
                if e2 != e and self.cnt[e2] > 0:
                    self._wait_eng(e, e2, self.cnt[e2])
            for b in self.dbufs:
                self._wait_dma(e, b)
        for b in self.dbufs:
            if self.dsem_q[id(b.dsem)] == 'sp':
                self.dsem_free['sp'].append(b.dsem)
            b.dsem = None
            b.dcnt = 0
            for e in ENGS:
                self.dknown[e].pop(b, None)
        self.dbufs = []

    def mm(self, out, lhsT, rhs, start, stop, reads, writes, inc=None):
        self.op('pe', 'matmul', reads, writes, inc=(stop if inc is None else inc),
                out=out, lhsT=lhsT, rhs=rhs, start=start, stop=stop)

    def tr(self, out, in_, identity, reads, writes, inc=True):
        self.op('pe', 'transpose', reads, writes, inc=inc, out=out, in_=in_, identity=identity)

    def act(self, out, in_, func, reads, writes, **kw):
        self.op('act', 'activation', reads, writes, out=out, in_=in_, func=func, **kw)

    def tt(self, e, out, in0, in1, op, reads, writes):
        self.op(e, 'tensor_tensor', reads, writes, out=out, in0=in0, in1=in1, op=op)

    def ts(self, e, out, in0, s1, s2, op0, op1, reads, writes):
        if op1 is None:
            self.op(e, 'tensor_scalar', reads, writes, out=out, in0=in0, scalar1=s1, scalar2=None, op0=op0)
        else:
            self.op(e, 'tensor_scalar', reads, writes, out=out, in0=in0, scalar1=s1, scalar2=s2, op0=op0, op1=op1)

    def stt(self, out, in0, scalar, in1, op0, op1, reads, writes):
        self.op('dve', 'scalar_tensor_tensor', reads, writes, out=out, in0=in0, scalar=scalar, in1=in1, op0=op0, op1=op1)

    def cp(self, e, out, in_, reads, writes):
        if e == 'act':
            self.op('act', 'copy', reads, writes, out=out, in_=in_)
        else:
            self.op(e, 'tensor_copy', reads, writes, out=out, in_=in_)

    def rsqrt(self, out, in_, scale, eps, reads, writes):
        self.act(out, in_, AF.Sqrt, reads, writes, scale=scale, bias=self.eps_ap(eps, out))
        self.op('dve', 'reciprocal', list(writes), list(writes), out=out, in_=out)

    def eps_ap(self, eps, like):
        return eps

    def memset(self, e, ap, val, writes):
        self.op(e, 'memset', (), writes, ap=ap, constant=val)

    def emit(self):
        self.barrier()
        nc = self.nc
        with nc.Block() as block:
            @block.tensor
            def _(eng):
                for f in self.ops['pe']:
                    f(eng)

            @block.scalar
            def _(eng):
                for f in self.ops['act']:
                    f(eng)

            @block.vector
            def _(eng):
                for f in self.ops['dve']:
                    f(eng)

            @block.gpsimd
            def _(eng):
                for f in self.ops['pool']:
                    f(eng)

            @block.sync
            def _(eng):
                for f in self.ops['sp']:
                    f(eng)


_UID = [0]


def U(name):
    _UID[0] += 1
    return '%s_u%d' % (name, _UID[0])


class Rot:
    def __init__(self, P, es, name, shape, dtype, n):
        self.tiles = [es.enter_context(P.nc.sbuf_tensor(U('%s%d' % (name, i)), shape, dtype)) for i in range(n)]
        self.bufs = [P.buf('%s%d' % (name, i)) for i in range(n)]
        self.i = 0

    def next(self):
        k = self.i % len(self.tiles)
        self.i += 1
        return self.tiles[k], self.bufs[k]


def groups_of(T, gsz=4):
    gs = [(0, 1)]
    t = 1
    while t < T:
        n = min(gsz, T - t)
        gs.append((t, n))
        t += n
    return gs


class K:
    pass


def load_w_cast(P, dst, dst_buf, src, K_chunks, ncols, col0=0, pk=128):
    v = src.rearrange("(kc p) n -> p kc n", p=pk)
    for k0 in range(0, K_chunks, 8):
        k1 = min(K_chunks, k0 + 8)
        c = 0
        while c < ncols:
            w = min(2048, ncols - c)
            P.dma('pool', dst[:, k0:k1, c:c + w], v[:, k0:k1, col0 + c:col0 + c + w], dst_buf, True)
            c += w


class Prefetch:
    def __init__(self, items, load_fn, ahead=1):
        self.items, self.load_fn, self.ahead = list(items), load_fn, ahead
        self.loaded = {}
        self.next = 0

    def get(self, i):
        while self.next < len(self.items) and self.next <= i + self.ahead:
            self.loaded[self.next] = self.load_fn(*self.items[self.next])
            self.next += 1
        return self.loaded.pop(i)


class BankRot:
    def __init__(self, k, banks):
        self.k, self.banks, self.i = k, list(banks), 0

    def next(self):
        b = self.banks[self.i % len(self.banks)]
        self.i += 1
        return self.k.bank(b)


def build(T, stop_after=None, debug=False, nlayers=DEPTH):
    SEQ = (T - 1) * 128
    TT = T * 128
    nc = bass.Bass("TRN2", target_bir_lowering=False)

    def din(name, shape):
        return nc.dram_tensor(name, list(shape), F32, kind="ExternalInput").ap()

    def dscr(name, shape, dt):
        if debug:
            return nc.dram_tensor(name, list(shape), dt, kind="ExternalOutput").ap()
        return nc.dram_tensor(name, list(shape), dt).ap()

    k = K()
    k.T, k.TT, k.SEQ = T, TT, SEQ
    x_in = din("x", (SEQ, D))
    meta = din("meta", (N_META, D))
    norm1_g = din("norm1_g", (DEPTH, D))
    w_in = din("w_in", (DEPTH, D, N_IN))
    conv_w = din("conv_w", (DEPTH, CONV_K, 512))
    conv_b = din("conv_b", (DEPTH, 512))
    conv_ln_g = din("conv_ln_g", (DEPTH, 512))
    conv_ln_b = din("conv_ln_b", (DEPTH, 512))
    w_conv_out = din("w_conv_out", (DEPTH, 512, D))
    q_a_norm_g = din("q_a_norm_g", (DEPTH, 256))
    w_uq = din("w_uq", (DEPTH, 256, 768))
    kv_a_norm_g = din("kv_a_norm_g", (DEPTH, 128))
    w_ukv = din("w_ukv", (DEPTH, 128, 1024))
    q_norm_g = din("q_norm_g", (DEPTH, 96))
    k_norm_g = din("k_norm_g", (DEPTH, 96))
    w_attn_out = din("w_attn_out", (DEPTH, 512, D))
    hgrn_lb_logits = din("hgrn_lb_logits", (DEPTH, 512))
    hgrn_norm_g = din("hgrn_norm_g", (DEPTH, 512))
    w_hgrn_out = din("w_hgrn_out", (DEPTH, 512, D))
    w_out = din("w_out", (DEPTH, D, D))
    norm2_g = din("norm2_g", (DEPTH, D))
    w_ff1 = din("w_ff1", (DEPTH, D, 4096))
    w_ff2 = din("w_ff2", (DEPTH, 4096, D))
    consts = din("consts", (128, NCONST))
    cs_tab = din("cs_tab", (128, T * 32))
    out = nc.dram_tensor("out", [SEQ, D], F32, kind="ExternalOutput").ap()

    xres = dscr("xres", (TT, D), F32)
    hT = dscr("hT", (D, TT), BF16)
    mixa = dscr("mixa", (D, TT), BF16)
    mixb = dscr("mixb", (D, TT), BF16)
    mixc = dscr("mixc", (D, TT), BF16)
    QT = dscr("QT", (NH, QK, TT), BF16)
    KT = dscr("KT", (NH, QK, TT), BF16)
    VA = dscr("VA", (TT, NH * 65), BF16)
    OT = dscr("OT", (NH, 64, TT), BF16)

    groups = groups_of(T)

    with ExitStack() as es0:
        P = Prog(nc, es0)
        k.P, k.nc = P, nc
        k.PS = [es0.enter_context(nc.psum_tensor('ps%d' % i, [128, 1024], F32)) for i in range(4)]
        k.PSB = [P.buf('psb%d' % i) for i in range(8)]

        def bank(i):
            return k.PS[i // 2][:, (i % 2) * 512:(i % 2) * 512 + 512], k.PSB[i]
        k.bank = bank
        cst = es0.enter_context(nc.sbuf_tensor('cst', [128, NCONST], F32))
        bcst = P.buf('cst')
        P.dma('sp', cst[:], consts, bcst, True)
        identb = es0.enter_context(nc.sbuf_tensor('identb', [128, 128], BF16))
        bidb = P.buf('identb')
        P.cp('dve', identb[:], cst[:, C_IDENT:C_IDENT + 128], [bcst], [bidb])
        k.cst, k.bcst, k.identb, k.bidb = cst, bcst, identb, bidb
        k.identf = cst[:, C_IDENT:C_IDENT + 128]
        k.onesf = cst[:, C_ONES:C_ONES + 128]

        with ExitStack() as es:
            rot = Rot(P, es, 'xi', [128, D], F32, 3)
            tz, bz = rot.next()
            P.memset('pool', tz[:], 0.0, [bz])
            P.dma('sp', tz[PAD:128, :], meta, bz, True)
            P.dma('sp', xres[0:128, :], tz[:], bz, False)
            P.barrier()

        def xsrc(l, t):
            if l == 0 and t >= 1:
                return x_in[(t - 1) * 128:t * 128, :]
            return xres[t * 128:(t + 1) * 128, :]
        k.xsrc = xsrc

        for l in range(nlayers):
            if stop_after == 'init':
                break
            last = (l == nlayers - 1)
            es_wc = ExitStack()
            wconv = conv_load(k, es_wc, w_in[l], conv_w[l], conv_b[l], conv_ln_g[l], conv_ln_b[l], w_conv_out[l])
            phase_norm1(k, l, xres, hT, norm1_g, groups)
            if stop_after == 'p0':
                es_wc.close()
                break
            phase_conv(k, wconv, hT, mixa, groups)
            es_wc.close()
            if stop_after == 'p1':
                break
            phase_mla_pre(k, l, hT, QT, KT, VA, w_in[l], q_a_norm_g[l], w_uq[l], kv_a_norm_g[l], w_ukv[l],
                          q_norm_g, k_norm_g, cs_tab, groups)
            if stop_after == 'p2a':
                break
            es_wh = ExitStack()
            whg = hgrn_load(k, es_wh, l, w_in[l], hgrn_lb_logits, hgrn_norm_g[l], w_hgrn_out[l])
            es_wa = ExitStack()
            wao = attn_out_load(k, es_wa, w_in[l], w_attn_out[l])
            phase_attn(k, l, QT, KT, VA, OT, groups)
            if stop_after == 'p2b':
                es_wa.close(); es_wh.close()
                break
            phase_attn_out(k, wao, hT, OT, mixb, groups)
            es_wa.close()
            if stop_after == 'p2c':
                es_wh.close()
                break
            phase_hgrn(k, whg, hT, mixc, groups)
            es_wh.close()
            if stop_after == 'p3':
                break
            es_w1 = ExitStack()
            w1pre = ffn_load_w1(k, es_w1, w_ff1[l])
            phase_merge(k, l, xres, hT, mixa, mixb, mixc, w_out[l], norm2_g, groups, after_loads=w1pre[2])
            if stop_after == 'p4a':
                es_w1.close()
                break
            phase_ffn(k, l, xres, hT, out, w1pre, w_ff2[l], last)
            es_w1.close()
        P.emit()
    k.P = P
    build.last_k = k
    return nc


def interleave(gens, depth):
    active = []
    it = iter(gens)
    more = True
    while True:
        while more and len(active) < depth:
            try:
                active.append(next(it))
            except StopIteration:
                more = False
        if not active:
            break
        for g in list(active):
            try:
                next(g)
            except StopIteration:
                active.remove(g)


class NormTools:
    def __init__(self, k, es, banks=(0, 1), nrot=4):
        self.k = k
        P = k.P
        self.junk = Rot(P, es, 'njunk', [128, D], BF16, 2)
        self.ss = Rot(P, es, 'nss', [128, 4], F32, nrot)
        self.hb = Rot(P, es, 'nhb', [128, D], BF16, nrot)
        self.br = BankRot(k, banks)

    def norm_tile_gen(self, xt, bx, gbc, bg, hg, bhg, j):
        k = self.k
        P = k.P
        jk, bj = self.junk.next()
        ss, bss = self.ss.next()
        hb, bhb = self.hb.next()
        P.act(jk[:], xt[:], AF.Square, [bx], [bj, bss], accum_out=ss[:, 0:1])
        P.rsqrt(ss[:, 2:3], ss[:, 0:1], 1.0 / D, EPS, [bss], [bss])
        P.stt(hb[:], xt[:], ss[:, 2:3], gbc[:], ALU.mult, ALU.mult, [bx, bss, bg], [bhb])
        yield
        pb, bpb = self.br.next()
        pbb = pb.bitcast(BF16)
        for kc in range(8):
            P.tr(pbb[:, kc * 128:(kc + 1) * 128], hb[:, kc * 128:(kc + 1) * 128], k.identb[:],
                 [bhb, k.bidb], [bpb], inc=(kc == 7))
        P.cp('act', hg[:, :, j * 128:(j + 1) * 128], pbb.rearrange("p (kc t) -> p kc t", kc=8), [bpb], [bhg])


def phase_norm1(k, l, xres, hT, norm1_g, groups):
    P, nc = k.P, k.nc
    with ExitStack() as es:
        gbc = es.enter_context(nc.sbuf_tensor(U('gbc'), [128, D], F32))
        bg = P.buf('gbc')
        P.dma('sp', gbc[:], norm1_g[l:l + 1, :].to_broadcast([128, D]), bg, True)
        xr = Rot(P, es, 'x', [128, D], F32, 6)
        hgr = Rot(P, es, 'hTg', [128, 8, 512], BF16, 3)
        nt = NormTools(k, es)
        hTv = hT.rearrange("(kc p) t -> p kc t", p=128)

        def load_x(t):
            xt, bx = xr.next()
            P.dma('sp', xt[:], k.xsrc(l, t), bx, True)
            return xt, bx
        pfx = Prefetch([(t,) for t in range(k.T)], load_x, ahead=2)

        def tile_gen(t0, n, j, hg, bhg):
            t = t0 + j
            xt, bx = pfx.get(t)
            yield from nt.norm_tile_gen(xt, bx, gbc, bg, hg, bhg, j)
            if j == n - 1:
                P.dma('sp', hTv[:, :, t0 * 128:(t0 + n) * 128], hg[:, :, 0:n * 128], bhg, False)

        def all_tiles():
            for (t0, n) in groups:
                hg, bhg = hgr.next()
                for j in range(n):
                    yield tile_gen(t0, n, j, hg, bhg)
        interleave(all_tiles(), 3)
        P.barrier()


def conv_load(k, es, w_in_l, conv_w, conv_b, ln_g, ln_b, w_co):
    P, nc = k.P, k.nc
    if True:
        def sb(name, shape, dt):
            return es.enter_context(nc.sbuf_tensor(U(name), shape, dt))
        Wc = sb('Wc', [128, 8, 1024], BF16); bWc = P.buf('Wc')
        Wga = sb('Wga', [128, 8, 1024], BF16); bWga = P.buf('Wga')
        Wco = sb('Wco', [128, 4, 1024], BF16); bWco = P.buf('Wco')
        load_w_cast(P, Wc, bWc, w_in_l, 8, 1024, O_CONV)
        load_w_cast(P, Wga, bWga, w_in_l, 8, 1024, O_GA)
        load_w_cast(P, Wco, bWco, w_co, 4, 1024, 0)
        cw31 = sb('cw31', [32, 512], F32); bcw31 = P.buf('cw31')
        P.dma('sp', cw31[0:CONV_K, :], conv_w, bcw31, True)
        cw = sb('cw', [128, 4, 32], F32); bcw = P.buf('cw')
        pb, bpb = k.bank(0)
        for cc in range(4):
            P.tr(pb[:, cc * 32:cc * 32 + CONV_K], cw31[0:CONV_K, cc * 128:(cc + 1) * 128],
                 k.identf[0:CONV_K, 0:CONV_K], [bcw31, k.bcst], [bpb], inc=(cc == 3))
        P.cp('dve', cw[:, :, 0:CONV_K], pb[:, 0:128].rearrange("p (c j) -> p c j", c=4)[:, :, 0:CONV_K], [bpb], [bcw])
        Dg = sb('Dg', [128, 4, CONV_K, 128], BF16); bDg = P.buf('Dg')
        for cc in range(4):
            P.tt('dve' if cc % 2 == 0 else 'pool', Dg[:, cc, :, :],
                 k.identf.unsqueeze(1).to_broadcast([128, CONV_K, 128]),
                 cw[:, cc, 0:CONV_K].unsqueeze(2).to_broadcast([128, CONV_K, 128]), ALU.mult, [k.bcst, bcw], [bDg])
        cols = sb('ccols', [128, 12], F32); bcols = P.buf('ccols')
        for i, src in enumerate((conv_b, ln_g, ln_b)):
            P.dma('sp', cols[:, i * 4:(i + 1) * 4], src.rearrange("(c p) -> p c", p=128), bcols, True,
                  allow_slow_non_contiguous=True)
    return dict(Wc=Wc, bWc=bWc, Wga=Wga, bWga=bWga, Wco=Wco, bWco=bWco, Dg=Dg, bDg=bDg, cols=cols, bcols=bcols)


def phase_conv(k, w, hT, mixa, groups):
    P, nc = k.P, k.nc
    Wc, bWc, Wga, bWga, Wco, bWco = w['Wc'], w['bWc'], w['Wga'], w['bWga'], w['Wco'], w['bWco']
    Dg, bDg, cols, bcols = w['Dg'], w['bDg'], w['cols'], w['bcols']
    with ExitStack() as es:
        hgr = Rot(P, es, 'hTg', [128, 8, 512], BF16, 3)
        hcr = Rot(P, es, 'hc', [128, 4, 30 + 512], BF16, 2)
        sgr = Rot(P, es, 'sg', [128, 512], F32, 2)
        cvr = Rot(P, es, 'cv', [128, 4, 512], F32, 2)
        sqr = Rot(P, es, 'sq', [128, 4, 512], F32, 2)
        str_ = Rot(P, es, 'st', [128, 4, 512], F32, 2)
        xcr = Rot(P, es, 'xc', [128, 512], F32, 2)
        actr = Rot(P, es, 'cact', [128, 4, 512], BF16, 2)
        gar = Rot(P, es, 'ga', [128, 512], F32, 2)
        mgr = Rot(P, es, 'mixg', [128, 8, 512], BF16, 2)
        hTv = hT.rearrange("(kc p) t -> p kc t", p=128)
        mixv = mixa.rearrange("(kc p) t -> p kc t", p=128)
        br = BankRot(k, range(8))
        prev = {}

        def load_hg(t0, n):
            hg, bhg = hgr.next()
            P.dma('sp', hg[:, :, 0:n * 128], hTv[:, :, t0 * 128:(t0 + n) * 128], bhg, True)
            return hg, bhg
        pf = Prefetch(groups, load_hg)

        def group_gen(gi, t0, n):
            N = n * 128
            tok0 = t0 * 128
            hg, bhg = pf.get(gi)
            hc, bhc = hcr.next()
            if prev:
                P.cp('pool', hc[:, :, 0:30], prev['hc'][:, :, prev['N']:prev['N'] + 30], [prev['bhc']], [bhc])
            else:
                P.memset('pool', hc[:, :, 0:30], 0.0, [bhc])
            for cc in range(4):
                pa, bpa = br.next()
                pg, bpg = br.next()
                for kc in range(8):
                    P.mm(pa[:, 0:N], Wc[:, kc, cc * 128:(cc + 1) * 128], hg[:, kc, 0:N], kc == 0, kc == 7, [bWc, bhg], [bpa])
                for kc in range(8):
                    P.mm(pg[:, 0:N], Wc[:, kc, 512 + cc * 128:512 + (cc + 1) * 128], hg[:, kc, 0:N], kc == 0, kc == 7, [bWc, bhg], [bpg])
                sg, bsg = sgr.next()
                P.act(sg[:, 0:N], pg[:, 0:N], AF.Sigmoid, [bpg], [bsg])
                P.tt('dve', hc[:, cc, 30:30 + N], pa[:, 0:N], sg[:, 0:N], ALU.mult, [bpa, bsg], [bhc])
            prev.update(hc=hc, bhc=bhc, N=N)
            yield
            cv, bcv = cvr.next()
            sq, bsq = sqr.next()
            for cc in range(4):
                pc, bpc = br.next()
                for j in range(CONV_K):
                    P.mm(pc[:, 0:N], Dg[:, cc, j, :], hc[:, cc, j:j + N], j == 0, j == CONV_K - 1, [bDg, bhc], [bpc])
                P.act(cv[:, cc, 0:N], pc[:, 0:N], AF.Identity, [bpc, bcols], [bcv], bias=cols[:, cc:cc + 1], scale=1.0)
                P.act(sq[:, cc, 0:N], pc[:, 0:N], AF.Square, [bpc, bcols], [bsq], bias=cols[:, cc:cc + 1], scale=1.0)
            yield
            st, bst = str_.next()
            pm, bpm = br.next()
            pq, bpq = br.next()
            for cc in range(4):
                P.mm(pm[:, 0:N], k.onesf, cv[:, cc, 0:N], cc == 0, cc == 3, [k.bcst, bcv], [bpm])
            for cc in range(4):
                P.mm(pq[:, 0:N], k.onesf, sq[:, cc, 0:N], cc == 0, cc == 3, [k.bcst, bsq], [bpq])
            P.act(st[:, 0, 0:N], pm[:, 0:N], AF.Copy, [bpm], [bst], scale=1.0 / 512)
            P.tt('pool', st[:, 1, 0:N], st[:, 0, 0:N], st[:, 0, 0:N], ALU.mult, [bst], [bst])
            P.stt(st[:, 2, 0:N], pq[:, 0:N], 1.0 / 512, st[:, 1, 0:N], ALU.mult, ALU.subtract, [bpq, bst], [bst])
            P.rsqrt(st[:, 3, 0:N], st[:, 2, 0:N], 1.0, EPS, [bst], [bst])
            yield
            act, bact = actr.next()
            for cc in range(4):
                xc, bxc = xcr.next()
                P.tt('pool', xc[:, 0:N], cv[:, cc, 0:N], st[:, 0, 0:N], ALU.subtract, [bcv, bst], [bxc])
                P.tt('dve', xc[:, 0:N], xc[:, 0:N], st[:, 3, 0:N], ALU.mult, [bxc, bst], [bxc])
                P.act(act[:, cc, 0:N], xc[:, 0:N], AF.Silu, [bxc, bcols], [bact],
                      bias=cols[:, 8 + cc:9 + cc], scale=cols[:, 4 + cc:5 + cc])
            yield
            mg, bmg = mgr.next()
            for nn in range(8):
                py, bpy = br.next()
                pgt, bpgt = br.next()
                for cc in range(4):
                    P.mm(py[:, 0:N], Wco[:, cc, nn * 128:(nn + 1) * 128], act[:, cc, 0:N], cc == 0, cc == 3, [bWco, bact], [bpy])
                for kc in range(8):
                    P.mm(pgt[:, 0:N], Wga[:, kc, nn * 128:(nn + 1) * 128], hg[:, kc, 0:N], kc == 0, kc == 7, [bWga, bhg], [bpgt])
                ga, bga = gar.next()
                P.act(ga[:, 0:N], pgt[:, 0:N], AF.Sigmoid, [bpgt], [bga])
                P.tt('dve', mg[:, nn, 0:N], py[:, 0:N], ga[:, 0:N], ALU.mult, [bpy, bga], [bmg])
            P.dma('sp', mixv[:, :, tok0:tok0 + N], mg[:, :, 0:N], bmg, False)

        interleave((group_gen(gi, t0, n) for gi, (t0, n) in enumerate(groups)), 2)
        P.barrier()

def phase_mla_pre(k, l, hT, QT, KT, VA, w_in_l, q_a_g, w_uq, kv_a_g, w_ukv, q_norm_g, k_norm_g, cs_tab, groups):
    P, nc = k.P, k.nc
    T = k.T
    with ExitStack() as es:
        def sb(name, shape, dt):
            return es.enter_context(nc.sbuf_tensor(U(name), shape, dt))
        Wm = sb('Wm', [128, 8, 416], BF16); bWm = P.buf('Wm')
        load_w_cast(P, Wm, bWm, w_in_l, 8, 416, O_CQ)
        wtmp = sb('wtmp', [128, 2, 1024], F32); bwt = P.buf('wtmp')
        gcol = sb('gcol', [128, 4], F32); bgc = P.buf('gcol')
        P.dma('sp', gcol[:, 0:2], q_a_g.rearrange("(c p) -> p c", p=128), bgc, True, allow_slow_non_contiguous=True)
        P.dma('sp', gcol[:, 2:3], kv_a_g.rearrange("(c p) -> p c", p=128), bgc, True, allow_slow_non_contiguous=True)
        Wuq = sb('Wuq', [128, 2, 768], BF16); bWuq = P.buf('Wuq')
        Wukv = sb('Wukv', [128, 1024], BF16); bWukv = P.buf('Wukv')
        P.dma('sp', wtmp[:, :, 0:768], w_uq.rearrange("(c p) n -> p c n", p=128), bwt, True)
        for c in range(2):
            P.ts('dve', Wuq[:, c, :], wtmp[:, c, 0:768], gcol[:, c:c + 1], None, ALU.mult, None, [bwt, bgc], [bWuq])
        P.dma('sp', wtmp[:, 0, :], w_ukv, bwt, True)
        P.ts('dve', Wukv[:], wtmp[:, 0, :], gcol[:, 2:3], None, ALU.mult, None, [bwt, bgc], [bWukv])
        gqk = sb('gqk', [128, 2, 96], F32); bgqk = P.buf('gqk')
        P.dma('sp', gqk[:, 0, :], q_norm_g[l:l + 1, :].to_broadcast([128, 96]), bgqk, True)
        P.dma('sp', gqk[:, 1, :], k_norm_g[l:l + 1, :].to_broadcast([128, 96]), bgqk, True)
        P.ts('dve', gqk[:, 0, :], gqk[:, 0, :], float(QK) ** -0.5, None, ALU.mult, None, [bgqk], [bgqk])
        cs = sb('cs', [128, T, 32], F32); bcs = P.buf('cs')
        P.dma('sp', cs[:], cs_tab.rearrange("p (t e) -> p t e", e=32), bcs, True)

        hgr = Rot(P, es, 'hTg', [128, 8, 512], BF16, 3)
        cTr = Rot(P, es, 'cT', [128, 3, 512], BF16, 2)
        junk = sb('mjunk', [128, 256], BF16); bjunk = P.buf('mjunk')
        smr = Rot(P, es, 'sm', [128, 48], F32, 4)
        krr_ = Rot(P, es, 'kr', [128, 4, 32], F32, 4)
        sqr = Rot(P, es, 'sqq', [128, 768], F32, 4)
        qnr = Rot(P, es, 'qn', [128, 8, 96], F32, 4)
        rtr = Rot(P, es, 'rt', [128, 4, 8, 16], F32, 4)
        qfr = Rot(P, es, 'qf', [128, 8, 96], BF16, 4)
        kfr = Rot(P, es, 'kf', [128, 8, 96], BF16, 4)
        knr = Rot(P, es, 'kn', [128, 8, 64], F32, 4)
        var_ = Rot(P, es, 'va', [128, 8, 65], BF16, 5)
        for t_, b_ in zip(var_.tiles, var_.bufs):
            P.memset('pool', t_[:], 1.0, [b_])
        QTg = Rot(P, es, 'QTg', [96, 8, 512], BF16, 2)
        KTg = Rot(P, es, 'KTg', [96, 8, 512], BF16, 2)
        hTv = hT.rearrange("(kc p) t -> p kc t", p=128)
        QTv = QT.rearrange("h d t -> d h t")
        KTv = KT.rearrange("h d t -> d h t")

        def load_hg(t0, n):
            hg, bhg = hgr.next()
            P.dma('sp', hg[:, :, 0:n * 128], hTv[:, :, t0 * 128:(t0 + n) * 128], bhg, True)
            return hg, bhg
        pf = Prefetch(groups, load_hg)

        def group_prep(t0, n):
            N = n * 128
            tok0 = t0 * 128
            hg, bhg = pf.get(groups.index((t0, n)))
            cT, bcT = cTr.next()
            pb, bpb = k.bank(0)
            for ch in range(3):
                for kc in range(8):
                    P.mm(pb[:, 0:N], Wm[:, kc, ch * 128:(ch + 1) * 128], hg[:, kc, 0:N], kc == 0, kc == 7, [bWm, bhg], [bpb])
                P.cp('act', cT[:, ch, 0:N], pb[:, 0:N], [bpb], [bcT])
            qtg, bqtg = QTg.next()
            ktg, bktg = KTg.next()
            return dict(N=N, tok0=tok0, hg=hg, bhg=bhg, cT=cT, bcT=bcT, qtg=qtg, bqtg=bqtg, ktg=ktg, bktg=bktg)

        def tile_gen(t0, n, j, gs):
            if j == 0:
                gs.update(group_prep(t0, n))
            N, tok0, hg, bhg, cT, bcT = gs['N'], gs['tok0'], gs['hg'], gs['bhg'], gs['cT'], gs['bcT']
            qtg, bqtg, ktg, bktg = gs['qtg'], gs['bqtg'], gs['ktg'], gs['bktg']
            t = t0 + j
            c0, c1 = j * 128, (j + 1) * 128
            sm, bsm = smr.next()
            kr, bkr = krr_.next()
            ptm, bptm = k.bank(1)
            for kc in range(8):
                P.mm(ptm[:, 0:416], hg[:, kc, c0:c1], Wm[:, kc, 0:416], kc == 0, kc == 7, [bhg, bWm], [bptm])
            P.act(junk[:, 0:256], ptm[:, 0:256], AF.Square, [bptm], [bjunk, bsm], accum_out=sm[:, 0:1])
            P.act(junk[:, 0:128], ptm[:, 256:384], AF.Square, [bptm], [bjunk, bsm], accum_out=sm[:, 1:2])
            P.act(junk[:, 0:32], ptm[:, 384:416], AF.Square, [bptm], [bjunk, bsm], accum_out=sm[:, 2:3])
            P.cp('act', kr[:, 0, :], ptm[:, 384:416], [bptm], [bkr])
            P.rsqrt(sm[:, 3:4], sm[:, 0:1], 1.0 / 256, EPS, [bsm], [bsm])
            P.rsqrt(sm[:, 4:5], sm[:, 1:2], 1.0 / 128, EPS, [bsm], [bsm])
            P.stt(sm[:, 5:6], sm[:, 3:4], 1.0 / QK, sm[:, 3:4], ALU.mult, ALU.mult, [bsm], [bsm])
            P.tt('dve', sm[:, 6:7], sm[:, 4:5], sm[:, 4:5], ALU.mult, [bsm], [bsm])
            yield
            pq0, bpq0 = k.bank(2)
            pq1, bpq1 = k.bank(3)
            pq = k.PS[1]
            for c in range(2):
                P.mm(pq0[:, 0:512], cT[:, c, c0:c1], Wuq[:, c, 0:512], c == 0, c == 1, [bcT, bWuq], [bpq0])
            for c in range(2):
                P.mm(pq1[:, 0:256], cT[:, c, c0:c1], Wuq[:, c, 512:768], c == 0, c == 1, [bcT, bWuq], [bpq1])
            sq, bsq = sqr.next()
            P.act(sq[:], pq[:, 0:768], AF.Square, [bpq0, bpq1], [bsq])
            P.op('dve', 'tensor_reduce', [bsq], [bsm], out=sm[:, 8:16], in_=sq[:].rearrange("p (h d) -> p h d", h=8),
                 axis=AX.X, op=ALU.add)
            P.ts('dve', sm[:, 8:16], sm[:, 8:16], sm[:, 5:6], None, ALU.mult, None, [bsm], [bsm])
            P.rsqrt(sm[:, 16:24], sm[:, 8:16], 1.0, EPS, [bsm], [bsm])
            P.ts('dve', sm[:, 16:24], sm[:, 16:24], sm[:, 3:4], None, ALU.mult, None, [bsm], [bsm])
            qn, bqn = qnr.next()
            P.tt('dve', qn[:], pq[:, 0:768].rearrange("p (h d) -> p h d", h=8),
                 sm[:, 16:24].unsqueeze(2).to_broadcast([128, 8, 96]), ALU.mult, [bpq0, bpq1, bsm], [bqn])
            yield
            P.tt('dve', qn[:], qn[:], gqk[:, 0, :].unsqueeze(1).to_broadcast([128, 8, 96]), ALU.mult, [bqn, bgqk], [bqn])
            qf, bqf = qfr.next()
            P.cp('act', qf[:, :, 0:64], qn[:, :, 0:64], [bqn], [bqf])
            yield
            rt, brt = rtr.next()
            cosb = cs[:, t, 0:16].unsqueeze(1).to_broadcast([128, 8, 16])
            sinb = cs[:, t, 16:32].unsqueeze(1).to_broadcast([128, 8, 16])
            x1, x2 = qn[:, :, 64:80], qn[:, :, 80:96]
            P.tt('pool', rt[:, 0], x1, cosb, ALU.mult, [bqn, bcs], [brt])
            P.tt('pool', rt[:, 1], x2, sinb, ALU.mult, [bqn, bcs], [brt])
            P.tt('pool', rt[:, 2], x1, sinb, ALU.mult, [bqn, bcs], [brt])
            P.tt('pool', rt[:, 3], x2, cosb, ALU.mult, [bqn, bcs], [brt])
            P.tt('pool', qf[:, :, 64:80], rt[:, 0], rt[:, 1], ALU.subtract, [brt], [bqf])
            P.tt('pool', qf[:, :, 80:96], rt[:, 2], rt[:, 3], ALU.add, [brt], [bqf])
            yield
            pk0, bpk0 = k.bank(4)
            pk1, bpk1 = k.bank(5)
            pkv = k.PS[2]
            P.mm(pk0[:, 0:512], cT[:, 2, c0:c1], Wukv[:, 0:512], True, True, [bcT, bWukv], [bpk0])
            P.mm(pk1[:, 0:512], cT[:, 2, c0:c1], Wukv[:, 512:1024], True, True, [bcT, bWukv], [bpk1])
            pkv3 = pkv[:, :].rearrange("p (h e) -> p h e", h=8)
            kn, bkn = knr.next()
            P.act(kn[:], pkv3[:, :, 0:64], AF.Square, [bpk0, bpk1], [bkn])
            P.op('dve', 'tensor_reduce', [bkn], [bsm], out=sm[:, 24:32], in_=kn[:], axis=AX.X, op=ALU.add)
            P.ts('dve', sm[:, 24:32], sm[:, 24:32], sm[:, 6:7], sm[:, 2:3], ALU.mult, ALU.add, [bsm], [bsm])
            P.rsqrt(sm[:, 32:40], sm[:, 24:32], 1.0 / QK, EPS, [bsm], [bsm])
            P.ts('dve', sm[:, 40:48], sm[:, 32:40], sm[:, 4:5], None, ALU.mult, None, [bsm], [bsm])
            P.tt('dve', kn[:], pkv3[:, :, 0:64], sm[:, 40:48].unsqueeze(2).to_broadcast([128, 8, 64]), ALU.mult,
                 [bpk0, bpk1, bsm], [bkn])
            va, bva = var_.next()
            P.ts('dve', va[:, :, 0:64], pkv3[:, :, 64:128], sm[:, 4:5], None, ALU.mult, None, [bpk0, bpk1, bsm], [bva])
            P.dma('sp', VA[t * 128:(t + 1) * 128, :], va[:].rearrange("p h e -> p (h e)"), bva, False)
            yield
            kf, bkf = kfr.next()
            P.tt('dve', kf[:, :, 0:64], kn[:], gqk[:, 1, 0:64].unsqueeze(1).to_broadcast([128, 8, 64]), ALU.mult, [bkn, bgqk], [bkf])
            yield
            P.tt('pool', kr[:, 1, :], kr[:, 0, :], gqk[:, 1, 64:96], ALU.mult, [bkr, bgqk], [bkr])
            P.tt('pool', kr[:, 2, 0:16], kr[:, 1, 0:16], cs[:, t, 0:16], ALU.mult, [bkr, bcs], [bkr])
            P.tt('pool', kr[:, 2, 16:32], kr[:, 1, 16:32], cs[:, t, 16:32], ALU.mult, [bkr, bcs], [bkr])
            P.tt('pool', kr[:, 3, 0:16], kr[:, 2, 0:16], kr[:, 2, 16:32], ALU.subtract, [bkr], [bkr])
            P.tt('pool', kr[:, 2, 0:16], kr[:, 1, 0:16], cs[:, t, 16:32], ALU.mult, [bkr, bcs], [bkr])
            P.tt('pool', kr[:, 2, 16:32], kr[:, 1, 16:32], cs[:, t, 0:16], ALU.mult, [bkr, bcs], [bkr])
            P.tt('pool', kr[:, 3, 16:32], kr[:, 2, 0:16], kr[:, 2, 16:32], ALU.add, [bkr], [bkr])
            P.tt('pool', kf[:, :, 64:96], kr[:, 3, :].unsqueeze(1).to_broadcast([128, 8, 32]),
                 sm[:, 32:40].unsqueeze(2).to_broadcast([128, 8, 32]), ALU.mult, [bkr, bsm], [bkf])
            yield
            ptq, bptq = k.bank(6)
            ptk, bptk = k.bank(7)
            ptqb = ptq.bitcast(BF16)
            ptkb = ptk.bitcast(BF16)
            for h in range(NH):
                P.tr(ptqb[0:QK, h * 128:(h + 1) * 128], qf[:, h, :], k.identb[:], [bqf, k.bidb], [bptq], inc=(h == NH - 1))
            P.cp('act', qtg[:, :, c0:c1], ptqb[0:QK, :].rearrange("p (h t) -> p h t", h=8), [bptq], [bqtg])
            for h in range(NH):
                P.tr(ptkb[0:QK, h * 128:(h + 1) * 128], kf[:, h, :], k.identb[:], [bkf, k.bidb], [bptk], inc=(h == NH - 1))
            P.cp('dve', ktg[:, :, c0:c1], ptkb[0:QK, :].rearrange("p (h t) -> p h t", h=8), [bptk], [bktg])
            if j == n - 1:
                P.dma('sp', QTv[:, :, tok0:tok0 + N], qtg[:, :, 0:N], bqtg, False)
                P.dma('sp', KTv[:, :, tok0:tok0 + N], ktg[:, :, 0:N], bktg, False)

        def all_tiles():
            for (t0, n) in groups:
                gs = {}
                for j in range(n):
                    yield tile_gen(t0, n, j, gs)
        interleave(all_tiles(), 3)
        P.barrier()


def phase_attn(k, l, QT, KT, VA, OT, groups):
    P, nc = k.P, k.nc
    T, TT = k.T, k.TT
    LA = 2
    with ExitStack() as es:
        def sb(name, shape, dt):
            return es.enter_context(nc.sbuf_tensor(U(name), shape, dt))
        Vall = sb('Vall', [128, T, NH * 65], BF16)
        VAv = VA.rearrange("(t p) e -> p t e", p=128)
        vcuts = [0, min(T, 2), min(T, 10), min(T, 20), T]
        bVs = []
        for ci in range(4):
            a, b_ = vcuts[ci], vcuts[ci + 1]
            bVs.append(P.buf('Vall%d' % ci))
            if b_ > a:
                P.dma('sp', Vall[:, a:b_, :], VAv[:, a:b_, :], bVs[ci], True)

        def bV_of(kt):
            for ci in range(4):
                if vcuts[ci] <= kt < vcuts[ci + 1]:
                    return bVs[ci]
        qhr = Rot(P, es, 'QTh', [QK, TT], BF16, 2)
        khr = Rot(P, es, 'KTh', [QK, TT], BF16, 2)
        pTr = Rot(P, es, 'pT', [128, 2, 512], BF16, LA + 2)
        osr = Rot(P, es, 'osb', [65, 512], F32, 3)
        rcr = Rot(P, es, 'rc', [65, 512], F32, 3)
        otr = Rot(P, es, 'ot', [64, 512], BF16, 3)
        obr = BankRot(k, (0, 1))
        pair_i = [0]
        tri = k.cst[:, C_TRI:C_TRI + 128]
        padb = k.cst[:, C_PADB:C_PADB + 1]
        heads = {}

        def load_head(h):
            qh, bqh = qhr.next()
            kh, bkh = khr.next()
            P.dma('sp', qh[:], QT[h], bqh, True)
            P.dma('sp', kh[:], KT[h], bkh, True)
            heads[h] = (qh, bqh, kh, bkh)

        items = []
        for h in range(NH):
            for (t0, n) in groups:
                kts = list(range(t0 + n))
                i = 0
                while i < len(kts):
                    kt = kts[i]
                    if 1 <= kt and kt + 1 < t0:
                        items.append(dict(h=h, t0=t0, n=n, kts=[kt, kt + 1]))
                        i += 2
                    else:
                        items.append(dict(h=h, t0=t0, n=n, kts=[kt]))
                        i += 1
        state = {}

        def emit_score(it):
            h, t0, n, kts = it['h'], it['t0'], it['n'], it['kts']
            if h not in heads:
                load_head(h)
            if kts[0] == 0 and t0 == 0 and h + 1 < NH and (h + 1) not in heads:
                load_head(h + 1)
            qh, bqh, kh, bkh = heads[h]
            N = n * 128
            tok0 = t0 * 128
            pi_ = 1 + (pair_i[0] % 3)
            pair_i[0] += 1
            pp = k.PS[pi_]
            (p0, bp0), (p1, bp1) = k.bank(2 * pi_), k.bank(2 * pi_ + 1)
            pT, bpT = pTr.next()
            if len(kts) == 2:
                for j, (pb_, bpb_) in enumerate(((p0, bp0), (p1, bp1))):
                    kt = kts[j]
                    P.mm(pb_[:, 0:N], kh[:, kt * 128:(kt + 1) * 128], qh[:, tok0:tok0 + N], True, True, [bkh, bqh], [bpb_])
                if N == 512:
                    P.act(pT[:].rearrange("p a b -> p (a b)"), pp[:, 0:1024], AF.Exp, [bp0, bp1], [bpT])
                else:
                    P.act(pT[:, :, 0:N], pp[:, :].rearrange("p (a b) -> p a b", a=2)[:, :, 0:N], AF.Exp, [bp0, bp1], [bpT])
                it['c0'] = 0
            else:
                kt = kts[0]
                c0 = max(kt - t0, 0) * 128
                P.mm(p0[:, 0:N - c0], kh[:, kt * 128:(kt + 1) * 128], qh[:, tok0 + c0:tok0 + N], True, True, [bkh, bqh], [bp0])
                if kt == 0:
                    P.act(pT[:, 0, 0:N - c0], p0[:, 0:N - c0], AF.Exp, [bp0, k.bcst], [bpT], bias=padb, scale=1.0)
                else:
                    P.act(pT[:, 0, 0:N - c0], p0[:, 0:N - c0], AF.Exp, [bp0], [bpT])
                if kt >= t0:
                    P.tt('dve', pT[:, 0, 0:128], pT[:, 0, 0:128], tri, ALU.mult, [bpT, k.bcst], [bpT])
                it['c0'] = c0
            it['pT'], it['bpT'], it['N'] = pT, bpT, N

        pending = []

        def emit_pv(it):
            h, t0, n, kts = it['h'], it['t0'], it['n'], it['kts']
            N, c0 = it['N'], it['c0']
            if kts[0] == 0:
                state['po'] = obr.next()
            po, bpo = state['po']
            for j, kt in enumerate(kts):
                P.mm(po[0:65, c0:N], Vall[:, kt, h * 65:(h + 1) * 65], it['pT'][:, j, 0:N - c0], kt == 0, kt == t0 + n - 1,
                     [bV_of(kt), it['bpT']], [bpo])
            if kts[-1] == t0 + n - 1:
                osb, bos = osr.next()
                P.cp('dve', osb[:, 0:N], po[0:65, 0:N], [bpo], [bos])
                rc, brc = rcr.next()
                P.ts('dve', rc[64:65, 0:N], osb[64:65, 0:N], 1e-30, None, ALU.add, None, [bos], [brc])
                P.op('dve', 'reciprocal', [brc], [brc], out=rc[64:65, 0:N], in_=rc[64:65, 0:N])
                pending.append(dict(h=h, t0=t0, N=N, osb=osb, bos=bos, rc=rc, brc=brc, age=0))

        def emit_final(f):
            N, tok0 = f['N'], f['t0'] * 128
            pi_ = 1 + (pair_i[0] % 3)
            pair_i[0] += 1
            pbc, bpbc = k.bank(2 * pi_)
            P.mm(pbc[0:64, 0:N], k.onesf[64:65, 0:64], f['rc'][64:65, 0:N], True, True, [k.bcst, f['brc']], [bpbc])
            ot, bot = otr.next()
            P.tt('dve', ot[:, 0:N], f['osb'][0:64, 0:N], pbc[0:64, 0:N], ALU.mult, [f['bos'], bpbc], [bot])
            P.dma('sp', OT[f['h'], :, tok0:tok0 + N], ot[:, 0:N], bot, False)

        for i in range(len(items) + LA):
            if i < len(items):
                emit_score(items[i])
            if i - LA >= 0:
                emit_pv(items[i - LA])
            for f in pending:
                f['age'] += 1
            while pending and pending[0]['age'] > 4:
                emit_final(pending.pop(0))
        while pending:
            emit_final(pending.pop(0))
        P.barrier()


def attn_out_load(k, es, w_in_l, w_ao):
    P, nc = k.P, k.nc

    def sb(name, shape, dt):
        return es.enter_context(nc.sbuf_tensor(U(name), shape, dt))
    Wao = sb('Wao', [128, 4, 1024], BF16); bWao = P.buf('Wao')
    load_w_cast(P, Wao, bWao, w_ao, 4, 1024, 0)
    Wgb = sb('Wgb', [128, 8, 1024], BF16); bWgb = P.buf('Wgb')
    load_w_cast(P, Wgb, bWgb, w_in_l, 8, 1024, O_GB)
    return dict(Wao=Wao, bWao=bWao, Wgb=Wgb, bWgb=bWgb)


def phase_attn_out(k, w, hT, OT, mixb, groups):
    P, nc = k.P, k.nc
    Wao, bWao, Wgb, bWgb = w['Wao'], w['bWao'], w['Wgb'], w['bWgb']
    with ExitStack() as es:
        hgr = Rot(P, es, 'hTg', [128, 8, 512], BF16, 2)
        ogr = Rot(P, es, 'OTg', [128, 4, 512], BF16, 2)
        gar = Rot(P, es, 'gb', [128, 512], F32, 2)
        mgr = Rot(P, es, 'mixg', [128, 8, 512], BF16, 2)
        hTv = hT.rearrange("(kc p) t -> p kc t", p=128)
        mixv = mixb.rearrange("(kc p) t -> p kc t", p=128)
        OTv = OT.rearrange("(hp h2) d t -> (h2 d) hp t", h2=2)
        br = BankRot(k, range(8))
        def loads(t0, n):
            N = n * 128
            tok0 = t0 * 128
            hg, bhg = hgr.next()
            P.dma('sp', hg[:, :, 0:N], hTv[:, :, tok0:tok0 + N], bhg, True)
            og, bog = ogr.next()
            P.dma('sp', og[:, :, 0:N], OTv[:, :, tok0:tok0 + N], bog, True)
            return hg, bhg, og, bog

        nxt = loads(*groups[0])
        for gi, (t0, n) in enumerate(groups):
            N = n * 128
            tok0 = t0 * 128
            hg, bhg, og, bog = nxt
            if gi + 1 < len(groups):
                nxt = loads(*groups[gi + 1])
            mg, bmg = mgr.next()
            for nn in range(8):
                py, bpy = br.next()
                pgt, bpgt = br.next()
                for hp in range(NH // 2):
                    P.mm(py[:, 0:N], Wao[:, hp, nn * 128:(nn + 1) * 128], og[:, hp, 0:N], hp == 0, hp == NH // 2 - 1, [bWao, bog], [bpy])
                for kc in range(8):
                    P.mm(pgt[:, 0:N], Wgb[:, kc, nn * 128:(nn + 1) * 128], hg[:, kc, 0:N], kc == 0, kc == 7, [bWgb, bhg], [bpgt])
                ga, bga = gar.next()
                P.act(ga[:, 0:N], pgt[:, 0:N], AF.Sigmoid, [bpgt], [bga])
                P.tt('dve', mg[:, nn, 0:N], py[:, 0:N], ga[:, 0:N], ALU.mult, [bpy, bga], [bmg])
            P.dma('sp', mixv[:, :, tok0:tok0 + N], mg[:, :, 0:N], bmg, False)
        P.barrier()


def hgrn_load(k, es, l, w_in_l, lb_logits, norm_g, w_ho):
    P, nc = k.P, k.nc
    assert DEPTH == 2
    if True:
        def sb(name, shape, dt):
            return es.enter_context(nc.sbuf_tensor(U(name), shape, dt))
        Whq = sb('Whq', [128, 8, 512], BF16); bWhq = P.buf('Whq')
        Whf = sb('Whf', [128, 8, 512], BF16); bWhf = P.buf('Whf')
        Whi = sb('Whi', [128, 8, 512], BF16); bWhi = P.buf('Whi')
        Whg = sb('Whg', [128, 8, 512], BF16); bWhg = P.buf('Whg')
        Wgc = sb('Wgc', [128, 8, 1024], BF16); bWgc = P.buf('Wgc')
        Who = sb('Who', [128, 4, 1024], BF16); bWho = P.buf('Who')
        load_w_cast(P, Whq, bWhq, w_in_l, 8, 512, O_HQ)
        load_w_cast(P, Whf, bWhf, w_in_l, 8, 512, O_HF)
        load_w_cast(P, Whi, bWhi, w_in_l, 8, 512, O_HI)
        load_w_cast(P, Whg, bWhg, w_in_l, 8, 512, O_HG)
        load_w_cast(P, Wgc, bWgc, w_in_l, 8, 1024, O_GC)
        load_w_cast(P, Who, bWho, w_ho, 4, 1024, 0)
        omlb = sb('omlb', [64, 512], F32); bomlb = P.buf('omlb')
        ocol = sb('ocol', [128, 8], F32); bocol = P.buf('ocol')
        P.dma('sp', ocol[:, 4:8], norm_g.rearrange("(h p) -> p h", p=128), bocol, True, allow_slow_non_contiguous=True)
        P.ts('dve', ocol[:, 4:8], ocol[:, 4:8], 0.125, None, ALU.mult, None, [bocol], [bocol])
        if l == 0:
            P.memset('pool', omlb[:], 0.5, [bomlb])
            P.memset('pool', ocol[:, 0:4], 0.5, [bocol])
        else:
            lt = sb('lbt', [64, 2, 512], F32); blt = P.buf('lbt')
            lc = sb('lbc', [128, 2, 4], F32); blc = P.buf('lbc')
            for r in range(2):
                P.dma('sp', lt[:, r, :], lb_logits[r:r + 1, :].to_broadcast([64, 512]), blt, True)
                P.dma('sp', lc[:, r, :], lb_logits[r].rearrange("(h p) -> p h", p=128), blc, True, allow_slow_non_contiguous=True)
            P.tt('dve', lt[:, 0, :], lt[:, 0, :], lt[:, 1, :], ALU.subtract, [blt], [blt])
            P.act(omlb[:], lt[:, 0, :], AF.Sigmoid, [blt], [bomlb])
            P.ts('dve', omlb[:], omlb[:], 0.5, None, ALU.mult, None, [bomlb], [bomlb])
            P.tt('dve', lc[:, 0, :], lc[:, 0, :], lc[:, 1, :], ALU.subtract, [blc], [blc])
            P.act(ocol[:, 0:4], lc[:, 0, :], AF.Sigmoid, [blc], [bocol])
            P.ts('dve', ocol[:, 0:4], ocol[:, 0:4], 0.5, None, ALU.mult, None, [bocol], [bocol])
    return dict(Whq=Whq, bWhq=bWhq, Whf=Whf, bWhf=bWhf, Whi=Whi, bWhi=bWhi, Whg=Whg, bWhg=bWhg, Wgc=Wgc, bWgc=bWgc,
                Who=Who, bWho=bWho, omlb=omlb, bomlb=bomlb, ocol=ocol, bocol=bocol)


def phase_hgrn(k, w, hT, mixc, groups):
    P, nc = k.P, k.nc
    Whq, bWhq, Whf, bWhf, Whi, bWhi, Whg, bWhg = (w[n] for n in ('Whq', 'bWhq', 'Whf', 'bWhf', 'Whi', 'bWhi', 'Whg', 'bWhg'))
    Wgc, bWgc, Who, bWho, omlb, bomlb, ocol, bocol = (w[n] for n in ('Wgc', 'bWgc', 'Who', 'bWho', 'omlb', 'bomlb', 'ocol', 'bocol'))
    with ExitStack() as es:
        def sb(name, shape, dt):
            return es.enter_context(nc.sbuf_tensor(U(name), shape, dt))
        Lm = k.cst[0:64, C_L:C_L + 64]
        UMa = k.cst[0:64, C_UM:C_UM + 64]
        UMb = k.cst[0:64, C_UM + 64:C_UM + 66]
        tri8 = k.cst[0:64, C_TRI8:C_TRI8 + 256]

        S = sb('S', [128, HH, 128], F32); bS = P.buf('S')
        St = sb('St', [128, HH, 128], F32); bSt = P.buf('St')
        P.memset('pool', S[:], 0.0, [bS])
        GN = 256
        NC_ = GN // 64
        hgr = Rot(P, es, 'hTg', [128, 8, GN], BF16, 3)
        qTr = Rot(P, es, 'hqT', [128, HH, GN], BF16, 2)
        kTsr = Rot(P, es, 'hkT', [128, HH, GN], BF16, 2)
        sgTr = Rot(P, es, 'hsgT', [128, HH, GN], BF16, 2)
        lfr = Rot(P, es, 'lf', [64, NC_, 512], F32, 2)
        kdr = Rot(P, es, 'kd', [64, NC_, 512], BF16, 2)
        vr = Rot(P, es, 'hv', [64, NC_, 512], BF16, 2)
        qer = Rot(P, es, 'qe', [128, HH, GN], BF16, 2)
        ker = Rot(P, es, 'ke', [128, HH, GN], BF16, 2)
        ktmr = Rot(P, es, 'ktm', [64, 512], F32, 2)
        kclr = Rot(P, es, 'kcl', [64, 512], F32, 2)
        thir = Rot(P, es, 'thi', [64, 512], F32, 2)
        erbr = Rot(P, es, 'erb', [64, 512], BF16, 2)
        scr = Rot(P, es, 'hscr', [128, GN], F32, 4)
        thr = epr = emr = osqr = rsr = onr = scr
        exsr = Rot(P, es, 'exs', [128, NC_, 2, HH], F32, 2)
        ATr = Rot(P, es, 'ATa', [64, NC_, 256], BF16, 2)
        Mr = Rot(P, es, 'Ma', [128, NC_, 512], F32, 2)
        Sbr = Rot(P, es, 'Sba', [128, NC_, 512], BF16, 2)
        ogr = Rot(P, es, 'og', [128, HH, GN], BF16, 2)
        gar = Rot(P, es, 'gc', [128, GN], F32, 2)
        mgr = Rot(P, es, 'mixg', [128, 8, GN], BF16, 2)
        hTv = hT.rearrange("(kc p) t -> p kc t", p=128)
        mixv = mixc.rearrange("(kc p) t -> p kc t", p=128)

        hgroups = groups_of(k.T, 2)

        def load_hg(t0, n):
            hg, bhg = hgr.next()
            P.dma('sp', hg[:, :, 0:n * 128], hTv[:, :, t0 * 128:(t0 + n) * 128], bhg, True)
            return hg, bhg
        pf = Prefetch(hgroups, load_hg)

        def group_gen(gi, t0, n):
            N = n * 128
            nch = N // 64
            tok0 = t0 * 128
            hg, bhg = pf.get(gi)
            qT, bqT = qTr.next(); kTs, bkTs = kTsr.next(); sgT, bsgT = sgTr.next()
            lf_all, blf = lfr.next(); kd_all, bkd = kdr.next(); v_all, bv = vr.next()
            qe, bqe = qer.next(); ke, bke = ker.next(); exs, bexs = exsr.next()
            AT_all, bAT = ATr.next(); M_all, bM = Mr.next(); Sb_all, bSb = Sbr.next(); og, bog = ogr.next()
            br1 = BankRot(k, (0, 1, 2, 3))
            for h in range(HH):
                hsl = slice(h * 128, (h + 1) * 128)
                pb, bpb = br1.next()
                for kc in range(8):
                    P.mm(pb[:, 0:N], Whq[:, kc, hsl], hg[:, kc, 0:N], kc == 0, kc == 7, [bWhq, bhg], [bpb])
                P.cp('act', qT[:, h, 0:N], pb[:, 0:N], [bpb], [bqT])
                pb, bpb = br1.next()
                for kc in range(8):
                    P.mm(pb[:, 0:N], Whf[:, kc, hsl], hg[:, kc, 0:N], kc == 0, kc == 7, [bWhf, bhg], [bpb])
                th, bth = thr.next()
                P.act(th[:, 0:N], pb[:, 0:N], AF.Tanh, [bpb], [bth], scale=0.5)
                P.ts('dve', kTs[:, h, 0:N], th[:, 0:N], -1.0, 1.0, ALU.mult, ALU.add, [bth], [bkTs])
                pb, bpb = br1.next()
                for kc in range(8):
                    P.mm(pb[:, 0:N], Whg[:, kc, hsl], hg[:, kc, 0:N], kc == 0, kc == 7, [bWhg, bhg], [bpb])
                th, bth = thr.next()
                P.act(th[:, 0:N], pb[:, 0:N], AF.Tanh, [bpb], [bth], scale=0.5)
                P.stt(sgT[:, h, 0:N], th[:, 0:N], 1.0, pb[:, 0:N], ALU.add, ALU.mult, [bth, bpb], [bsgT])
            yield
            brf = BankRot(k, (4, 5))
            bri = BankRot(k, (6, 7))
            brr = BankRot(k, (2, 3))
            st2 = {}

            def s2_mm(c):
                pf, bpf = brf.next()
                pi, bpi = bri.next()
                for kc in range(8):
                    P.mm(pf[0:64, 0:512], hg[:, kc, c * 64:(c + 1) * 64], Whf[:, kc, :], kc == 0, kc == 7, [bhg, bWhf], [bpf])
                for kc in range(8):
                    P.mm(pi[0:64, 0:512], hg[:, kc, c * 64:(c + 1) * 64], Whi[:, kc, :], kc == 0, kc == 7, [bhg, bWhi], [bpi])
                ktm, bktm = ktmr.next()
                thi, bthi = thir.next()
                P.act(ktm[:], pf[0:64, 0:512], AF.Tanh, [bpf], [bktm], scale=0.5)
                P.act(thi[:], pi[0:64, 0:512], AF.Tanh, [bpi], [bthi], scale=0.5)
                P.ts('dve', ktm[:], ktm[:], -1.0, 1.0, ALU.mult, ALU.add, [bktm], [bktm])
                P.tt('dve', ktm[:], ktm[:], omlb[:], ALU.mult, [bktm, bomlb], [bktm])
                P.stt(v_all[:, c, :], thi[:], 1.0, pi[0:64, 0:512], ALU.add, ALU.mult, [bthi, bpi], [bv])
                kcl, bkcl = kclr.next()
                P.ts('dve', kcl[:], ktm[:], CLAMP, None, ALU.min, None, [bktm], [bkcl])
                st2[c] = (ktm, bktm, kcl, bkcl)

            def s2_ln(c):
                ktm, bktm, kcl, bkcl = st2[c]
                P.act(lf_all[:, c, :], kcl[:], AF.Ln, [bkcl], [blf], scale=-1.0, bias=1.0)

            def s2_back(c):
                ktm, bktm, kcl, bkcl = st2.pop(c)
                prb, bprb = brr.next()
                P.mm(prb[0:64, 0:512], Lm, lf_all[:, c, :], True, True, [k.bcst, blf], [bprb])
                erb, berb = erbr.next()
                P.act(erb[:], prb[0:64, 0:512], AF.Exp, [bprb], [berb])
                P.tt('dve', kd_all[:, c, :], ktm[:], erb[:], ALU.mult, [bktm, berb], [bkd])

            for c2 in range(0, nch, 2):
                s2_mm(c2)
                s2_mm(c2 + 1)
                s2_ln(c2)
                s2_ln(c2 + 1)
                s2_back(c2)
                s2_back(c2 + 1)
            yield
            br3 = BankRot(k, (0, 1))
            exv = exs[:].rearrange("p c j h -> p h c j")
            for h in range(HH):
                pbm, bpbm = br3.next()
                pex, bpex = brr.next()
                for c in range(nch):
                    P.mm(pbm[:, c * 64:(c + 1) * 64], lf_all[:, c, h * 128:(h + 1) * 128], UMa, True, True, [blf, k.bcst], [bpbm],
                         inc=(c == nch - 1))
                for c in range(nch):
                    P.mm(pex[:, c * 2:c * 2 + 2], lf_all[:, c, h * 128:(h + 1) * 128], UMb, True, True,
                         [blf, k.bcst], [bpex], inc=(c == nch - 1))
                ep, bep = epr.next()
                em, bem = emr.next()
                P.act(ep[:, 0:N], pbm[:, 0:N], AF.Exp, [bpbm], [bep])
                P.act(em[:, 0:N], pbm[:, 0:N], AF.Exp, [bpbm], [bem], scale=-1.0)
                P.tt('dve', qe[:, h, 0:N], qT[:, h, 0:N], ep[:, 0:N], ALU.mult, [bqT, bep], [bqe])
                P.stt(ke[:, h, 0:N], kTs[:, h, 0:N], ocol[:, h:h + 1], em[:, 0:N], ALU.mult, ALU.mult, [bkTs, bocol, bem], [bke])
                P.act(exv[:, h, 0:nch, :], pex[:, 0:nch * 2].rearrange("p (c j) -> p c j", j=2), AF.Exp, [bpex], [bexs])
            yield
            brA = BankRot(k, (4, 5))
            brM = BankRot(k, (6, 7))
            for c in range(nch):
                cs_ = slice(c * 64, (c + 1) * 64)
                pA, bpA = brA.next()
                for h in range(HH):
                    P.mm(pA[0:64, h * 64:(h + 1) * 64], ke[:, h, cs_], qe[:, h, cs_], True, True, [bke, bqe], [bpA], inc=(h == HH - 1))
                P.tt('dve', AT_all[:, c, :], pA[0:64, 0:256], tri8, ALU.mult, [bpA, k.bcst], [bAT])
                pM, bpM_ = brM.next()
                for h in range(HH):
                    hs = slice(h * 128, (h + 1) * 128)
                    P.mm(pM[:, hs], kd_all[:, c, hs], v_all[:, c, hs], True, True, [bkd, bv], [bpM_], inc=(h == HH - 1))
                P.cp('act', M_all[:, c, :], pM[:, 0:512], [bpM_], [bM])
            yield
            S3 = S[:]
            for c in range(nch):
                e_mid = exs[:, c, 0, :].unsqueeze(2).to_broadcast([128, HH, 128])
                e_last = exs[:, c, 1, :].unsqueeze(2).to_broadcast([128, HH, 128])
                P.tt('pool', Sb_all[:, c, :].rearrange("p (h d) -> p h d", h=HH), S3, e_mid, ALU.mult, [bS, bexs], [bSb])
                P.tt('dve', St[:], S3, e_last, ALU.mult, [bS, bexs], [bSt])
                P.tt('dve', S3, St[:], M_all[:, c, :].rearrange("p (h d) -> p h d", h=HH), ALU.add, [bSt, bM], [bS])
            yield
            bpo = [k.bank(h) for h in range(HH)]
            for c in range(nch):
                cs_ = slice(c * 64, (c + 1) * 64)
                for h in range(HH):
                    po, bpo_h = bpo[h]
                    hs = slice(h * 128, (h + 1) * 128)
                    P.mm(po[:, cs_], v_all[:, c, hs], AT_all[:, c, h * 64:(h + 1) * 64], True, False, [bv, bAT], [bpo_h], inc=False)
                    P.mm(po[:, cs_], Sb_all[:, c, hs], qe[:, h, cs_], False, True, [bSb, bqe], [bpo_h], inc=True)
            for h in range(HH):
                po, bpo_h = bpo[h]
                osq, bosq = osqr.next()
                P.act(osq[:, 0:N], po[:, 0:N], AF.Square, [bpo_h], [bosq])
                pms, bpms = brA.next()
                P.mm(pms[:, 0:N], k.onesf, osq[:, 0:N], True, True, [k.bcst, bosq], [bpms])
                rs, brs = rsr.next()
                P.act(rs[:, 0:N], pms[:, 0:N], AF.Ln, [bpms], [brs], scale=0.25 / 128, bias=EPS)
                P.act(rs[:, 0:N], rs[:, 0:N], AF.Exp, [brs], [brs], scale=-0.5)
                on, bon = onr.next()
                P.tt('dve', on[:, 0:N], po[:, 0:N], rs[:, 0:N], ALU.mult, [bpo_h, brs], [bon])
                P.stt(og[:, h, 0:N], on[:, 0:N], ocol[:, 4 + h:5 + h], sgT[:, h, 0:N], ALU.mult, ALU.mult, [bon, bocol, bsgT], [bog])
            yield
            mg, bmg = mgr.next()
            br6 = BankRot(k, (4, 5, 6, 7))
            for nn in range(8):
                py, bpy = br6.next()
                pgt, bpgt = br6.next()
                for h in range(HH):
                    P.mm(py[:, 0:N], Who[:, h, nn * 128:(nn + 1) * 128], og[:, h, 0:N], h == 0, h == HH - 1, [bWho, bog], [bpy])
                for kc in range(8):
                    P.mm(pgt[:, 0:N], Wgc[:, kc, nn * 128:(nn + 1) * 128], hg[:, kc, 0:N], kc == 0, kc == 7, [bWgc, bhg], [bpgt])
                ga, bga = gar.next()
                P.act(ga[:, 0:N], pgt[:, 0:N], AF.Tanh, [bpgt], [bga], scale=0.5)
                P.stt(mg[:, nn, 0:N], ga[:, 0:N], 1.0, py[:, 0:N], ALU.add, ALU.mult, [bga, bpy], [bmg])
            P.dma('sp', mixv[:, :, tok0:tok0 + N], mg[:, :, 0:N], bmg, False)

        interleave((group_gen(gi, t0, n) for gi, (t0, n) in enumerate(hgroups)), 2)
        P.barrier()


def phase_merge(k, l, xres, hT, mixa, mixb, mixc, w_out_l, norm2_g, groups, after_loads=None):
    P, nc = k.P, k.nc
    with ExitStack() as es:
        def sb(name, shape, dt):
            return es.enter_context(nc.sbuf_tensor(U(name), shape, dt))
        Wo = sb('Wo', [128, 8, 1024], BF16); bWo = P.buf('Wo')
        load_w_cast(P, Wo, bWo, w_out_l, 8, 1024, 0)
        gbc = sb('g2bc', [128, D], F32); bg = P.buf('g2bc')
        P.dma('sp', gbc[:], norm2_g[l:l + 1, :].to_broadcast([128, D]), bg, True)
        if after_loads is not None:
            after_loads()
        groups = groups_of(k.T, 2)
        mar = Rot(P, es, 'ma', [128, 8, 256], BF16, 3)
        mbr = Rot(P, es, 'mb', [128, 8, 256], BF16, 3)
        mcr = Rot(P, es, 'mc', [128, 8, 256], BF16, 3)
        tmp = sb('mtmp', [128, 8, 256], F32); btmp = P.buf('mtmp')
        mxr = Rot(P, es, 'mx', [128, 8, 256], BF16, 2)
        xr = Rot(P, es, 'x', [128, D], F32, 6)
        x1r = Rot(P, es, 'x1', [128, D], F32, 4)
        hgr = Rot(P, es, 'h2g', [128, 8, 256], BF16, 2)
        nt = NormTools(k, es, banks=(0, 1))
        hTv = hT.rearrange("(kc p) t -> p kc t", p=128)
        views = [m.rearrange("(kc p) t -> p kc t", p=128) for m in (mixa, mixb, mixc)]
        pair = [0]

        def load_mix(t0, n):
            N = n * 128
            tok0 = t0 * 128
            ma, bma = mar.next()
            mb, bmb = mbr.next()
            mc, bmc = mcr.next()
            for (mt, bm, v) in ((ma, bma, views[0]), (mb, bmb, views[1]), (mc, bmc, views[2])):
                P.dma('sp', mt[:, :, 0:N], v[:, :, tok0:tok0 + N], bm, True)
            return ma, bma, mb, bmb, mc, bmc
        pf = Prefetch(groups, load_mix)

        def group_prep(t0, n):
            N = n * 128
            tok0 = t0 * 128
            ma, bma, mb, bmb, mc, bmc = pf.get(groups.index((t0, n)))
            P.tt('dve', tmp[:, :, 0:N], ma[:, :, 0:N], mb[:, :, 0:N], ALU.add, [bma, bmb], [btmp])
            mx, bmx = mxr.next()
            P.tt('dve', mx[:, :, 0:N], tmp[:, :, 0:N], mc[:, :, 0:N], ALU.add, [btmp, bmc], [bmx])
            return mx, bmx

        def load_x(t):
            xt, bx = xr.next()
            P.dma('sp', xt[:], k.xsrc(l, t), bx, True)
            return xt, bx
        pfx = Prefetch([(t,) for t in range(k.T)], load_x, ahead=2)

        def tile_gen(t0, n, j, gs):
            t = t0 + j
            if j == 0:
                gs['mx'] = group_prep(t0, n)
                gs['hg'] = hgr.next()
            mx, bmx = gs['mx']
            hg, bhg = gs['hg']
            xt, bx = pfx.get(t)
            pi_ = 1 + (pair[0] % 3)
            pair[0] += 1
            pd = k.PS[pi_]
            (p0, bp0), (p1, bp1) = k.bank(2 * pi_), k.bank(2 * pi_ + 1)
            for kc in range(8):
                P.mm(p0[:, 0:512], mx[:, kc, j * 128:(j + 1) * 128], Wo[:, kc, 0:512], kc == 0, kc == 7, [bmx, bWo], [bp0])
            for kc in range(8):
                P.mm(p1[:, 0:512], mx[:, kc, j * 128:(j + 1) * 128], Wo[:, kc, 512:1024], kc == 0, kc == 7, [bmx, bWo], [bp1])
            yield
            x1, bx1 = x1r.next()
            P.tt('dve', x1[:], pd[:, 0:1024], xt[:], ALU.add, [bp0, bp1, bx], [bx1])
            if t == 0:
                P.dma('sp', xres[PAD:128, :], x1[PAD:128, :], bx1, False)
            else:
                P.dma('sp', xres[t * 128:(t + 1) * 128, :], x1[:], bx1, False)
            yield from nt.norm_tile_gen(x1, bx1, gbc, bg, hg, bhg, j)
            if j == n - 1:
                P.dma('sp', hTv[:, :, t0 * 128:(t0 + n) * 128], hg[:, :, 0:n * 128], bhg, False)

        def all_tiles():
            for (t0, n) in groups:
                gs = {}
                for j in range(n):
                    yield tile_gen(t0, n, j, gs)
        interleave(all_tiles(), 3)
        P.barrier()


def ffn_load_w1(k, es, w1):
    P, nc = k.P, k.nc
    W1 = es.enter_context(nc.sbuf_tensor(U('W1'), [128, 8, 4096], BF16))
    bW1q = [P.buf('W1q%d' % q) for q in range(4)]
    w1v = w1.rearrange("(kc p) n -> p kc n", p=128)

    def issue():
        for q in range(4):
            P.dma('pool', W1[:, :, q * 1024:(q + 1) * 1024], w1v[:, :, q * 1024:(q + 1) * 1024], bW1q[q], True)
    return W1, bW1q, issue


def phase_ffn(k, l, xres, hT, out, w1pre, w2, last):
    P, nc = k.P, k.nc
    T = k.T
    W1, bW1q, _ = w1pre
    with ExitStack() as es:
        def sb(name, shape, dt):
            return es.enter_context(nc.sbuf_tensor(U(name), shape, dt))
        W2 = sb('W2', [128, 32, 1024], BF16); bW2 = P.buf('W2')
        load_w_cast(P, W2, bW2, w2, 32, 1024, 0)
        hgr = Rot(P, es, 'h2g', [128, 8, 256], BF16, 3)
        fTr = Rot(P, es, 'fT', [128, 32, 256], BF16, 2)
        rr = Rot(P, es, 'frl', [128, 256], F32, 3)
        xr = Rot(P, es, 'x1', [128, D], F32, 2)
        xor_ = Rot(P, es, 'xo', [128, D], F32, 2)
        hTv = hT.rearrange("(kc p) t -> p kc t", p=128)
        br = BankRot(k, (0, 1, 2, 3))
        pair = [0]
        fgroups = [(t0, n) for (t0, n) in groups_of(T, 2) if not (last and t0 == 0)]

        def load_hg(t0, n):
            hg, bhg = hgr.next()
            P.dma('sp', hg[:, :, 0:n * 128], hTv[:, :, t0 * 128:(t0 + n) * 128], bhg, True)
            return hg, bhg
        pf = Prefetch(fgroups, load_hg)

        def ffn1(t0, n):
            N = n * 128
            tok0 = t0 * 128
            hg, bhg = pf.get(fgroups.index((t0, n)))
            fT, bfT = fTr.next()
            for fc in range(32):
                pb, bpb = br.next()
                for kc in range(8):
                    P.mm(pb[:, 0:N], W1[:, kc, fc * 128:(fc + 1) * 128], hg[:, kc, 0:N], kc == 0, kc == 7, [bW1q[fc // 8], bhg], [bpb])
                r, brl = rr.next()
                P.act(r[:, 0:N], pb[:, 0:N], AF.Relu, [bpb], [brl])
                P.tt('dve', fT[:, fc, 0:N], r[:, 0:N], r[:, 0:N], ALU.mult, [brl], [bfT])
            return fT, bfT

        def ffn2(t0, n, fT, bfT):
            for j in range(n):
                t = t0 + j
                xt, bx = xr.next()
                P.dma('sp', xt[:], xres[t * 128:(t + 1) * 128, :], bx, True)
                pi_ = 2 + (pair[0] % 2)
                pair[0] += 1
                pd = k.PS[pi_]
                (p0, bp0), (p1, bp1) = k.bank(2 * pi_), k.bank(2 * pi_ + 1)
                for fc in range(32):
                    P.mm(p0[:, 0:512], fT[:, fc, j * 128:(j + 1) * 128], W2[:, fc, 0:512], fc == 0, fc == 31, [bfT, bW2], [bp0])
                    P.mm(p1[:, 0:512], fT[:, fc, j * 128:(j + 1) * 128], W2[:, fc, 512:1024], fc == 0, fc == 31, [bfT, bW2], [bp1])
                xo, bxo = xor_.next()
                P.tt('dve', xo[:], pd[:, 0:1024], xt[:], ALU.add, [bp0, bp1, bx], [bxo])
                if last:
                    P.dma('sp', out[(t - 1) * 128:t * 128, :], xo[:], bxo, False)
                elif t == 0:
                    P.dma('sp', xres[PAD:128, :], xo[PAD:128, :], bxo, False)
                else:
                    P.dma('sp', xres[t * 128:(t + 1) * 128, :], xo[:], bxo, False)

        prev = None
        for (t0, n) in fgroups:
            cur = (t0, n) + ffn1(t0, n)
            if prev is not None:
                ffn2(*prev)
            prev = cur
        ffn2(*prev)
        P.barrier()

C_IDENT = 0
C_ONES = 128
C_TRI = 256
C_L = 384
C_UM = 448
C_PADB = 514
C_TRI8 = 515
NCONST = C_TRI8 + 512


def make_consts():
    c = np.zeros((128, NCONST), np.float32)
    c[:, C_IDENT:C_IDENT + 128] = np.eye(128, dtype=np.float32)
    c[:, C_ONES:C_ONES + 128] = 1.0
    s = np.arange(128)[:, None]
    t = np.arange(128)[None, :]
    c[:, C_TRI:C_TRI + 128] = (s <= t).astype(np.float32)
    s64 = np.arange(64)[:, None]
    t64 = np.arange(64)[None, :]
    c[:64, C_L:C_L + 64] = (s64 > t64).astype(np.float32)
    c[:64, C_UM:C_UM + 64] = (s64 <= t64).astype(np.float32) - (s64 <= 31).astype(np.float32)
    c[:64, C_UM + 64] = (np.arange(64) <= 31).astype(np.float32)
    c[:64, C_UM + 65] = 1.0
    c[:PAD, C_PADB] = -30000.0
    c[:64, C_TRI8:C_TRI8 + 512] = np.tile((s64 <= t64).astype(np.float32), (1, 8))
    return c


def make_cs_tab(T):
    half = 16
    pos = (np.arange(T * 128, dtype=np.float32) - np.float32(PAD)).astype(np.float32)
    inv_freq = (np.float32(10000.0) ** (-np.arange(half, dtype=np.float32) / np.float32(half))).astype(np.float32)
    ang = (pos[:, None] * inv_freq[None, :]).astype(np.float32)
    cs = np.concatenate([np.cos(ang), np.sin(ang)], axis=1).astype(np.float32)
    return np.ascontiguousarray(cs.reshape(T, 128, 32).transpose(1, 0, 2).reshape(128, T * 32))


_W_NAMES = ['meta', 'norm1_g', 'w_in', 'conv_w', 'conv_b', 'conv_ln_g', 'conv_ln_b', 'w_conv_out',
            'q_a_norm_g', 'w_uq', 'kv_a_norm_g', 'w_ukv', 'q_norm_g', 'k_norm_g', 'w_attn_out',
            'hgrn_lb_logits', 'hgrn_norm_g', 'w_hgrn_out', 'w_out', 'norm2_g', 'w_ff1', 'w_ff2']


def kernel(**inputs):
    x = np.ascontiguousarray(inputs['x'], dtype=np.float32)
    B, SEQ, _ = x.shape
    T = 1 + SEQ // 128
    nc = build(T)
    shared = {n: np.ascontiguousarray(inputs[n], dtype=np.float32) for n in _W_NAMES}
    shared['consts'] = make_consts()
    shared['cs_tab'] = make_cs_tab(T)
    in_maps = []
    for b in range(B):
        m = dict(shared)
        m['x'] = x[b]
        in_maps.append(m)
    res = run_bass_kernel_spmd(nc, in_maps, core_ids=list(range(B)))
    return np.stack([np.asarray(r['out']) for r in res.results], axis=0).astype(np.float32)
```

```python
import numpy as np
import concourse.bass as bass
import concourse.mybir as mybir
from concourse.bass_utils import run_bass_kernel_spmd
from contextlib import ExitStack

F32 = mybir.dt.float32
BF16 = mybir.dt.bfloat16
ALU = mybir.AluOpType
AF = mybir.ActivationFunctionType
AX = mybir.AxisListType

D = 1024
DEPTH = 2
N_META = 16
PAD = 112
EPS = 1e-6
CLAMP = 1.0 - 1e-6
N_IN = 6560
O_CONV, O_CQ, O_CKV, O_KR, O_HQ, O_HF, O_HI, O_HG, O_GA, O_GB, O_GC = (
    0, 1024, 1280, 1408, 1440, 1952, 2464, 2976, 3488, 4512, 5536)
CONV_K = 31
NH = 8
QK = 96
HH = 4

CENGS = ['pe', 'act', 'dve', 'pool']
ENGS = CENGS + ['sp']


class Buf:
    __slots__ = ('name', 'last_w', 'readers', 'dsem', 'dcnt')

    def __init__(self, name):
        self.name = name
        self.last_w = None
        self.readers = {}
        self.dsem = None
        self.dcnt = 0


class Prog:
    def __init__(self, nc, es):
        self.nc = nc
        self.es = es
        self.ops = {e: [] for e in ENGS}
        self.cnt = {e: 0 for e in CENGS}
        self.sem = {e: es.enter_context(nc.semaphore('c_' + e)) for e in CENGS}
        self.known = {e: {e2: 0 for e2 in CENGS} for e in ENGS}
        self.dknown = {e: {} for e in ENGS}
        self.dbufs = []
        self.dsem_free = {'sp': [], 'pool': []}
        self.dsem_q = {}
        self.dsem_tot = {}
        self.nbuf = 0

    def buf(self, name=None):
        self.nbuf += 1
        return Buf('%s_%d' % (name or 'b', self.nbuf))

    def _wait_eng(self, e, e2, idx):
        if e == 'pe' and e2 == 'pe':
            return
        if self.known[e][e2] >= idx:
            return
        self.known[e][e2] = idx
        sem = self.sem[e2]
        self.ops[e].append(lambda eng: eng.wait_ge(sem, idx))

    def _wait_dma(self, e, b):
        if b.dcnt == 0:
            return
        if self.dknown[e].get(b, 0) >= b.dcnt:
            return
        self.dknown[e][b] = b.dcnt
        sem, val = b.dsem, 16 * self.dsem_tot[id(b.dsem)]
        self.ops[e].append(lambda eng: eng.wait_ge(sem, val))

    def _deps(self, e, reads, writes):
        for b in reads:
            self._wait_dma(e, b)
            if b.last_w is not None:
                self._wait_eng(e, *b.last_w)
        for b in writes:
            self._wait_dma(e, b)
            if b.last_w is not None:
                self._wait_eng(e, *b.last_w)
            for e2, idx in b.readers.items():
                self._wait_eng(e, e2, idx)

    def op(self, e, name, reads=(), writes=(), inc=True, **kw):
        self._deps(e, reads, writes)
        if inc:
            self.cnt[e] += 1
            idx = self.cnt[e]
            sem = self.sem[e]
            self.ops[e].append(lambda eng: getattr(eng, name)(**kw).then_inc(sem, 1))
        else:
            idx = self.cnt[e] + 1
            self.ops[e].append(lambda eng: getattr(eng, name)(**kw))
        for b in writes:
            b.last_w = (e, idx)
            b.readers = {}
        for b in reads:
            b.readers[e] = idx

    def dma(self, e, out, in_, b, load, **kw):
        if b.dsem is None:
            if self.dsem_free[e]:
                b.dsem = self.dsem_free[e].pop()
            else:
                b.dsem = self.es.enter_context(self.nc.semaphore('d%d' % len(self.dsem_tot)))
                self.dsem_tot[id(b.dsem)] = 0
            b.dcnt = 0
            self.dsem_q[id(b.dsem)] = e
            self.dbufs.append(b)
        assert self.dsem_q[id(b.dsem)] == e, 'a DMA semaphore must stay on one queue'
        if load:
            self._deps(e, (), (b,))
        else:
            self._deps(e, (b,), ())
        b.dcnt += 1
        self.dsem_tot[id(b.dsem)] += 1
        sem = b.dsem
        self.ops[e].append(lambda eng: eng.dma_start(out=out, in_=in_, **kw).then_inc(sem, 16))
        if load:
            b.last_w = None
            b.readers = {}

    def barrier(self):
        for e in ENGS:
            for e2 in CENGS:
                if e2 != e and self.cnt[e2] > 0:
                    self._wait_eng(e, e2, self.cnt[e2])
            for b in self.dbufs:
                self._wait_dma(e, b)
        for b in self.dbufs:
            if self.dsem_q[id(b.dsem)] == 'sp':
                self.dsem_free['sp'].append(b.dsem)
            b.dsem = None
            b.dcnt = 0
            for e in ENGS:
                self.dknown[e].pop(b, None)
        self.dbufs = []

    def mm(self, out, lhsT, rhs, start, stop, reads, writes, inc=None):
        self.op('pe', 'matmul', reads, writes, inc=(stop if inc is None else inc),
                out=out, lhsT=lhsT, rhs=rhs, start=start, stop=stop)

    def tr(self, out, in_, identity, reads, writes, inc=True):
        self.op('pe', 'transpose', reads, writes, inc=inc, out=out, in_=in_, identity=identity)

    def act(self, out, in_, func, reads, writes, **kw):
        self.op('act', 'activation', reads, writes, out=out, in_=in_, func=func, **kw)

    def tt(self, e, out, in0, in1, op, reads, writes):
        self.op(e, 'tensor_tensor', reads, writes, out=out, in0=in0, in1=in1, op=op)

    def ts(self, e, out, in0, s1, s2, op0, op1, reads, writes):
        if op1 is None:
            self.op(e, 'tensor_scalar', reads, writes, out=out, in0=in0, scalar1=s1, scalar2=None, op0=op0)
        else:
            self.op(e, 'tensor_scalar', reads, writes, out=out, in0=in0, scalar1=s1, scalar2=s2, op0=op0, op1=op1)

    def stt(self, out, in0, scalar, in1, op0, op1, reads, writes):
        self.op('dve', 'scalar_tensor_tensor', reads, writes, out=out, in0=in0, scalar=scalar, in1=in1, op0=op0, op1=op1)

    def cp(self, e, out, in_, reads, writes):
        if e == 'act':
            self.op('act', 'copy', reads, writes, out=out, in_=in_)
        else:
            self.op(e, 'tensor_copy', reads, writes, out=out, in_=in_)

    def rsqrt(self, out, in_, scale, eps, reads, writes):
        self.act(out, in_, AF.Sqrt, reads, writes, scale=scale, bias=self.eps_ap(eps, out))
        self.op('dve', 'reciprocal', list(writes), list(writes), out=out, in_=out)

    def eps_ap(self, eps, like):
        return eps

    def memset(self, e, ap, val, writes):
        self.op(e, 'memset', (), writes, ap=ap, constant=val)

    def emit(self):
        self.barrier()
        nc = self.nc
        with nc.Block() as block:
            @block.tensor
            def _(eng):
                for f in self.ops['pe']:
                    f(eng)

            @block.scalar
            def _(eng):
                for f in self.ops['act']:
                    f(eng)

            @block.vector
            def _(eng):
                for f in self.ops['dve']:
                    f(eng)

            @block.gpsimd
            def _(eng):
                for f in self.ops['pool']:
                    f(eng)

            @block.sync
            def _(eng):
                for f in self.ops['sp']:
                    f(eng)


_UID = [0]


def U(name):
    _UID[0] += 1
    return '%s_u%d' % (name, _UID[0])


class Rot:
    def __init__(self, P, es, name, shape, dtype, n):
        self.tiles = [es.enter_context(P.nc.sbuf_tensor(U('%s%d' % (name, i)), shape, dtype)) for i in range(n)]
        self.bufs = [P.buf('%s%d' % (name, i)) for i in range(n)]
        self.i = 0

    def next(self):
        k = self.i % len(self.tiles)
        self.i += 1
        return self.tiles[k], self.bufs[k]


def groups_of(T, gsz=4):
    gs = [(0, 1)]
    t = 1
    while t < T:
        n = min(gsz, T - t)
        gs.append((t, n))
        t += n
    return gs


class K:
    pass


def load_w_cast(P, dst, dst_buf, src, K_chunks, ncols, col0=0, pk=128):
    v = src.rearrange("(kc p) n -> p kc n", p=pk)
    for k0 in range(0, K_chunks, 8):
        k1 = min(K_chunks, k0 + 8)
        c = 0
        while c < ncols:
            w = min(2048, ncols - c)
            P.dma('pool', dst[:, k0:k1, c:c + w], v[:, k0:k1, col0 + c:col0 + c + w], dst_buf, True)
            c += w


class Prefetch:
    def __init__(self, items, load_fn, ahead=1):
        self.items, self.load_fn, self.ahead = list(items), load_fn, ahead
        self.loaded = {}
        self.next = 0

    def get(self, i):
        while self.next < len(self.items) and self.next <= i + self.ahead:
            self.loaded[self.next] = self.load_fn(*self.items[self.next])
            self.next += 1
        return self.loaded.pop(i)


class BankRot:
    def __init__(self, k, banks):
        self.k, self.banks, self.i = k, list(banks), 0

    def next(self):
        b = self.banks[self.i % len(self.banks)]
        self.i += 1
        return self.k.bank(b)


def build(T, stop_after=None, debug=False, nlayers=DEPTH):
    SEQ = (T - 1) * 128
    TT = T * 128
    nc = bass.Bass("TRN2", target_bir_lowering=False)

    def din(name, shape):
        return nc.dram_tensor(name, list(shape), F32, kind="ExternalInput").ap()

    def dscr(name, shape, dt):
        if debug:
            return nc.dram_tensor(name, list(shape), dt, kind="ExternalOutput").ap()
        return nc.dram_tensor(name, list(shape), dt).ap()

    k = K()
    k.T, k.TT, k.SEQ = T, TT, SEQ
    x_in = din("x", (SEQ, D))
    meta = din("meta", (N_META, D))
    norm1_g = din("norm1_g", (DEPTH, D))
    w_in = din("w_in", (DEPTH, D, N_IN))
    conv_w = din("conv_w", (DEPTH, CONV_K, 512))
    conv_b = din("conv_b", (DEPTH, 512))
    conv_ln_g = din("conv_ln_g", (DEPTH, 512))
    conv_ln_b = din("conv_ln_b", (DEPTH, 512))
    w_conv_out = din("w_conv_out", (DEPTH, 512, D))
    q_a_norm_g = din("q_a_norm_g", (DEPTH, 256))
    w_uq = din("w_uq", (DEPTH, 256, 768))
    kv_a_norm_g = din("kv_a_norm_g", (DEPTH, 128))
    w_ukv = din("w_ukv", (DEPTH, 128, 1024))
    q_norm_g = din("q_norm_g", (DEPTH, 96))
    k_norm_g = din("k_norm_g", (DEPTH, 96))
    w_attn_out = din("w_attn_out", (DEPTH, 512, D))
    hgrn_lb_logits = din("hgrn_lb_logits", (DEPTH, 512))
    hgrn_norm_g = din("hgrn_norm_g", (DEPTH, 512))
    w_hgrn_out = din("w_hgrn_out", (DEPTH, 512, D))
    w_out = din("w_out", (DEPTH, D, D))
    norm2_g = din("norm2_g", (DEPTH, D))
    w_ff1 = din("w_ff1", (DEPTH, D, 4096))
    w_ff2 = din("w_ff2", (DEPTH, 4096, D))
    consts = din("consts", (128, NCONST))
    cs_tab = din("cs_tab", (128, T * 32))
    out = nc.dram_tensor("out", [SEQ, D], F32, kind="ExternalOutput").ap()

    xres = dscr("xres", (TT, D), F32)
    hT = dscr("hT", (D, TT), BF16)
    mixa = dscr("mixa", (D, TT), BF16)
    mixb = dscr("mixb", (D, TT), BF16)
    mixc = dscr("mixc", (D, TT), BF16)
    QT = dscr("QT", (NH, QK, TT), BF16)
    KT = dscr("KT", (NH, QK, TT), BF16)
    VA = dscr("VA", (TT, NH * 65), BF16)
    OT = dscr("OT", (NH, 64, TT), BF16)

    groups = groups_of(T)

    with ExitStack() as es0:
        P = Prog(nc, es0)
        k.P, k.nc = P, nc
        k.PS = [es0.enter_context(nc.psum_tensor('ps%d' % i, [128, 1024], F32)) for i in range(4)]
        k.PSB = [P.buf('psb%d' % i) for i in range(8)]

        def bank(i):
            return k.PS[i // 2][:, (i % 2) * 512:(i % 2) * 512 + 512], k.PSB[i]
        k.bank = bank
        cst = es0.enter_context(nc.sbuf_tensor('cst', [128, NCONST], F32))
        bcst = P.buf('cst')
        P.dma('sp', cst[:], consts, bcst, True)
        identb = es0.enter_context(nc.sbuf_tensor('identb', [128, 128], BF16))
        bidb = P.buf('identb')
        P.cp('dve', identb[:], cst[:, C_IDENT:C_IDENT + 128], [bcst], [bidb])
        k.cst, k.bcst, k.identb, k.bidb = cst, bcst, identb, bidb
        k.identf = cst[:, C_IDENT:C_IDENT + 128]
        k.onesf = cst[:, C_ONES:C_ONES + 128]

        with ExitStack() as es:
            rot = Rot(P, es, 'xi', [128, D], F32, 3)
            tz, bz = rot.next()
            P.memset('pool', tz[:], 0.0, [bz])
            P.dma('sp', tz[PAD:128, :], meta, bz, True)
            P.dma('sp', xres[0:128, :], tz[:], bz, False)
            P.barrier()

        def xsrc(l, t):
            if l == 0 and t >= 1:
                return x_in[(t - 1) * 128:t * 128, :]
            return xres[t * 128:(t + 1) * 128, :]
        k.xsrc = xsrc

        for l in range(nlayers):
            if stop_after == 'init':
                break
            last = (l == nlayers - 1)
            es_wc = ExitStack()
            wconv = conv_load(k, es_wc, w_in[l], conv_w[l], conv_b[l], conv_ln_g[l], conv_ln_b[l], w_conv_out[l])
            phase_norm1(k, l, xres, hT, norm1_g, groups)
            if stop_after == 'p0':
                es_wc.close()
                break
            phase_conv(k, wconv, hT, mixa, groups)
            es_wc.close()
            if stop_after == 'p1':
                break
            phase_mla_pre(k, l, hT, QT, KT, VA, w_in[l], q_a_norm_g[l], w_uq[l], kv_a_norm_g[l], w_ukv[l],
                          q_norm_g, k_norm_g, cs_tab, groups)
            if stop_after == 'p2a':
                break
            es_wh = ExitStack()
            whg = hgrn_load(k, es_wh, l, w_in[l], hgrn_lb_logits, hgrn_norm_g[l], w_hgrn_out[l])
            es_wa = ExitStack()
            wao = attn_out_load(k, es_wa, w_in[l], w_attn_out[l])
            phase_attn(k, l, QT, KT, VA, OT, groups)
            if stop_after == 'p2b':
                es_wa.close(); es_wh.close()
                break
            phase_attn_out(k, wao, hT, OT, mixb, groups)
            es_wa.close()
            if stop_after == 'p2c':
                es_wh.close()
                break
            phase_hgrn(k, whg, hT, mixc, groups)
            es_wh.close()
            if stop_after == 'p3':
                break
            es_w1 = ExitStack()
            w1pre = ffn_load_w1(k, es_w1, w_ff1[l])
            phase_merge(k, l, xres, hT, mixa, mixb, mixc, w_out[l], norm2_g, groups, after_loads=w1pre[2])
            if stop_after == 'p4a':
                es_w1.close()
                break
            phase_ffn(k, l, xres, hT, out, w1pre, w_ff2[l], last)
            es_w1.close()
        P.emit()
    k.P = P
    build.last_k = k
    return nc


def interleave(gens, depth):
    active = []
    it = iter(gens)
    more = True
    while True:
        while more and len(active) < depth:
            try:
                active.append(next(it))
            except StopIteration:
                more = False
        if not active:
            break
        for g in list(active):
            try:
                next(g)
            except StopIteration:
                active.remove(g)


class NormTools:
    def __init__(self, k, es, banks=(0, 1), nrot=4):
        self.k = k
        P = k.P
        self.junk = Rot(P, es, 'njunk', [128, D], BF16, 2)
        self.ss = Rot(P, es, 'nss', [128, 4], F32, nrot)
        self.hb = Rot(P, es, 'nhb', [128, D], BF16, nrot)
        self.br = BankRot(k, banks)

    def norm_tile_gen(self, xt, bx, gbc, bg, hg, bhg, j):
        k = self.k
        P = k.P
        jk, bj = self.junk.next()
        ss, bss = self.ss.next()
        hb, bhb = self.hb.next()
        P.act(jk[:], xt[:], AF.Square, [bx], [bj, bss], accum_out=ss[:, 0:1])
        P.rsqrt(ss[:, 2:3], ss[:, 0:1], 1.0 / D, EPS, [bss], [bss])
        P.stt(hb[:], xt[:], ss[:, 2:3], gbc[:], ALU.mult, ALU.mult, [bx, bss, bg], [bhb])
        yield
        pb, bpb = self.br.next()
        pbb = pb.bitcast(BF16)
        for kc in range(8):
            P.tr(pbb[:, kc * 128:(kc + 1) * 128], hb[:, kc * 128:(kc + 1) * 128], k.identb[:],
                 [bhb, k.bidb], [bpb], inc=(kc == 7))
        P.cp('act', hg[:, :, j * 128:(j + 1) * 128], pbb.rearrange("p (kc t) -> p kc t", kc=8), [bpb], [bhg])


def phase_norm1(k, l, xres, hT, norm1_g, groups):
    P, nc = k.P, k.nc
    with ExitStack() as es:
        gbc = es.enter_context(nc.sbuf_tensor(U('gbc'), [128, D], F32))
        bg = P.buf('gbc')
        P.dma('sp', gbc[:], norm1_g[l:l + 1, :].to_broadcast([128, D]), bg, True)
        xr = Rot(P, es, 'x', [128, D], F32, 6)
        hgr = Rot(P, es, 'hTg', [128, 8, 512], BF16, 3)
        nt = NormTools(k, es)
        hTv = hT.rearrange("(kc p) t -> p kc t", p=128)

        def load_x(t):
            xt, bx = xr.next()
            P.dma('sp', xt[:], k.xsrc(l, t), bx, True)
            return xt, bx
        pfx = Prefetch([(t,) for t in range(k.T)], load_x, ahead=2)

        def tile_gen(t0, n, j, hg, bhg):
            t = t0 + j
            xt, bx = pfx.get(t)
            yield from nt.norm_tile_gen(xt, bx, gbc, bg, hg, bhg, j)
            if j == n - 1:
                P.dma('sp', hTv[:, :, t0 * 128:(t0 + n) * 128], hg[:, :, 0:n * 128], bhg, False)

        def all_tiles():
            for (t0, n) in groups:
                hg, bhg = hgr.next()
                for j in range(n):
                    yield tile_gen(t0, n, j, hg, bhg)
        interleave(all_tiles(), 3)
        P.barrier()


def conv_load(k, es, w_in_l, conv_w, conv_b, ln_g, ln_b, w_co):
    P, nc = k.P, k.nc
    if True:
        def sb(name, shape, dt):
            return es.enter_context(nc.sbuf_tensor(U(name), shape, dt))
        Wc = sb('Wc', [128, 8, 1024], BF16); bWc = P.buf('Wc')
        Wga = sb('Wga', [128, 8, 1024], BF16); bWga = P.buf('Wga')
        Wco = sb('Wco', [128, 4, 1024], BF16); bWco = P.buf('Wco')
        load_w_cast(P, Wc, bWc, w_in_l, 8, 1024, O_CONV)
        load_w_cast(P, Wga, bWga, w_in_l, 8, 1024, O_GA)
        load_w_cast(P, Wco, bWco, w_co, 4, 1024, 0)
        cw31 = sb('cw31', [32, 512], F32); bcw31 = P.buf('cw31')
        P.dma('sp', cw31[0:CONV_K, :], conv_w, bcw31, True)
        cw = sb('cw', [128, 4, 32], F32); bcw = P.buf('cw')
        pb, bpb = k.bank(0)
        for cc in range(4):
            P.tr(pb[:, cc * 32:cc * 32 + CONV_K], cw31[0:CONV_K, cc * 128:(cc + 1) * 128],
                 k.identf[0:CONV_K, 0:CONV_K], [bcw31, k.bcst], [bpb], inc=(cc == 3))
        P.cp('dve', cw[:, :, 0:CONV_K], pb[:, 0:128].rearrange("p (c j) -> p c j", c=4)[:, :, 0:CONV_K], [bpb], [bcw])
        Dg = sb('Dg', [128, 4, CONV_K, 128], BF16); bDg = P.buf('Dg')
        for cc in range(4):
            P.tt('dve' if cc % 2 == 0 else 'pool', Dg[:, cc, :, :],
                 k.identf.unsqueeze(1).to_broadcast([128, CONV_K, 128]),
                 cw[:, cc, 0:CONV_K].unsqueeze(2).to_broadcast([128, CONV_K, 128]), ALU.mult, [k.bcst, bcw], [bDg])
        cols = sb('ccols', [128, 12], F32); bcols = P.buf('ccols')
        for i, src in enumerate((conv_b, ln_g, ln_b)):
            P.dma('sp', cols[:, i * 4:(i + 1) * 4], src.rearrange("(c p) -> p c", p=128), bcols, True,
                  allow_slow_non_contiguous=True)
    return dict(Wc=Wc, bWc=bWc, Wga=Wga, bWga=bWga, Wco=Wco, bWco=bWco, Dg=Dg, bDg=bDg, cols=cols, bcols=bcols)


def phase_conv(k, w, hT, mixa, groups):
    P, nc = k.P, k.nc
    Wc, bWc, Wga, bWga, Wco, bWco = w['Wc'], w['bWc'], w['Wga'], w['bWga'], w['Wco'], w['bWco']
    Dg, bDg, cols, bcols = w['Dg'], w['bDg'], w['cols'], w['bcols']
    with ExitStack() as es:
        hgr = Rot(P, es, 'hTg', [128, 8, 512], BF16, 3)
        hcr = Rot(P, es, 'hc', [128, 4, 30 + 512], BF16, 2)
        sgr = Rot(P, es, 'sg', [128, 512], F32, 2)
        cvr = Rot(P, es, 'cv', [128, 4, 512], F32, 2)
        sqr = Rot(P, es, 'sq', [128, 4, 512], F32, 2)
        str_ = Rot(P, es, 'st', [128, 4, 512], F32, 2)
        xcr = Rot(P, es, 'xc', [128, 512], F32, 2)
        actr = Rot(P, es, 'cact', [128, 4, 512], BF16, 2)
        gar = Rot(P, es, 'ga', [128, 512], F32, 2)
        mgr = Rot(P, es, 'mixg', [128, 8, 512], BF16, 2)
        hTv = hT.rearrange("(kc p) t -> p kc t", p=128)
        mixv = mixa.rearrange("(kc p) t -> p kc t", p=128)
        br = BankRot(k, range(8))
        prev = {}

        def load_hg(t0, n):
            hg, bhg = hgr.next()
            P.dma('sp', hg[:, :, 0:n * 128], hTv[:, :, t0 * 128:(t0 + n) * 128], bhg, True)
            return hg, bhg
        pf = Prefetch(groups, load_hg)

        def group_gen(gi, t0, n):
            N = n * 128
            tok0 = t0 * 128
            hg, bhg = pf.get(gi)
            hc, bhc = hcr.next()
            if prev:
                P.cp('pool', hc[:, :, 0:30], prev['hc'][:, :, prev['N']:prev['N'] + 30], [prev['bhc']], [bhc])
            else:
                P.memset('pool', hc[:, :, 0:30], 0.0, [bhc])
            for cc in range(4):
                pa, bpa = br.next()
                pg, bpg = br.next()
                for kc in range(8):
                    P.mm(pa[:, 0:N], Wc[:, kc, cc * 128:(cc + 1) * 128], hg[:, kc, 0:N], kc == 0, kc == 7, [bWc, bhg], [bpa])
                for kc in range(8):
                    P.mm(pg[:, 0:N], Wc[:, kc, 512 + cc * 128:512 + (cc + 1) * 128], hg[:, kc, 0:N], kc == 0, kc == 7, [bWc, bhg], [bpg])
                sg, bsg = sgr.next()
                P.act(sg[:, 0:N], pg[:, 0:N], AF.Sigmoid, [bpg], [bsg])
                P.tt('dve', hc[:, cc, 30:30 + N], pa[:, 0:N], sg[:, 0:N], ALU.mult, [bpa, bsg], [bhc])
            prev.update(hc=hc, bhc=bhc, N=N)
            yield
            cv, bcv = cvr.next()
            sq, bsq = sqr.next()
            for cc in range(4):
                pc, bpc = br.next()
                for j in range(CONV_K):
                    P.mm(pc[:, 0:N], Dg[:, cc, j, :], hc[:, cc, j:j + N], j == 0, j == CONV_K - 1, [bDg, bhc], [bpc])
                P.act(cv[:, cc, 0:N], pc[:, 0:N], AF.Identity, [bpc, bcols], [bcv], bias=cols[:, cc:cc + 1], scale=1.0)
                P.act(sq[:, cc, 0:N], pc[:, 0:N], AF.Square, [bpc, bcols], [bsq], bias=cols[:, cc:cc + 1], scale=1.0)
            yield
            st, bst = str_.next()
            pm, bpm = br.next()
            pq, bpq = br.next()
            for cc in range(4):
                P.mm(pm[:, 0:N], k.onesf, cv[:, cc, 0:N], cc == 0, cc == 3, [k.bcst, bcv], [bpm])
            for cc in range(4):
                P.mm(pq[:, 0:N], k.onesf, sq[:, cc, 0:N], cc == 0, cc == 3, [k.bcst, bsq], [bpq])
            P.act(st[:, 0, 0:N], pm[:, 0:N], AF.Copy, [bpm], [bst], scale=1.0 / 512)
            P.tt('pool', st[:, 1, 0:N], st[:, 0, 0:N], st[:, 0, 0:N], ALU.mult, [bst], [bst])
            P.stt(st[:, 2, 0:N], pq[:, 0:N], 1.0 / 512, st[:, 1, 0:N], ALU.mult, ALU.subtract, [bpq, bst], [bst])
            P.rsqrt(st[:, 3, 0:N], st[:, 2, 0:N], 1.0, EPS, [bst], [bst])
            yield
            act, bact = actr.next()
            for cc in range(4):
                xc, bxc = xcr.next()
                P.tt('pool', xc[:, 0:N], cv[:, cc, 0:N], st[:, 0, 0:N], ALU.subtract, [bcv, bst], [bxc])
                P.tt('dve', xc[:, 0:N], xc[:, 0:N], st[:, 3, 0:N], ALU.mult, [bxc, bst], [bxc])
                P.act(act[:, cc, 0:N], xc[:, 0:N], AF.Silu, [bxc, bcols], [bact],
                      bias=cols[:, 8 + cc:9 + cc], scale=cols[:, 4 + cc:5 + cc])
            yield
            mg, bmg = mgr.next()
            for nn in range(8):
                py, bpy = br.next()
                pgt, bpgt = br.next()
                for cc in range(4):
                    P.mm(py[:, 0:N], Wco[:, cc, nn * 128:(nn + 1) * 128], act[:, cc, 0:N], cc == 0, cc == 3, [bWco, bact], [bpy])
                for kc in range(8):
                    P.mm(pgt[:, 0:N], Wga[:, kc, nn * 128:(nn + 1) * 128], hg[:, kc, 0:N], kc == 0, kc == 7, [bWga, bhg], [bpgt])
                ga, bga = gar.next()
                P.act(ga[:, 0:N], pgt[:, 0:N], AF.Sigmoid, [bpgt], [bga])
                P.tt('dve', mg[:, nn, 0:N], py[:, 0:N], ga[:, 0:N], ALU.mult, [bpy, bga], [bmg])
            P.dma('sp', mixv[:, :, tok0:tok0 + N], mg[:, :, 0:N], bmg, False)

        interleave((group_gen(gi, t0, n) for gi, (t0, n) in enumerate(groups)), 2)
        P.barrier()

def phase_mla_pre(k, l, hT, QT, KT, VA, w_in_l, q_a_g, w_uq, kv_a_g, w_ukv, q_norm_g, k_norm_g, cs_tab, groups):
    P, nc = k.P, k.nc
    T = k.T
    with ExitStack() as es:
        def sb(name, shape, dt):
            return es.enter_context(nc.sbuf_tensor(U(name), shape, dt))
        Wm = sb('Wm', [128, 8, 416], BF16); bWm = P.buf('Wm')
        load_w_cast(P, Wm, bWm, w_in_l, 8, 416, O_CQ)
        wtmp = sb('wtmp', [128, 2, 1024], F32); bwt = P.buf('wtmp')
        gcol = sb('gcol', [128, 4], F32); bgc = P.buf('gcol')
        P.dma('sp', gcol[:, 0:2], q_a_g.rearrange("(c p) -> p c", p=128), bgc, True, allow_slow_non_contiguous=True)
        P.dma('sp', gcol[:, 2:3], kv_a_g.rearrange("(c p) -> p c", p=128), bgc, True, allow_slow_non_contiguous=True)
        Wuq = sb('Wuq', [128, 2, 768], BF16); bWuq = P.buf('Wuq')
        Wukv = sb('Wukv', [128, 1024], BF16); bWukv = P.buf('Wukv')
        P.dma('sp', wtmp[:, :, 0:768], w_uq.rearrange("(c p) n -> p c n", p=128), bwt, True)
        for c in range(2):
            P.ts('dve', Wuq[:, c, :], wtmp[:, c, 0:768], gcol[:, c:c + 1], None, ALU.mult, None, [bwt, bgc], [bWuq])
        P.dma('sp', wtmp[:, 0, :], w_ukv, bwt, True)
        P.ts('dve', Wukv[:], wtmp[:, 0, :], gcol[:, 2:3], None, ALU.mult, None, [bwt, bgc], [bWukv])
        gqk = sb('gqk', [128, 2, 96], F32); bgqk = P.buf('gqk')
        P.dma('sp', gqk[:, 0, :], q_norm_g[l:l + 1, :].to_broadcast([128, 96]), bgqk, True)
        P.dma('sp', gqk[:, 1, :], k_norm_g[l:l + 1, :].to_broadcast([128, 96]), bgqk, True)
        P.ts('dve', gqk[:, 0, :], gqk[:, 0, :], float(QK) ** -0.5, None, ALU.mult, None, [bgqk], [bgqk])
        cs = sb('cs', [128, T, 32], F32); bcs = P.buf('cs')
        P.dma('sp', cs[:], cs_tab.rearrange("p (t e) -> p t e", e=32), bcs, True)

        hgr = Rot(P, es, 'hTg', [128, 8, 512], BF16, 3)
        cTr = Rot(P, es, 'cT', [128, 3, 512], BF16, 2)
        junk = sb('mjunk', [128, 256], BF16); bjunk = P.buf('mjunk')
        smr = Rot(P, es, 'sm', [128, 48], F32, 4)
        krr_ = Rot(P, es, 'kr', [128, 4, 32], F32, 4)
        sqr = Rot(P, es, 'sqq', [128, 768], F32, 4)
        qnr = Rot(P, es, 'qn', [128, 8, 96], F32, 4)
        rtr = Rot(P, es, 'rt', [128, 4, 8, 16], F32, 4)
        qfr = Rot(P, es, 'qf', [128, 8, 96], BF16, 4)
        kfr = Rot(P, es, 'kf', [128, 8, 96], BF16, 4)
        knr = Rot(P, es, 'kn', [128, 8, 64], F32, 4)
        var_ = Rot(P, es, 'va', [128, 8, 65], BF16, 5)
        for t_, b_ in zip(var_.tiles, var_.bufs):
            P.memset('pool', t_[:], 1.0, [b_])
        QTg = Rot(P, es, 'QTg', [96, 8, 512], BF16, 2)
        KTg = Rot(P, es, 'KTg', [96, 8, 512], BF16, 2)
        hTv = hT.rearrange("(kc p) t -> p kc t", p=128)
        QTv = QT.rearrange("h d t -> d h t")
        KTv = KT.rearrange("h d t -> d h t")

        def load_hg(t0, n):
            hg, bhg = hgr.next()
            P.dma('sp', hg[:, :, 0:n * 128], hTv[:, :, t0 * 128:(t0 + n) * 128], bhg, True)
            return hg, bhg
        pf = Prefetch(groups, load_hg)

        def group_prep(t0, n):
            N = n * 128
            tok0 = t0 * 128
            hg, bhg = pf.get(groups.index((t0, n)))
            cT, bcT = cTr.next()
            pb, bpb = k.bank(0)
            for ch in range(3):
                for kc in range(8):
                    P.mm(pb[:, 0:N], Wm[:, kc, ch * 128:(ch + 1) * 128], hg[:, kc, 0:N], kc == 0, kc == 7, [bWm, bhg], [bpb])
                P.cp('act', cT[:, ch, 0:N], pb[:, 0:N], [bpb], [bcT])
            qtg, bqtg = QTg.next()
            ktg, bktg = KTg.next()
            return dict(N=N, tok0=tok0, hg=hg, bhg=bhg, cT=cT, bcT=bcT, qtg=qtg, bqtg=bqtg, ktg=ktg, bktg=bktg)

        def tile_gen(t0, n, j, gs):
            if j == 0:
                gs.update(group_prep(t0, n))
            N, tok0, hg, bhg, cT, bcT = gs['N'], gs['tok0'], gs['hg'], gs['bhg'], gs['cT'], gs['bcT']
            qtg, bqtg, ktg, bktg = gs['qtg'], gs['bqtg'], gs['ktg'], gs['bktg']
            t = t0 + j
            c0, c1 = j * 128, (j + 1) * 128
            sm, bsm = smr.next()
            kr, bkr = krr_.next()
            ptm, bptm = k.bank(1)
            for kc in range(8):
                P.mm(ptm[:, 0:416], hg[:, kc, c0:c1], Wm[:, kc, 0:416], kc == 0, kc == 7, [bhg, bWm], [bptm])
            P.act(junk[:, 0:256], ptm[:, 0:256], AF.Square, [bptm], [bjunk, bsm], accum_out=sm[:, 0:1])
            P.act(junk[:, 0:128], ptm[:, 256:384], AF.Square, [bptm], [bjunk, bsm], accum_out=sm[:, 1:2])
            P.act(junk[:, 0:32], ptm[:, 384:416], AF.Square, [bptm], [bjunk, bsm], accum_out=sm[:, 2:3])
            P.cp('act', kr[:, 0, :], ptm[:, 384:416], [bptm], [bkr])
            P.rsqrt(sm[:, 3:4], sm[:, 0:1], 1.0 / 256, EPS, [bsm], [bsm])
            P.rsqrt(sm[:, 4:5], sm[:, 1:2], 1.0 / 128, EPS, [bsm], [bsm])
            P.stt(sm[:, 5:6], sm[:, 3:4], 1.0 / QK, sm[:, 3:4], ALU.mult, ALU.mult, [bsm], [bsm])
            P.tt('dve', sm[:, 6:7], sm[:, 4:5], sm[:, 4:5], ALU.mult, [bsm], [bsm])
            yield
            pq0, bpq0 = k.bank(2)
            pq1, bpq1 = k.bank(3)
            pq = k.PS[1]
            for c in range(2):
                P.mm(pq0[:, 0:512], cT[:, c, c0:c1], Wuq[:, c, 0:512], c == 0, c == 1, [bcT, bWuq], [bpq0])
            for c in range(2):
                P.mm(pq1[:, 0:256], cT[:, c, c0:c1], Wuq[:, c, 512:768], c == 0, c == 1, [bcT, bWuq], [bpq1])
            sq, bsq = sqr.next()
            P.act(sq[:], pq[:, 0:768], AF.Square, [bpq0, bpq1], [bsq])
            P.op('dve', 'tensor_reduce', [bsq], [bsm], out=sm[:, 8:16], in_=sq[:].rearrange("p (h d) -> p h d", h=8),
                 axis=AX.X, op=ALU.add)
            P.ts('dve', sm[:, 8:16], sm[:, 8:16], sm[:, 5:6], None, ALU.mult, None, [bsm], [bsm])
            P.rsqrt(sm[:, 16:24], sm[:, 8:16], 1.0, EPS, [bsm], [bsm])
            P.ts('dve', sm[:, 16:24], sm[:, 16:24], sm[:, 3:4], None, ALU.mult, None, [bsm], [bsm])
            qn, bqn = qnr.next()
            P.tt('dve', qn[:], pq[:, 0:768].rearrange("p (h d) -> p h d", h=8),
                 sm[:, 16:24].unsqueeze(2).to_broadcast([128, 8, 96]), ALU.mult, [bpq0, bpq1, bsm], [bqn])
            yield
            P.tt('dve', qn[:], qn[:], gqk[:, 0, :].unsqueeze(1).to_broadcast([128, 8, 96]), ALU.mult, [bqn, bgqk], [bqn])
            qf, bqf = qfr.next()
            P.cp('act', qf[:, :, 0:64], qn[:, :, 0:64], [bqn], [bqf])
            yield
            rt, brt = rtr.next()
            cosb = cs[:, t, 0:16].unsqueeze(1).to_broadcast([128, 8, 16])
            sinb = cs[:, t, 16:32].unsqueeze(1).to_broadcast([128, 8, 16])
            x1, x2 = qn[:, :, 64:80], qn[:, :, 80:96]
            P.tt('pool', rt[:, 0], x1, cosb, ALU.mult, [bqn, bcs], [brt])
            P.tt('pool', rt[:, 1], x2, sinb, ALU.mult, [bqn, bcs], [brt])
            P.tt('pool', rt[:, 2], x1, sinb, ALU.mult, [bqn, bcs], [brt])
            P.tt('pool', rt[:, 3], x2, cosb, ALU.mult, [bqn, bcs], [brt])
            P.tt('pool', qf[:, :, 64:80], rt[:, 0], rt[:, 1], ALU.subtract, [brt], [bqf])
            P.tt('pool', qf[:, :, 80:96], rt[:, 2], rt[:, 3], ALU.add, [brt], [bqf])
            yield
            pk0, bpk0 = k.bank(4)
            pk1, bpk1 = k.bank(5)
            pkv = k.PS[2]
            P.mm(pk0[:, 0:512], cT[:, 2, c0:c1], Wukv[:, 0:512], True, True, [bcT, bWukv], [bpk0])
            P.mm(pk1[:, 0:512], cT[:, 2, c0:c1], Wukv[:, 512:1024], True, True, [bcT, bWukv], [bpk1])
            pkv3 = pkv[:, :].rearrange("p (h e) -> p h e", h=8)
            kn, bkn = knr.next()
            P.act(kn[:], pkv3[:, :, 0:64], AF.Square, [bpk0, bpk1], [bkn])
            P.op('dve', 'tensor_reduce', [bkn], [bsm], out=sm[:, 24:32], in_=kn[:], axis=AX.X, op=ALU.add)
            P.ts('dve', sm[:, 24:32], sm[:, 24:32], sm[:, 6:7], sm[:, 2:3], ALU.mult, ALU.add, [bsm], [bsm])
            P.rsqrt(sm[:, 32:40], sm[:, 24:32], 1.0 / QK, EPS, [bsm], [bsm])
            P.ts('dve', sm[:, 40:48], sm[:, 32:40], sm[:, 4:5], None, ALU.mult, None, [bsm], [bsm])
            P.tt('dve', kn[:], pkv3[:, :, 0:64], sm[:, 40:48].unsqueeze(2).to_broadcast([128, 8, 64]), ALU.mult,
                 [bpk0, bpk1, bsm], [bkn])
            va, bva = var_.next()
            P.ts('dve', va[:, :, 0:64], pkv3[:, :, 64:128], sm[:, 4:5], None, ALU.mult, None, [bpk0, bpk1, bsm], [bva])
            P.dma('sp', VA[t * 128:(t + 1) * 128, :], va[:].rearrange("p h e -> p (h e)"), bva, False)
            yield
            kf, bkf = kfr.next()
            P.tt('dve', kf[:, :, 0:64], kn[:], gqk[:, 1, 0:64].unsqueeze(1).to_broadcast([128, 8, 64]), ALU.mult, [bkn, bgqk], [bkf])
            yield
            P.tt('pool', kr[:, 1, :], kr[:, 0, :], gqk[:, 1, 64:96], ALU.mult, [bkr, bgqk], [bkr])
            P.tt('pool', kr[:, 2, 0:16], kr[:, 1, 0:16], cs[:, t, 0:16], ALU.mult, [bkr, bcs], [bkr])
            P.tt('pool', kr[:, 2, 16:32], kr[:, 1, 16:32], cs[:, t, 16:32], ALU.mult, [bkr, bcs], [bkr])
            P.tt('pool', kr[:, 3, 0:16], kr[:, 2, 0:16], kr[:, 2, 16:32], ALU.subtract, [bkr], [bkr])
            P.tt('pool', kr[:, 2, 0:16], kr[:, 1, 0:16], cs[:, t, 16:32], ALU.mult, [bkr, bcs], [bkr])
            P.tt('pool', kr[:, 2, 16:32], kr[:, 1, 16:32], cs[:, t, 0:16], ALU.mult, [bkr, bcs], [bkr])
            P.tt('pool', kr[:, 3, 16:32], kr[:, 2, 0:16], kr[:, 2, 16:32], ALU.add, [bkr], [bkr])
            P.tt('pool', kf[:, :, 64:96], kr[:, 3, :].unsqueeze(1).to_broadcast([128, 8, 32]),
                 sm[:, 32:40].unsqueeze(2).to_broadcast([128, 8, 32]), ALU.mult, [bkr, bsm], [bkf])
            yield
            ptq, bptq = k.bank(6)
            ptk, bptk = k.bank(7)
            ptqb = ptq.bitcast(BF16)
            ptkb = ptk.bitcast(BF16)
            for h in range(NH):
                P.tr(ptqb[0:QK, h * 128:(h + 1) * 128], qf[:, h, :], k.identb[:], [bqf, k.bidb], [bptq], inc=(h == NH - 1))
            P.cp('act', qtg[:, :, c0:c1], ptqb[0:QK, :].rearrange("p (h t) -> p h t", h=8), [bptq], [bqtg])
            for h in range(NH):
                P.tr(ptkb[0:QK, h * 128:(h + 1) * 128], kf[:, h, :], k.identb[:], [bkf, k.bidb], [bptk], inc=(h == NH - 1))
            P.cp('dve', ktg[:, :, c0:c1], ptkb[0:QK, :].rearrange("p (h t) -> p h t", h=8), [bptk], [bktg])
            if j == n - 1:
                P.dma('sp', QTv[:, :, tok0:tok0 + N], qtg[:, :, 0:N], bqtg, False)
                P.dma('sp', KTv[:, :, tok0:tok0 + N], ktg[:, :, 0:N], bktg, False)

        def all_tiles():
            for (t0, n) in groups:
                gs = {}
                for j in range(n):
                    yield tile_gen(t0, n, j, gs)
        interleave(all_tiles(), 3)
        P.barrier()


def phase_attn(k, l, QT, KT, VA, OT, groups):
    P, nc = k.P, k.nc
    T, TT = k.T, k.TT
    LA = 2
    with ExitStack() as es:
        def sb(name, shape, dt):
            return es.enter_context(nc.sbuf_tensor(U(name), shape, dt))
        Vall = sb('Vall', [128, T, NH * 65], BF16)
        VAv = VA.rearrange("(t p) e -> p t e", p=128)
        vcuts = [0, min(T, 2), min(T, 10), min(T, 20), T]
        bVs = []
        for ci in range(4):
            a, b_ = vcuts[ci], vcuts[ci + 1]
            bVs.append(P.buf('Vall%d' % ci))
            if b_ > a:
                P.dma('sp', Vall[:, a:b_, :], VAv[:, a:b_, :], bVs[ci], True)

        def bV_of(kt):
            for ci in range(4):
                if vcuts[ci] <= kt < vcuts[ci + 1]:
                    return bVs[ci]
        qhr = Rot(P, es, 'QTh', [QK, TT], BF16, 2)
        khr = Rot(P, es, 'KTh', [QK, TT], BF16, 2)
        pTr = Rot(P, es, 'pT', [128, 2, 512], BF16, LA + 2)
        osr = Rot(P, es, 'osb', [65, 512], F32, 3)
        rcr = Rot(P, es, 'rc', [65, 512], F32, 3)
        otr = Rot(P, es, 'ot', [64, 512], BF16, 3)
        obr = BankRot(k, (0, 1))
        pair_i = [0]
        tri = k.cst[:, C_TRI:C_TRI + 128]
        padb = k.cst[:, C_PADB:C_PADB + 1]
        heads = {}

        def load_head(h):
            qh, bqh = qhr.next()
            kh, bkh = khr.next()
            P.dma('sp', qh[:], QT[h], bqh, True)
            P.dma('sp', kh[:], KT[h], bkh, True)
            heads[h] = (qh, bqh, kh, bkh)

        items = []
        for h in range(NH):
            for (t0, n) in groups:
                kts = list(range(t0 + n))
                i = 0
                while i < len(kts):
                    kt = kts[i]
                    if 1 <= kt and kt + 1 < t0:
                        items.append(dict(h=h, t0=t0, n=n, kts=[kt, kt + 1]))
                        i += 2
                    else:
                        items.append(dict(h=h, t0=t0, n=n, kts=[kt]))
                        i += 1
        state = {}

        def emit_score(it):
            h, t0, n, kts = it['h'], it['t0'], it['n'], it['kts']
            if h not in heads:
                load_head(h)
            if kts[0] == 0 and t0 == 0 and h + 1 < NH and (h + 1) not in heads:
                load_head(h + 1)
            qh, bqh, kh, bkh = heads[h]
            N = n * 128
            tok0 = t0 * 128
            pi_ = 1 + (pair_i[0] % 3)
            pair_i[0] += 1
            pp = k.PS[pi_]
            (p0, bp0), (p1, bp1) = k.bank(2 * pi_), k.bank(2 * pi_ + 1)
            pT, bpT = pTr.next()
            if len(kts) == 2:
                for j, (pb_, bpb_) in enumerate(((p0, bp0), (p1, bp1))):
                    kt = kts[j]
                    P.mm(pb_[:, 0:N], kh[:, kt * 128:(kt + 1) * 128], qh[:, tok0:tok0 + N], True, True, [bkh, bqh], [bpb_])
                if N == 512:
                    P.act(pT[:].rearrange("p a b -> p (a b)"), pp[:, 0:1024], AF.Exp, [bp0, bp1], [bpT])
                else:
                    P.act(pT[:, :, 0:N], pp[:, :].rearrange("p (a b) -> p a b", a=2)[:, :, 0:N], AF.Exp, [bp0, bp1], [bpT])
                it['c0'] = 0
            else:
                kt = kts[0]
                c0 = max(kt - t0, 0) * 128
                P.mm(p0[:, 0:N - c0], kh[:, kt * 128:(kt + 1) * 128], qh[:, tok0 + c0:tok0 + N], True, True, [bkh, bqh], [bp0])
                if kt == 0:
                    P.act(pT[:, 0, 0:N - c0], p0[:, 0:N - c0], AF.Exp, [bp0, k.bcst], [bpT], bias=padb, scale=1.0)
                else:
                    P.act(pT[:, 0, 0:N - c0], p0[:, 0:N - c0], AF.Exp, [bp0], [bpT])
                if kt >= t0:
                    P.tt('dve', pT[:, 0, 0:128], pT[:, 0, 0:128], tri, ALU.mult, [bpT, k.bcst], [bpT])
                it['c0'] = c0
            it['pT'], it['bpT'], it['N'] = pT, bpT, N

        pending = []

        def emit_pv(it):
            h, t0, n, kts = it['h'], it['t0'], it['n'], it['kts']
            N, c0 = it['N'], it['c0']
            if kts[0] == 0:
                state['po'] = obr.next()
            po, bpo = state['po']
            for j, kt in enumerate(kts):
                P.mm(po[0:65, c0:N], Vall[:, kt, h * 65:(h + 1) * 65], it['pT'][:, j, 0:N - c0], kt == 0, kt == t0 + n - 1,
                     [bV_of(kt), it['bpT']], [bpo])
            if kts[-1] == t0 + n - 1:
                osb, bos = osr.next()
                P.cp('dve', osb[:, 0:N], po[0:65, 0:N], [bpo], [bos])
                rc, brc = rcr.next()
                P.ts('dve', rc[64:65, 0:N], osb[64:65, 0:N], 1e-30, None, ALU.add, None, [bos], [brc])
                P.op('dve', 'reciprocal', [brc], [brc], out=rc[64:65, 0:N], in_=rc[64:65, 0:N])
                pending.append(dict(h=h, t0=t0, N=N, osb=osb, bos=bos, rc=rc, brc=brc, age=0))

        def emit_final(f):
            N, tok0 = f['N'], f['t0'] * 128
            pi_ = 1 + (pair_i[0] % 3)
            pair_i[0] += 1
            pbc, bpbc = k.bank(2 * pi_)
            P.mm(pbc[0:64, 0:N], k.onesf[64:65, 0:64], f['rc'][64:65, 0:N], True, True, [k.bcst, f['brc']], [bpbc])
            ot, bot = otr.next()
            P.tt('dve', ot[:, 0:N], f['osb'][0:64, 0:N], pbc[0:64, 0:N], ALU.mult, [f['bos'], bpbc], [bot])
            P.dma('sp', OT[f['h'], :, tok0:tok0 + N], ot[:, 0:N], bot, False)

        for i in range(len(items) + LA):
            if i < len(items):
                emit_score(items[i])
            if i - LA >= 0:
                emit_pv(items[i - LA])
            for f in pending:
                f['age'] += 1
            while pending and pending[0]['age'] > 4:
                emit_final(pending.pop(0))
        while pending:
            emit_final(pending.pop(0))
        P.barrier()


def attn_out_load(k, es, w_in_l, w_ao):
    P, nc = k.P, k.nc

    def sb(name, shape, dt):
        return es.enter_context(nc.sbuf_tensor(U(name), shape, dt))
    Wao = sb('Wao', [128, 4, 1024], BF16); bWao = P.buf('Wao')
    load_w_cast(P, Wao, bWao, w_ao, 4, 1024, 0)
    Wgb = sb('Wgb', [128, 8, 1024], BF16); bWgb = P.buf('Wgb')
    load_w_cast(P, Wgb, bWgb, w_in_l, 8, 1024, O_GB)
    return dict(Wao=Wao, bWao=bWao, Wgb=Wgb, bWgb=bWgb)


def phase_attn_out(k, w, hT, OT, mixb, groups):
    P, nc = k.P, k.nc
    Wao, bWao, Wgb, bWgb = w['Wao'], w['bWao'], w['Wgb'], w['bWgb']
    with ExitStack() as es:
        hgr = Rot(P, es, 'hTg', [128, 8, 512], BF16, 2)
        ogr = Rot(P, es, 'OTg', [128, 4, 512], BF16, 2)
        gar = Rot(P, es, 'gb', [128, 512], F32, 2)
        mgr = Rot(P, es, 'mixg', [128, 8, 512], BF16, 2)
        hTv = hT.rearrange("(kc p) t -> p kc t", p=128)
        mixv = mixb.rearrange("(kc p) t -> p kc t", p=128)
        OTv = OT.rearrange("(hp h2) d t -> (h2 d) hp t", h2=2)
        br = BankRot(k, range(8))
        def loads(t0, n):
            N = n * 128
            tok0 = t0 * 128
            hg, bhg = hgr.next()
            P.dma('sp', hg[:, :, 0:N], hTv[:, :, tok0:tok0 + N], bhg, True)
            og, bog = ogr.next()
            P.dma('sp', og[:, :, 0:N], OTv[:, :, tok0:tok0 + N], bog, True)
            return hg, bhg, og, bog

        nxt = loads(*groups[0])
        for gi, (t0, n) in enumerate(groups):
            N = n * 128
            tok0 = t0 * 128
            hg, bhg, og, bog = nxt
            if gi + 1 < len(groups):
                nxt = loads(*groups[gi + 1])
            mg, bmg = mgr.next()
            for nn in range(8):
                py, bpy = br.next()
                pgt, bpgt = br.next()
                for hp in range(NH // 2):
                    P.mm(py[:, 0:N], Wao[:, hp, nn * 128:(nn + 1) * 128], og[:, hp, 0:N], hp == 0, hp == NH // 2 - 1, [bWao, bog], [bpy])
                for kc in range(8):
                    P.mm(pgt[:, 0:N], Wgb[:, kc, nn * 128:(nn + 1) * 128], hg[:, kc, 0:N], kc == 0, kc == 7, [bWgb, bhg], [bpgt])
                ga, bga = gar.next()
                P.act(ga[:, 0:N], pgt[:, 0:N], AF.Sigmoid, [bpgt], [bga])
                P.tt('dve', mg[:, nn, 0:N], py[:, 0:N], ga[:, 0:N], ALU.mult, [bpy, bga], [bmg])
            P.dma('sp', mixv[:, :, tok0:tok0 + N], mg[:, :, 0:N], bmg, False)
        P.barrier()


def hgrn_load(k, es, l, w_in_l, lb_logits, norm_g, w_ho):
    P, nc = k.P, k.nc
    assert DEPTH == 2
    if True:
        def sb(name, shape, dt):
            return es.enter_context(nc.sbuf_tensor(U(name), shape, dt))
        Whq = sb('Whq', [128, 8, 512], BF16); bWhq = P.buf('Whq')
        Whf = sb('Whf', [128, 8, 512], BF16); bWhf = P.buf('Whf')
        Whi = sb('Whi', [128, 8, 512], BF16); bWhi = P.buf('Whi')
        Whg = sb('Whg', [128, 8, 512], BF16); bWhg = P.buf('Whg')
        Wgc = sb('Wgc', [128, 8, 1024], BF16); bWgc = P.buf('Wgc')
        Who = sb('Who', [128, 4, 1024], BF16); bWho = P.buf('Who')
        load_w_cast(P, Whq, bWhq, w_in_l, 8, 512, O_HQ)
        load_w_cast(P, Whf, bWhf, w_in_l, 8, 512, O_HF)
        load_w_cast(P, Whi, bWhi, w_in_l, 8, 512, O_HI)
        load_w_cast(P, Whg, bWhg, w_in_l, 8, 512, O_HG)
        load_w_cast(P, Wgc, bWgc, w_in_l, 8, 1024, O_GC)
        load_w_cast(P, Who, bWho, w_ho, 4, 1024, 0)
        omlb = sb('omlb', [64, 512], F32); bomlb = P.buf('omlb')
        ocol = sb('ocol', [128, 8], F32); bocol = P.buf('ocol')
        P.dma('sp', ocol[:, 4:8], norm_g.rearrange("(h p) -> p h", p=128), bocol, True, allow_slow_non_contiguous=True)
        P.ts('dve', ocol[:, 4:8], ocol[:, 4:8], 0.125, None, ALU.mult, None, [bocol], [bocol])
        if l == 0:
            P.memset('pool', omlb[:], 0.5, [bomlb])
            P.memset('pool', ocol[:, 0:4], 0.5, [bocol])
        else:
            lt = sb('lbt', [64, 2, 512], F32); blt = P.buf('lbt')
            lc = sb('lbc', [128, 2, 4], F32); blc = P.buf('lbc')
            for r in range(2):
                P.dma('sp', lt[:, r, :], lb_logits[r:r + 1, :].to_broadcast([64, 512]), blt, True)
                P.dma('sp', lc[:, r, :], lb_logits[r].rearrange("(h p) -> p h", p=128), blc, True, allow_slow_non_contiguous=True)
            P.tt('dve', lt[:, 0, :], lt[:, 0, :], lt[:, 1, :], ALU.subtract, [blt], [blt])
            P.act(omlb[:], lt[:, 0, :], AF.Sigmoid, [blt], [bomlb])
            P.ts('dve', omlb[:], omlb[:], 0.5, None, ALU.mult, None, [bomlb], [bomlb])
            P.tt('dve', lc[:, 0, :], lc[:, 0, :], lc[:, 1, :], ALU.subtract, [blc], [blc])
            P.act(ocol[:, 0:4], lc[:, 0, :], AF.Sigmoid, [blc], [bocol])
            P.ts('dve', ocol[:, 0:4], ocol[:, 0:4], 0.5, None, ALU.mult, None, [bocol], [bocol])
    return dict(Whq=Whq, bWhq=bWhq, Whf=Whf, bWhf=bWhf, Whi=Whi, bWhi=bWhi, Whg=Whg, bWhg=bWhg, Wgc=Wgc, bWgc=bWgc,
                Who=Who, bWho=bWho, omlb=omlb, bomlb=bomlb, ocol=ocol, bocol=bocol)


def phase_hgrn(k, w, hT, mixc, groups):
    P, nc = k.P, k.nc
    Whq, bWhq, Whf, bWhf, Whi, bWhi, Whg, bWhg = (w[n] for n in ('Whq', 'bWhq', 'Whf', 'bWhf', 'Whi', 'bWhi', 'Whg', 'bWhg'))
    Wgc, bWgc, Who, bWho, omlb, bomlb, ocol, bocol = (w[n] for n in ('Wgc', 'bWgc', 'Who', 'bWho', 'omlb', 'bomlb', 'ocol', 'bocol'))
    with ExitStack() as es:
        def sb(name, shape, dt):
            return es.enter_context(nc.sbuf_tensor(U(name), shape, dt))
        Lm = k.cst[0:64, C_L:C_L + 64]
        UMa = k.cst[0:64, C_UM:C_UM + 64]
        UMb = k.cst[0:64, C_UM + 64:C_UM + 66]
        tri8 = k.cst[0:64, C_TRI8:C_TRI8 + 256]

        S = sb('S', [128, HH, 128], F32); bS = P.buf('S')
        St = sb('St', [128, HH, 128], F32); bSt = P.buf('St')
        P.memset('pool', S[:], 0.0, [bS])
        GN = 256
        NC_ = GN // 64
        hgr = Rot(P, es, 'hTg', [128, 8, GN], BF16, 3)
        qTr = Rot(P, es, 'hqT', [128, HH, GN], BF16, 2)
        kTsr = Rot(P, es, 'hkT', [128, HH, GN], BF16, 2)
        sgTr = Rot(P, es, 'hsgT', [128, HH, GN], BF16, 2)
        lfr = Rot(P, es, 'lf', [64, NC_, 512], F32, 2)
        kdr = Rot(P, es, 'kd', [64, NC_, 512], BF16, 2)
        vr = Rot(P, es, 'hv', [64, NC_, 512], BF16, 2)
        qer = Rot(P, es, 'qe', [128, HH, GN], BF16, 2)
        ker = Rot(P, es, 'ke', [128, HH, GN], BF16, 2)
        ktmr = Rot(P, es, 'ktm', [64, 512], F32, 2)
        kclr = Rot(P, es, 'kcl', [64, 512], F32, 2)
        thir = Rot(P, es, 'thi', [64, 512], F32, 2)
        erbr = Rot(P, es, 'erb', [64, 512], BF16, 2)
        scr = Rot(P, es, 'hscr', [128, GN], F32, 4)
        thr = epr = emr = osqr = rsr = onr = scr
        exsr = Rot(P, es, 'exs', [128, NC_, 2, HH], F32, 2)
        ATr = Rot(P, es, 'ATa', [64, NC_, 256], BF16, 2)
        Mr = Rot(P, es, 'Ma', [128, NC_, 512], F32, 2)
        Sbr = Rot(P, es, 'Sba', [128, NC_, 512], BF16, 2)
        ogr = Rot(P, es, 'og', [128, HH, GN], BF16, 2)
        gar = Rot(P, es, 'gc', [128, GN], F32, 2)
        mgr = Rot(P, es, 'mixg', [128, 8, GN], BF16, 2)
        hTv = hT.rearrange("(kc p) t -> p kc t", p=128)
        mixv = mixc.rearrange("(kc p) t -> p kc t", p=128)

        hgroups = groups_of(k.T, 2)

        def load_hg(t0, n):
            hg, bhg = hgr.next()
            P.dma('sp', hg[:, :, 0:n * 128], hTv[:, :, t0 * 128:(t0 + n) * 128], bhg, True)
            return hg, bhg
        pf = Prefetch(hgroups, load_hg)

        def group_gen(gi, t0, n):
            N = n * 128
            nch = N // 64
            tok0 = t0 * 128
            hg, bhg = pf.get(gi)
            qT, bqT = qTr.next(); kTs, bkTs = kTsr.next(); sgT, bsgT = sgTr.next()
            lf_all, blf = lfr.next(); kd_all, bkd = kdr.next(); v_all, bv = vr.next()
            qe, bqe = qer.next(); ke, bke = ker.next(); exs, bexs = exsr.next()
            AT_all, bAT = ATr.next(); M_all, bM = Mr.next(); Sb_all, bSb = Sbr.next(); og, bog = ogr.next()
            br1 = BankRot(k, (0, 1, 2, 3))
            for h in range(HH):
                hsl = slice(h * 128, (h + 1) * 128)
                pb, bpb = br1.next()
                for kc in range(8):
                    P.mm(pb[:, 0:N], Whq[:, kc, hsl], hg[:, kc, 0:N], kc == 0, kc == 7, [bWhq, bhg], [bpb])
                P.cp('act', qT[:, h, 0:N], pb[:, 0:N], [bpb], [bqT])
                pb, bpb = br1.next()
                for kc in range(8):
                    P.mm(pb[:, 0:N], Whf[:, kc, hsl], hg[:, kc, 0:N], kc == 0, kc == 7, [bWhf, bhg], [bpb])
                th, bth = thr.next()
                P.act(th[:, 0:N], pb[:, 0:N], AF.Tanh, [bpb], [bth], scale=0.5)
                P.ts('dve', kTs[:, h, 0:N], th[:, 0:N], -1.0, 1.0, ALU.mult, ALU.add, [bth], [bkTs])
                pb, bpb = br1.next()
                for kc in range(8):
                    P.mm(pb[:, 0:N], Whg[:, kc, hsl], hg[:, kc, 0:N], kc == 0, kc == 7, [bWhg, bhg], [bpb])
                th, bth = thr.next()
                P.act(th[:, 0:N], pb[:, 0:N], AF.Tanh, [bpb], [bth], scale=0.5)
                P.stt(sgT[:, h, 0:N], th[:, 0:N], 1.0, pb[:, 0:N], ALU.add, ALU.mult, [bth, bpb], [bsgT])
            yield
            brf = BankRot(k, (4, 5))
            bri = BankRot(k, (6, 7))
            brr = BankRot(k, (2, 3))
            st2 = {}

            def s2_mm(c):
                pf, bpf = brf.next()
                pi, bpi = bri.next()
                for kc in range(8):
                    P.mm(pf[0:64, 0:512], hg[:, kc, c * 64:(c + 1) * 64], Whf[:, kc, :], kc == 0, kc == 7, [bhg, bWhf], [bpf])
                for kc in range(8):
                    P.mm(pi[0:64, 0:512], hg[:, kc, c * 64:(c + 1) * 64], Whi[:, kc, :], kc == 0, kc == 7, [bhg, bWhi], [bpi])
                ktm, bktm = ktmr.next()
                thi, bthi = thir.next()
                P.act(ktm[:], pf[0:64, 0:512], AF.Tanh, [bpf], [bktm], scale=0.5)
                P.act(thi[:], pi[0:64, 0:512], AF.Tanh, [bpi], [bthi], scale=0.5)
                P.ts('dve', ktm[:], ktm[:], -1.0, 1.0, ALU.mult, ALU.add, [bktm], [bktm])
                P.tt('dve', ktm[:], ktm[:], omlb[:], ALU.mult, [bktm, bomlb], [bktm])
                P.stt(v_all[:, c, :], thi[:], 1.0, pi[0:64, 0:512], ALU.add, ALU.mult, [bthi, bpi], [bv])
                kcl, bkcl = kclr.next()
                P.ts('dve', kcl[:], ktm[:], CLAMP, None, ALU.min, None, [bktm], [bkcl])
                st2[c] = (ktm, bktm, kcl, bkcl)

            def s2_ln(c):
                ktm, bktm, kcl, bkcl = st2[c]
                P.act(lf_all[:, c, :], kcl[:], AF.Ln, [bkcl], [blf], scale=-1.0, bias=1.0)

            def s2_back(c):
                ktm, bktm, kcl, bkcl = st2.pop(c)
                prb, bprb = brr.next()
                P.mm(prb[0:64, 0:512], Lm, lf_all[:, c, :], True, True, [k.bcst, blf], [bprb])
                erb, berb = erbr.next()
                P.act(erb[:], prb[0:64, 0:512], AF.Exp, [bprb], [berb])
                P.tt('dve', kd_all[:, c, :], ktm[:], erb[:], ALU.mult, [bktm, berb], [bkd])

            for c2 in range(0, nch, 2):
                s2_mm(c2)
                s2_mm(c2 + 1)
                s2_ln(c2)
                s2_ln(c2 + 1)
                s2_back(c2)
                s2_back(c2 + 1)
            yield
            br3 = BankRot(k, (0, 1))
            exv = exs[:].rearrange("p c j h -> p h c j")
            for h in range(HH):
                pbm, bpbm = br3.next()
                pex, bpex = brr.next()
                for c in range(nch):
                    P.mm(pbm[:, c * 64:(c + 1) * 64], lf_all[:, c, h * 128:(h + 1) * 128], UMa, True, True, [blf, k.bcst], [bpbm],
                         inc=(c == nch - 1))
                for c in range(nch):
                    P.mm(pex[:, c * 2:c * 2 + 2], lf_all[:, c, h * 128:(h + 1) * 128], UMb, True, True,
                         [blf, k.bcst], [bpex], inc=(c == nch - 1))
                ep, bep = epr.next()
                em, bem = emr.next()
                P.act(ep[:, 0:N], pbm[:, 0:N], AF.Exp, [bpbm], [bep])
                P.act(em[:, 0:N], pbm[:, 0:N], AF.Exp, [bpbm], [bem], scale=-1.0)
                P.tt('dve', qe[:, h, 0:N], qT[:, h, 0:N], ep[:, 0:N], ALU.mult, [bqT, bep], [bqe])
                P.stt(ke[:, h, 0:N], kTs[:, h, 0:N], ocol[:, h:h + 1], em[:, 0:N], ALU.mult, ALU.mult, [bkTs, bocol, bem], [bke])
                P.act(exv[:, h, 0:nch, :], pex[:, 0:nch * 2].rearrange("p (c j) -> p c j", j=2), AF.Exp, [bpex], [bexs])
            yield
            brA = BankRot(k, (4, 5))
            brM = BankRot(k, (6, 7))
            for c in range(nch):
                cs_ = slice(c * 64, (c + 1) * 64)
                pA, bpA = brA.next()
                for h in range(HH):
                    P.mm(pA[0:64, h * 64:(h + 1) * 64], ke[:, h, cs_], qe[:, h, cs_], True, True, [bke, bqe], [bpA], inc=(h == HH - 1))
                P.tt('dve', AT_all[:, c, :], pA[0:64, 0:256], tri8, ALU.mult, [bpA, k.bcst], [bAT])
                pM, bpM_ = brM.next()
                for h in range(HH):
                    hs = slice(h * 128, (h + 1) * 128)
                    P.mm(pM[:, hs], kd_all[:, c, hs], v_all[:, c, hs], True, True, [bkd, bv], [bpM_], inc=(h == HH - 1))
                P.cp('act', M_all[:, c, :], pM[:, 0:512], [bpM_], [bM])
            yield
            S3 = S[:]
            for c in range(nch):
                e_mid = exs[:, c, 0, :].unsqueeze(2).to_broadcast([128, HH, 128])
                e_last = exs[:, c, 1, :].unsqueeze(2).to_broadcast([128, HH, 128])
                P.tt('pool', Sb_all[:, c, :].rearrange("p (h d) -> p h d", h=HH), S3, e_mid, ALU.mult, [bS, bexs], [bSb])
                P.tt('dve', St[:], S3, e_last, ALU.mult, [bS, bexs], [bSt])
                P.tt('dve', S3, St[:], M_all[:, c, :].rearrange("p (h d) -> p h d", h=HH), ALU.add, [bSt, bM], [bS])
            yield
            bpo = [k.bank(h) for h in range(HH)]
            for c in range(nch):
                cs_ = slice(c * 64, (c + 1) * 64)
                for h in range(HH):
                    po, bpo_h = bpo[h]
                    hs = slice(h * 128, (h + 1) * 128)
                    P.mm(po[:, cs_], v_all[:, c, hs], AT_all[:, c, h * 64:(h + 1) * 64], True, False, [bv, bAT], [bpo_h], inc=False)
                    P.mm(po[:, cs_], Sb_all[:, c, hs], qe[:, h, cs_], False, True, [bSb, bqe], [bpo_h], inc=True)
            for h in range(HH):
                po, bpo_h = bpo[h]
                osq, bosq = osqr.next()
                P.act(osq[:, 0:N], po[:, 0:N], AF.Square, [bpo_h], [bosq])
                pms, bpms = brA.next()
                P.mm(pms[:, 0:N], k.onesf, osq[:, 0:N], True, True, [k.bcst, bosq], [bpms])
                rs, brs = rsr.next()
                P.act(rs[:, 0:N], pms[:, 0:N], AF.Ln, [bpms], [brs], scale=0.25 / 128, bias=EPS)
                P.act(rs[:, 0:N], rs[:, 0:N], AF.Exp, [brs], [brs], scale=-0.5)
                on, bon = onr.next()
                P.tt('dve', on[:, 0:N], po[:, 0:N], rs[:, 0:N], ALU.mult, [bpo_h, brs], [bon])
                P.stt(og[:, h, 0:N], on[:, 0:N], ocol[:, 4 + h:5 + h], sgT[:, h, 0:N], ALU.mult, ALU.mult, [bon, bocol, bsgT], [bog])
            yield
            mg, bmg = mgr.next()
            br6 = BankRot(k, (4, 5, 6, 7))
            for nn in range(8):
                py, bpy = br6.next()
                pgt, bpgt = br6.next()
                for h in range(HH):
                    P.mm(py[:, 0:N], Who[:, h, nn * 128:(nn + 1) * 128], og[:, h, 0:N], h == 0, h == HH - 1, [bWho, bog], [bpy])
                for kc in range(8):
                    P.mm(pgt[:, 0:N], Wgc[:, kc, nn * 128:(nn + 1) * 128], hg[:, kc, 0:N], kc == 0, kc == 7, [bWgc, bhg], [bpgt])
                ga, bga = gar.next()
                P.act(ga[:, 0:N], pgt[:, 0:N], AF.Tanh, [bpgt], [bga], scale=0.5)
                P.stt(mg[:, nn, 0:N], ga[:, 0:N], 1.0, py[:, 0:N], ALU.add, ALU.mult, [bga, bpy], [bmg])
            P.dma('sp', mixv[:, :, tok0:tok0 + N], mg[:, :, 0:N], bmg, False)

        interleave((group_gen(gi, t0, n) for gi, (t0, n) in enumerate(hgroups)), 2)
        P.barrier()


def phase_merge(k, l, xres, hT, mixa, mixb, mixc, w_out_l, norm2_g, groups, after_loads=None):
    P, nc = k.P, k.nc
    with ExitStack() as es:
        def sb(name, shape, dt):
            return es.enter_context(nc.sbuf_tensor(U(name), shape, dt))
        Wo = sb('Wo', [128, 8, 1024], BF16); bWo = P.buf('Wo')
        load_w_cast(P, Wo, bWo, w_out_l, 8, 1024, 0)
        gbc = sb('g2bc', [128, D], F32); bg = P.buf('g2bc')
        P.dma('sp', gbc[:], norm2_g[l:l + 1, :].to_broadcast([128, D]), bg, True)
        if after_loads is not None:
            after_loads()
        groups = groups_of(k.T, 2)
        mar = Rot(P, es, 'ma', [128, 8, 256], BF16, 3)
        mbr = Rot(P, es, 'mb', [128, 8, 256], BF16, 3)
        mcr = Rot(P, es, 'mc', [128, 8, 256], BF16, 3)
        tmp = sb('mtmp', [128, 8, 256], F32); btmp = P.buf('mtmp')
        mxr = Rot(P, es, 'mx', [128, 8, 256], BF16, 2)
        xr = Rot(P, es, 'x', [128, D], F32, 6)
        x1r = Rot(P, es, 'x1', [128, D], F32, 4)
        hgr = Rot(P, es, 'h2g', [128, 8, 256], BF16, 2)
        nt = NormTools(k, es, banks=(0, 1))
        hTv = hT.rearrange("(kc p) t -> p kc t", p=128)
        views = [m.rearrange("(kc p) t -> p kc t", p=128) for m in (mixa, mixb, mixc)]
        pair = [0]

        def load_mix(t0, n):
            N = n * 128
            tok0 = t0 * 128
            ma, bma = mar.next()
            mb, bmb = mbr.next()
            mc, bmc = mcr.next()
            for (mt, bm, v) in ((ma, bma, views[0]), (mb, bmb, views[1]), (mc, bmc, views[2])):
                P.dma('sp', mt[:, :, 0:N], v[:, :, tok0:tok0 + N], bm, True)
            return ma, bma, mb, bmb, mc, bmc
        pf = Prefetch(groups, load_mix)

        def group_prep(t0, n):
            N = n * 128
            tok0 = t0 * 128
            ma, bma, mb, bmb, mc, bmc = pf.get(groups.index((t0, n)))
            P.tt('dve', tmp[:, :, 0:N], ma[:, :, 0:N], mb[:, :, 0:N], ALU.add, [bma, bmb], [btmp])
            mx, bmx = mxr.next()
            P.tt('dve', mx[:, :, 0:N], tmp[:, :, 0:N], mc[:, :, 0:N], ALU.add, [btmp, bmc], [bmx])
            return mx, bmx

        def load_x(t):
            xt, bx = xr.next()
            P.dma('sp', xt[:], k.xsrc(l, t), bx, True)
            return xt, bx
        pfx = Prefetch([(t,) for t in range(k.T)], load_x, ahead=2)

        def tile_gen(t0, n, j, gs):
            t = t0 + j
            if j == 0:
                gs['mx'] = group_prep(t0, n)
                gs['hg'] = hgr.next()
            mx, bmx = gs['mx']
            hg, bhg = gs['hg']
            xt, bx = pfx.get(t)
            pi_ = 1 + (pair[0] % 3)
            pair[0] += 1
            pd = k.PS[pi_]
            (p0, bp0), (p1, bp1) = k.bank(2 * pi_), k.bank(2 * pi_ + 1)
            for kc in range(8):
                P.mm(p0[:, 0:512], mx[:, kc, j * 128:(j + 1) * 128], Wo[:, kc, 0:512], kc == 0, kc == 7, [bmx, bWo], [bp0])
            for kc in range(8):
                P.mm(p1[:, 0:512], mx[:, kc, j * 128:(j + 1) * 128], Wo[:, kc, 512:1024], kc == 0, kc == 7, [bmx, bWo], [bp1])
            yield
            x1, bx1 = x1r.next()
            P.tt('dve', x1[:], pd[:, 0:1024], xt[:], ALU.add, [bp0, bp1, bx], [bx1])
            if t == 0:
                P.dma('sp', xres[PAD:128, :], x1[PAD:128, :], bx1, False)
            else:
                P.dma('sp', xres[t * 128:(t + 1) * 128, :], x1[:], bx1, False)
            yield from nt.norm_tile_gen(x1, bx1, gbc, bg, hg, bhg, j)
            if j == n - 1:
                P.dma('sp', hTv[:, :, t0 * 128:(t0 + n) * 128], hg[:, :, 0:n * 128], bhg, False)

        def all_tiles():
            for (t0, n) in groups:
                gs = {}
                for j in range(n):
                    yield tile_gen(t0, n, j, gs)
        interleave(all_tiles(), 3)
        P.barrier()


def ffn_load_w1(k, es, w1):
    P, nc = k.P, k.nc
    W1 = es.enter_context(nc.sbuf_tensor(U('W1'), [128, 8, 4096], BF16))
    bW1q = [P.buf('W1q%d' % q) for q in range(4)]
    w1v = w1.rearrange("(kc p) n -> p kc n", p=128)

    def issue():
        for q in range(4):
            P.dma('pool', W1[:, :, q * 1024:(q + 1) * 1024], w1v[:, :, q * 1024:(q + 1) * 1024], bW1q[q], True)
    return W1, bW1q, issue


def phase_ffn(k, l, xres, hT, out, w1pre, w2, last):
    P, nc = k.P, k.nc
    T = k.T
    W1, bW1q, _ = w1pre
    with ExitStack() as es:
        def sb(name, shape, dt):
            return es.enter_context(nc.sbuf_tensor(U(name), shape, dt))
        W2 = sb('W2', [128, 32, 1024], BF16)
        bW2q = [P.buf('W2q%d' % q) for q in range(4)]
        w2v = w2.rearrange("(kc p) n -> p kc n", p=128)
        for q in range(4):
            P.dma('pool', W2[:, q * 8:(q + 1) * 8, :], w2v[:, q * 8:(q + 1) * 8, :], bW2q[q], True)
        hgr = Rot(P, es, 'h2g', [128, 8, 256], BF16, 3)
        fTr = Rot(P, es, 'fT', [128, 32, 256], BF16, 2)
        rr = Rot(P, es, 'frl', [128, 256], F32, 3)
        xr = Rot(P, es, 'x1', [128, D], F32, 2)
        xor_ = Rot(P, es, 'xo', [128, D], F32, 2)
        hTv = hT.rearrange("(kc p) t -> p kc t", p=128)
        br = BankRot(k, (0, 1, 2, 3))
        pair = [0]
        fgroups = [(t0, n) for (t0, n) in groups_of(T, 2) if not (last and t0 == 0)]

        def load_hg(t0, n):
            hg, bhg = hgr.next()
            P.dma('sp', hg[:, :, 0:n * 128], hTv[:, :, t0 * 128:(t0 + n) * 128], bhg, True)
            return hg, bhg
        pf = Prefetch(fgroups, load_hg)

        def ffn1(t0, n):
            N = n * 128
            tok0 = t0 * 128
            hg, bhg = pf.get(fgroups.index((t0, n)))
            fT, bfT = fTr.next()
            for fc in range(32):
                pb, bpb = br.next()
                for kc in range(8):
                    P.mm(pb[:, 0:N], W1[:, kc, fc * 128:(fc + 1) * 128], hg[:, kc, 0:N], kc == 0, kc == 7, [bW1q[fc // 8], bhg], [bpb])
                r, brl = rr.next()
                P.act(r[:, 0:N], pb[:, 0:N], AF.Relu, [bpb], [brl])
                P.tt('dve', fT[:, fc, 0:N], r[:, 0:N], r[:, 0:N], ALU.mult, [brl], [bfT])
            return fT, bfT

        def ffn2(t0, n, fT, bfT):
            for j in range(n):
                t = t0 + j
                xt, bx = xr.next()
                P.dma('sp', xt[:], xres[t * 128:(t + 1) * 128, :], bx, True)
                pi_ = 2 + (pair[0] % 2)
                pair[0] += 1
                pd = k.PS[pi_]
                (p0, bp0), (p1, bp1) = k.bank(2 * pi_), k.bank(2 * pi_ + 1)
                for fc in range(32):
                    P.mm(p0[:, 0:512], fT[:, fc, j * 128:(j + 1) * 128], W2[:, fc, 0:512], fc == 0, fc == 31, [bfT, bW2q[fc // 8]], [bp0])
                    P.mm(p1[:, 0:512], fT[:, fc, j * 128:(j + 1) * 128], W2[:, fc, 512:1024], fc == 0, fc == 31, [bfT, bW2q[fc // 8]], [bp1])
                xo, bxo = xor_.next()
                P.tt('dve', xo[:], pd[:, 0:1024], xt[:], ALU.add, [bp0, bp1, bx], [bxo])
                if last:
                    P.dma('sp', out[(t - 1) * 128:t * 128, :], xo[:], bxo, False)
                elif t == 0:
                    P.dma('sp', xres[PAD:128, :], xo[PAD:128, :], bxo, False)
                else:
                    P.dma('sp', xres[t * 128:(t + 1) * 128, :], xo[:], bxo, False)

        prev = None
        for (t0, n) in fgroups:
            cur = (t0, n) + ffn1(t0, n)
            if prev is not None:
                ffn2(*prev)
            prev = cur
        ffn2(*prev)
        P.barrier()

C_IDENT = 0
C_ONES = 128
C_TRI = 256
C_L = 384
C_UM = 448
C_PADB = 514
C_TRI8 = 515
NCONST = C_TRI8 + 512


def make_consts():
    c = np.zeros((128, NCONST), np.float32)
    c[:, C_IDENT:C_IDENT + 128] = np.eye(128, dtype=np.float32)
    c[:, C_ONES:C_ONES + 128] = 1.0
    s = np.arange(128)[:, None]
    t = np.arange(128)[None, :]
    c[:, C_TRI:C_TRI + 128] = (s <= t).astype(np.float32)
    s64 = np.arange(64)[:, None]
    t64 = np.arange(64)[None, :]
    c[:64, C_L:C_L + 64] = (s64 > t64).astype(np.float32)
    c[:64, C_UM:C_UM + 64] = (s64 <= t64).astype(np.float32) - (s64 <= 31).astype(np.float32)
    c[:64, C_UM + 64] = (np.arange(64) <= 31).astype(np.float32)
    c[:64, C_UM + 65] = 1.0
    c[:PAD, C_PADB] = -30000.0
    c[:64, C_TRI8:C_TRI8 + 512] = np.tile((s64 <= t64).astype(np.float32), (1, 8))
    return c


def make_cs_tab(T):
    half = 16
    pos = (np.arange(T * 128, dtype=np.float32) - np.float32(PAD)).astype(np.float32)
    inv_freq = (np.float32(10000.0) ** (-np.arange(half, dtype=np.float32) / np.float32(half))).astype(np.float32)
    ang = (pos[:, None] * inv_freq[None, :]).astype(np.float32)
    cs = np.concatenate([np.cos(ang), np.sin(ang)], axis=1).astype(np.float32)
    return np.ascontiguousarray(cs.reshape(T, 128, 32).transpose(1, 0, 2).reshape(128, T * 32))


_W_NAMES = ['meta', 'norm1_g', 'w_in', 'conv_w', 'conv_b', 'conv_ln_g', 'conv_ln_b', 'w_conv_out',
            'q_a_norm_g', 'w_uq', 'kv_a_norm_g', 'w_ukv', 'q_norm_g', 'k_norm_g', 'w_attn_out',
            'hgrn_lb_logits', 'hgrn_norm_g', 'w_hgrn_out', 'w_out', 'norm2_g', 'w_ff1', 'w_ff2']


def kernel(**inputs):
    x = np.ascontiguousarray(inputs['x'], dtype=np.float32)
    B, SEQ, _ = x.shape
    T = 1 + SEQ // 128
    nc = build(T)
    shared = {n: np.ascontiguousarray(inputs[n], dtype=np.float32) for n in _W_NAMES}
    shared['consts'] = make_consts()
    shared['cs_tab'] = make_cs_tab(T)
    in_maps = []
    for b in range(B):
        m = dict(shared)
        m['x'] = x[b]
        in_maps.append(m)
    res = run_bass_kernel_spmd(nc, in_maps, core_ids=list(range(B)))
    return np.stack([np.asarray(r['out']) for r in res.results], axis=0).astype(np.float32)
```
